# Optimizing a Trainium2 kernel written in Bass

```python
import math
import jax
import jax.numpy as jnp
from jax import lax
import numpy as np

D_MODEL = 1024
BATCH = 4
SEQ = 4096
DEPTH = 4
DEC_BATCH = 16
DEC_SEQ = 32
PAST_LEN = 2048

CHUNK = 64
Q_BLOCK = 128
N_MIXERS = 3
N_A_LAYERS = (DEPTH + 2) // 3
N_B_LAYERS = (DEPTH + 1) // 3
N_C_LAYERS = DEPTH // 3
ROPE_THETA = 500000.0
EPS = 1e-6
D_SUB = 64
H_A = D_MODEL // (2 * D_SUB)
ROT_A = D_SUB // 4
H_B = 4
DQK_B = D_MODEL // (2 * H_B)
DV_B = D_MODEL // H_B
FORGET_BIAS = 3.0
H_C = 8
NOPE_C = 128
ROPE_C = 64
V_C = 128
Q_LORA = 384
KV_LORA = 256
D_FF = -(-8 * D_MODEL // (3 * 256)) * 256

kernel_name = 'hybrid_streaming_encoder_step'


def rmsnorm(x, g):
    xf = x.astype(jnp.float32)
    y = xf * lax.rsqrt(jnp.mean(xf * xf, axis=-1, keepdims=True) + EPS)
    return (y * g.astype(jnp.float32)).astype(x.dtype)


def rope(x, pos, rot):
    half = rot // 2
    inv = jnp.power(jnp.float32(ROPE_THETA), -jnp.arange(half, dtype=jnp.float32) * (2.0 / rot))
    ang = pos.astype(jnp.float32)[:, None] * inv[None, :]
    cos = jnp.cos(ang)[:, None, :]
    sin = jnp.sin(ang)[:, None, :]
    xr = x[..., :rot].astype(jnp.float32)
    x1, x2 = xr[..., :half], xr[..., half:]
    out = jnp.concatenate([x1 * cos - x2 * sin, x2 * cos + x1 * sin], axis=-1).astype(x.dtype)
    return jnp.concatenate([out, x[..., rot:]], axis=-1)


def chunk_mask(pos_q, pos_k):
    return (pos_k[None, :] // CHUNK) <= (pos_q[:, None] // CHUNK)


def sweep_queries(fn, qs, pos_q):
    T = pos_q.shape[0]
    if T <= Q_BLOCK:
        return fn(qs, pos_q)
    nb = T // Q_BLOCK
    qs_b = tuple(jnp.moveaxis(q.reshape(q.shape[0], nb, Q_BLOCK, *q.shape[2:]), 1, 0) for q in qs)
    out = lax.map(lambda a: fn(a[0], a[1]), (qs_b, pos_q.reshape(nb, Q_BLOCK)))
    out = jnp.moveaxis(out, 0, 1)
    return out.reshape(out.shape[0], T, *out.shape[3:])


def diff_attention(h, past_k, past_v, w_qkv, lam_vecs, g_sub, w_o, lambda_init):
    B, T, _ = h.shape
    P = 0 if past_k is None else past_k.shape[1]
    pos_q = P + jnp.arange(T, dtype=jnp.int32)
    pos_k = jnp.arange(P + T, dtype=jnp.int32)
    q, k, v = jnp.split(h @ w_qkv, 3, axis=-1)
    q = rope(q.reshape(B, T, 2 * H_A, D_SUB), pos_q, ROT_A).reshape(B, T, H_A, 2, D_SUB)
    k_row = rope(k.reshape(B, T, 2 * H_A, D_SUB), pos_q, ROT_A).reshape(B, T, H_A, 2 * D_SUB)
    v_row = v.reshape(B, T, H_A, 2 * D_SUB)
    k_all = k_row if past_k is None else jnp.concatenate([past_k.astype(k_row.dtype), k_row], axis=1)
    v_all = v_row if past_v is None else jnp.concatenate([past_v.astype(v_row.dtype), v_row], axis=1)
    k1, k2 = k_all[..., :D_SUB], k_all[..., D_SUB:]
    lv = lam_vecs.astype(jnp.float32)
    lam = jnp.exp(jnp.sum(lv[0] * lv[1])) - jnp.exp(jnp.sum(lv[2] * lv[3])) + lambda_init
    scale = D_SUB ** -0.5

    def block(qs, pq):
        q1, q2 = qs
        mask = chunk_mask(pq, pos_k)

        def probs(qq, kk):
            s = jnp.einsum('bqhd,bkhd->bhqk', qq, kk, preferred_element_type=jnp.float32) * scale
            return jax.nn.softmax(jnp.where(mask, s, -jnp.inf), axis=-1)

        p = probs(q1, k1) - lam * probs(q2, k2)
        return jnp.einsum('bhqk,bkhd->bqhd', p.astype(v_all.dtype), v_all)

    o = sweep_queries(block, (q[:, :, :, 0], q[:, :, :, 1]), pos_q)
    o = rmsnorm(o, g_sub) * (1.0 - lambda_init)
    return o.reshape(B, T, D_MODEL) @ w_o, k_row, v_row


def to_blocks(a, nb, L):
    B, T, H = a.shape[:3]
    a = a.reshape(B, nb, L, H, *a.shape[3:])
    return jnp.moveaxis(a, (1, 3), (0, 2))


def mlstm(h, c0, n0, m0, w_in, b_gates, g_out, w_out):
    B, T, _ = h.shape
    nqk, nv = H_B * DQK_B, H_B * DV_B
    f32 = jnp.float32
    proj = h @ w_in
    q = proj[..., :nqk].reshape(B, T, H_B, DQK_B).astype(f32)
    k = proj[..., nqk:2 * nqk].reshape(B, T, H_B, DQK_B).astype(f32) * (DQK_B ** -0.5)
    v = proj[..., 2 * nqk:2 * nqk + nv].reshape(B, T, H_B, DV_B).astype(f32)
    o_pre = proj[..., 2 * nqk + nv:2 * nqk + 2 * nv]
    gates = proj[..., 2 * nqk + 2 * nv:].astype(f32) + b_gates.astype(f32)
    ig = gates[..., :H_B]
    lf = jax.nn.log_sigmoid(gates[..., H_B:])
    L = CHUNK if T % CHUNK == 0 else T
    nb = T // L
    tril = jnp.tril(jnp.ones((L, L), dtype=bool))

    def step(carry, blk):
        C, n, m = carry
        qb, kb, vb, ib, fb = blk
        b = jnp.cumsum(fb, axis=-1)
        g = b[..., -1]
        dm = jnp.where(tril, b[..., :, None] - b[..., None, :] + ib[..., None, :], -jnp.inf)
        inter = b + m[..., None]
        m_t = jnp.maximum(inter, jnp.max(dm, axis=-1))
        w_ts = jnp.exp(dm - m_t[..., None])
        a = jnp.exp(inter - m_t)
        qk = jnp.einsum('bhtd,bhsd->bhts', qb, kb) * w_ts
        num = a[..., None] * jnp.einsum('bhtd,bhdv->bhtv', qb, C) + jnp.einsum('bhts,bhsv->bhtv', qk, vb)
        den = a * jnp.einsum('bhtd,bhd->bht', qb, n) + jnp.sum(qk, axis=-1)
        h_out = num / jnp.maximum(jnp.abs(den), jnp.exp(-m_t))[..., None]
        r = g[..., None] - b + ib
        m_new = jnp.maximum(g + m, jnp.max(r, axis=-1))
        wr = jnp.exp(r - m_new[..., None])
        decay = jnp.exp(g + m - m_new)
        kw = kb * wr[..., None]
        C_new = decay[..., None, None] * C + jnp.einsum('bhsd,bhsv->bhdv', kw, vb)
        n_new = decay[..., None] * n + jnp.sum(kw, axis=2)
        return (C_new, n_new, m_new), h_out

    carry0 = (c0.astype(f32), n0.astype(f32), m0.astype(f32))
    blocks = tuple(to_blocks(t, nb, L) for t in (q, k, v, ig, lf))
    (cT, nT, mT), hs = lax.scan(step, carry0, blocks)
    hs = jnp.moveaxis(hs, (0, 2), (1, 3)).reshape(B, T, H_B, DV_B)
    hn = rmsnorm(hs, g_out.reshape(H_B, DV_B)).astype(h.dtype).reshape(B, T, nv)
    out = (jax.nn.sigmoid(o_pre) * hn) @ w_out
    dt = h.dtype
    return out, cT.astype(dt), nT.astype(dt), mT.astype(dt)


def mla(h, past_kv, past_kr, w_dq, g_q, w_uq, w_dkv, g_kv, w_ukv, w_o):
    B, T, _ = h.shape
    P = 0 if past_kv is None else past_kv.shape[1]
    pos_q = P + jnp.arange(T, dtype=jnp.int32)
    pos_k = jnp.arange(P + T, dtype=jnp.int32)
    q = (rmsnorm(h @ w_dq, g_q) @ w_uq).reshape(B, T, H_C, NOPE_C + ROPE_C)
    q_nope = q[..., :NOPE_C]
    q_rope = rope(q[..., NOPE_C:], pos_q, ROPE_C)
    dkv = h @ w_dkv
    kv_row = rmsnorm(dkv[..., :KV_LORA], g_kv)
    kr_row = rope(dkv[..., KV_LORA:][:, :, None, :], pos_q, ROPE_C)[:, :, 0, :]
    kv_all = kv_row if past_kv is None else jnp.concatenate([past_kv.astype(kv_row.dtype), kv_row], axis=1)
    kr_all = kr_row if past_kr is None else jnp.concatenate([past_kr.astype(kr_row.dtype), kr_row], axis=1)
    kv = (kv_all @ w_ukv).reshape(B, P + T, H_C, NOPE_C + V_C)
    k_nope, v = kv[..., :NOPE_C], kv[..., NOPE_C:]
    scale = (NOPE_C + ROPE_C) ** -0.5

    def block(qs, pq):
        qn, qr = qs
        mask = chunk_mask(pq, pos_k)
        s = (jnp.einsum('bqhd,bkhd->bhqk', qn, k_nope, preferred_element_type=jnp.float32)
             + jnp.einsum('bqhr,bkr->bhqk', qr, kr_all, preferred_element_type=jnp.float32)) * scale
        p = jax.nn.softmax(jnp.where(mask, s, -jnp.inf), axis=-1)
        return jnp.einsum('bhqk,bkhd->bqhd', p.astype(v.dtype), v)

    o = sweep_queries(block, (q_nope, q_rope), pos_q)
    return o.reshape(B, T, H_C * V_C) @ w_o, kv_row, kr_row


def swiglu(h, w_in, w_out):
    a, b = jnp.split(h @ w_in, 2, axis=-1)
    return (jax.nn.silu(a) * b) @ w_out


def trunk(x, c, past, p):
    B = x.shape[0]
    a_k, a_v, b_c, b_n, b_m, c_kv, c_kr = [], [], [], [], [], [], []
    for i in range(DEPTH):
        kind, j = i % N_MIXERS, i // N_MIXERS
        mod = (jax.nn.silu(c) @ p['w_ada'][i] + p['b_ada'][i])[:, None, :]
        sh1, sc1, gt1, sh2, sc2, gt2 = jnp.split(mod, 6, axis=-1)
        h = rmsnorm(x, p['g_norm1'][i]) * (1.0 + sc1) + sh1
        if kind == 0:
            pk, pv = (None, None) if past is None else (past[0][j], past[1][j])
            out, kr, vr = diff_attention(h, pk, pv, p['w_a_qkv'][j], p['a_lambda'][j], p['g_a_sub'][j],
                                         p['w_a_o'][j], 0.8 - 0.6 * math.exp(-0.3 * i))
            a_k.append(kr)
            a_v.append(vr)
        elif kind == 1:
            if past is None:
                c0 = jnp.zeros((B, H_B, DQK_B, DV_B), jnp.float32)
                n0 = jnp.zeros((B, H_B, DQK_B), jnp.float32)
                m0 = jnp.zeros((B, H_B), jnp.float32)
            else:
                c0, n0, m0 = past[2][j], past[3][j], past[4][j]
            out, cT, nT, mT = mlstm(h, c0, n0, m0, p['w_b_in'][j], p['b_b_gates'][j], p['g_b_out'][j], p['w_b_out'][j])
            b_c.append(cT)
            b_n.append(nT)
            b_m.append(mT)
        else:
            pkv, pkr = (None, None) if past is None else (past[5][j], past[6][j])
            out, kvr, krr = mla(h, pkv, pkr, p['w_c_dq'][j], p['g_c_q'][j], p['w_c_uq'][j], p['w_c_dkv'][j],
                                p['g_c_kv'][j], p['w_c_ukv'][j], p['w_c_o'][j])
            c_kv.append(kvr)
            c_kr.append(krr)
        x = x + gt1 * out
        h = rmsnorm(x, p['g_norm2'][i]) * (1.0 + sc2) + sh2
        x = x + gt2 * swiglu(h, p['w_ffn_in'][i], p['w_ffn_out'][i])
    y = rmsnorm(x, p['g_final'])
    return y, (jnp.stack(a_k), jnp.stack(a_v), jnp.stack(b_c), jnp.stack(b_n), jnp.stack(b_m),
               jnp.stack(c_kv), jnp.stack(c_kr))


def setup_inputs(seed: int = 0) -> dict:
    key = jax.random.key(seed)
    ks = iter(jax.random.split(key, 48))
    d = D_MODEL

    def nrm(shape, scale):
        return jax.random.normal(next(ks), shape, jnp.float32) * scale

    n_b_in = 2 * H_B * DQK_B + 2 * H_B * DV_B + 2 * H_B
    return {
        'x_prompt': nrm((BATCH, SEQ, d), 1.0),
        'x_sample': nrm((DEC_BATCH, DEC_SEQ, d), 1.0),
        'c_prompt': nrm((BATCH, d), 1.0),
        'c_sample': nrm((DEC_BATCH, d), 1.0),
        'cache_a_k': nrm((N_A_LAYERS, DEC_BATCH, PAST_LEN, H_A, 2 * D_SUB), 1.0),
        'cache_a_v': nrm((N_A_LAYERS, DEC_BATCH, PAST_LEN, H_A, 2 * D_SUB), 1.0),
        'state_b_c': nrm((N_B_LAYERS, DEC_BATCH, H_B, DQK_B, DV_B), 0.1),
        'state_b_n': nrm((N_B_LAYERS, DEC_BATCH, H_B, DQK_B), 0.1),
        'state_b_m': nrm((N_B_LAYERS, DEC_BATCH, H_B), 0.5),
        'cache_c_kv': nrm((N_C_LAYERS, DEC_BATCH, PAST_LEN, KV_LORA), 1.0),
        'cache_c_kr': nrm((N_C_LAYERS, DEC_BATCH, PAST_LEN, ROPE_C), 1.0),
        'w_ada': nrm((DEPTH, d, 6 * d), 0.5 * d ** -0.5),
        'b_ada': nrm((DEPTH, 6 * d), 0.02),
        'g_norm1': 1.0 + nrm((DEPTH, d), 0.02),
        'g_norm2': 1.0 + nrm((DEPTH, d), 0.02),
        'w_a_qkv': nrm((N_A_LAYERS, d, 3 * d), d ** -0.5),
        'a_lambda': nrm((N_A_LAYERS, 4, D_SUB), 0.1),
        'g_a_sub': 1.0 + nrm((N_A_LAYERS, 2 * D_SUB), 0.02),
        'w_a_o': nrm((N_A_LAYERS, d, d), d ** -0.5),
        'w_b_in': nrm((N_B_LAYERS, d, n_b_in), d ** -0.5),
        'b_b_gates': jnp.concatenate([nrm((N_B_LAYERS, H_B), 0.1),
                                      FORGET_BIAS + nrm((N_B_LAYERS, H_B), 0.1)], axis=-1),
        'g_b_out': 1.0 + nrm((N_B_LAYERS, H_B * DV_B), 0.02),
        'w_b_out': nrm((N_B_LAYERS, H_B * DV_B, d), (H_B * DV_B) ** -0.5),
        'w_c_dq': nrm((N_C_LAYERS, d, Q_LORA), d ** -0.5),
        'g_c_q': 1.0 + nrm((N_C_LAYERS, Q_LORA), 0.02),
        'w_c_uq': nrm((N_C_LAYERS, Q_LORA, H_C * (NOPE_C + ROPE_C)), Q_LORA ** -0.5),
        'w_c_dkv': nrm((N_C_LAYERS, d, KV_LORA + ROPE_C), d ** -0.5),
        'g_c_kv': 1.0 + nrm((N_C_LAYERS, KV_LORA), 0.02),
        'w_c_ukv': nrm((N_C_LAYERS, KV_LORA, H_C * (NOPE_C + V_C)), KV_LORA ** -0.5),
        'w_c_o': nrm((N_C_LAYERS, H_C * V_C, d), (H_C * V_C) ** -0.5),
        'w_ffn_in': nrm((DEPTH, d, 2 * D_FF), d ** -0.5),
        'w_ffn_out': nrm((DEPTH, D_FF, d), D_FF ** -0.5),
        'g_final': 1.0 + nrm((d,), 0.02),
    }


def reference(x_prompt, x_sample, c_prompt, c_sample, cache_a_k, cache_a_v, state_b_c, state_b_n,
              state_b_m, cache_c_kv, cache_c_kr, w_ada, b_ada, g_norm1, g_norm2, w_a_qkv, a_lambda,
              g_a_sub, w_a_o, w_b_in, b_b_gates, g_b_out, w_b_out, w_c_dq, g_c_q, w_c_uq, w_c_dkv,
              g_c_kv, w_c_ukv, w_c_o, w_ffn_in, w_ffn_out, g_final):
    p = dict(w_ada=w_ada, b_ada=b_ada, g_norm1=g_norm1, g_norm2=g_norm2, w_a_qkv=w_a_qkv,
             a_lambda=a_lambda, g_a_sub=g_a_sub, w_a_o=w_a_o, w_b_in=w_b_in, b_b_gates=b_b_gates,
             g_b_out=g_b_out, w_b_out=w_b_out, w_c_dq=w_c_dq, g_c_q=g_c_q, w_c_uq=w_c_uq,
             w_c_dkv=w_c_dkv, g_c_kv=g_c_kv, w_c_ukv=w_c_ukv, w_c_o=w_c_o, w_ffn_in=w_ffn_in,
             w_ffn_out=w_ffn_out, g_final=g_final)
    y_prompt, sp = trunk(x_prompt, c_prompt, None, p)
    past = (cache_a_k, cache_a_v, state_b_c, state_b_n, state_b_m, cache_c_kv, cache_c_kr)
    y_sample, ss = trunk(x_sample, c_sample, past, p)
    a_k_p, a_v_p, b_c_p, b_n_p, b_m_p, c_kv_p, c_kr_p = sp
    a_k_s, a_v_s, b_c_s, b_n_s, b_m_s, c_kv_s, c_kr_s = ss
    return (y_prompt, y_sample, a_k_p, a_v_p, b_c_p, b_n_p, b_m_p, c_kv_p, c_kr_p,
            a_k_s, a_v_s, b_c_s, b_n_s, b_m_s, c_kv_s, c_kr_s)
```

```python
import math
from contextlib import ExitStack
import numpy as np
import concourse.bass as bass
import concourse.mybir as mybir
from concourse.bass_utils import run_bass_kernel_spmd

F32 = mybir.dt.float32
BF16 = mybir.dt.bfloat16
AF = mybir.ActivationFunctionType
ALU = mybir.AluOpType
AX = mybir.AxisListType

D = 1024
NCH = 8
EPS = 1e-6
ROPE_THETA = 500000.0
DEPTH = 4
H_A = 8
H_B = 4
DQK_B = 128
DV_B = 256
H_C = 8
Q_LORA = 384
KV_LORA = 256
D_FF = 2816
NFF = 22
N_B_IN = 3080


class Cfg:
    def __init__(self, TP=4096, G=256, PAST=2048, TS=32, NS=2, layers=(0, 1, 2, 3)):
        self.TP, self.G, self.PAST, self.TS, self.NS = TP, G, PAST, TS, NS
        self.layers = tuple(layers)


class Prog:
    ENGS = ("pe", "act", "dve", "pool", "sp")

    def __init__(self, nc, es):
        self.nc, self.es = nc, es
        self.eobj = {"pe": nc.tensor, "act": nc.scalar, "dve": nc.vector, "pool": nc.gpsimd, "sp": nc.sync}
        self.cnt = {e: 0 for e in self.ENGS}
        self.esem = {e: es.enter_context(nc.semaphore("s_" + e)) for e in self.ENGS}
        self.seen = {e: {} for e in self.ENGS}
        self.res = {}
        self.dsem = {}
        self.nsem = 0
        self.nwait = 0
        self.nops = 0
        self.trace = {e: [] for e in self.ENGS}

    def _waits(self, eng, r, w):
        toks = []
        for k in r:
            st = self.res.get(k)
            if st is not None and st[0] is not None:
                toks.append(st[0])
        for k in w:
            st = self.res.get(k)
            if st is not None:
                if st[0] is not None:
                    toks.append(st[0])
                toks.extend(st[1])
        e = self.eobj[eng]
        seen = self.seen[eng]
        for (name, sem, val, src) in toks:
            if src == eng and eng == "pe":
                continue
            if seen.get(name, 0) >= val:
                continue
            seen[name] = val
            e.wait_ge(sem, val)
            self.trace[eng].append(("w", name, val))
            self.nwait += 1

    def _commit(self, tok, r, w):
        for k in r:
            st = self.res.get(k)
            if st is None:
                st = [None, []]
                self.res[k] = st
            st[1].append(tok)
        for k in w:
            self.res[k] = [tok, []]

    def op(self, eng, fn, r=(), w=()):
        if eng != "pe":
            psr = [k for k in r if isinstance(k, tuple) and k[0] == "ps"]
            if psr:
                r = [k for k in r if k not in psr]
                w = list(w) + psr
        self._waits(eng, r, w)
        inst = fn(self.eobj[eng])
        self.cnt[eng] += 1
        inst.then_inc(self.esem[eng], 1)
        self.trace[eng].append(("i", "s_" + eng, 1))
        tok = ("s_" + eng, self.esem[eng], self.cnt[eng], eng)
        self._commit(tok, r, w)
        self.nops += 1

    def dma(self, q, out, in_, r=(), w=(), semkey=None):
        self._waits(q, r, w)
        if semkey is None:
            semkey = w[0] if len(w) else r[0]
        if semkey not in self.dsem:
            self.dsem[semkey] = [self.es.enter_context(self.nc.semaphore("d%d" % self.nsem)), 0]
            self.nsem += 1
        ds = self.dsem[semkey]
        inst = self.eobj[q].dma_start(out=out, in_=in_)
        ds[1] += 16
        inst.then_inc(ds[0], 16)
        self.trace[q].append(("i", "d" + str(semkey), 16))
        tok = ("d" + str(semkey), ds[0], ds[1], None)
        self._commit(tok, r, w)
        self.nops += 1

    def barrier(self):
        for e in self.ENGS:
            eo = self.eobj[e]
            for e2 in self.ENGS:
                if e2 != e and self.cnt[e2] > self.seen[e].get("s_" + e2, 0):
                    eo.wait_ge(self.esem[e2], self.cnt[e2])
                    self.trace[e].append(("w", "s_" + e2, self.cnt[e2]))
                    self.seen[e]["s_" + e2] = self.cnt[e2]
            for k, ds in self.dsem.items():
                nm = "d" + str(k)
                if ds[1] > self.seen[e].get(nm, 0):
                    eo.wait_ge(ds[0], ds[1])
                    self.trace[e].append(("w", nm, ds[1]))
                    self.seen[e][nm] = ds[1]

    def finish(self):
        eo = self.eobj["sp"]
        for k, ds in self.dsem.items():
            eo.wait_ge(ds[0], ds[1])
            self.trace["sp"].append(("w", "d" + str(k), ds[1]))
        self.check_deadlock()

    def check_deadlock(self):
        val = {}
        ptr = {e: 0 for e in self.ENGS}
        prog = True
        while prog:
            prog = False
            for e in self.ENGS:
                tr = self.trace[e]
                while ptr[e] < len(tr):
                    k, nm, v = tr[ptr[e]]
                    if k == "w":
                        if val.get(nm, 0) >= v:
                            ptr[e] += 1
                            prog = True
                        else:
                            break
                    else:
                        val[nm] = val.get(nm, 0) + v
                        ptr[e] += 1
                        prog = True
        bad = {e: (ptr[e], len(self.trace[e]), self.trace[e][ptr[e]]) for e in self.ENGS if ptr[e] < len(self.trace[e])}
        if bad:
            raise RuntimeError("DEADLOCK in schedule: %r ; sem values: %r" % (bad, {k: val.get(k, 0) for k in [b[2][1] for b in bad.values()]}))

    def mm(self, out, lhsT, rhs, start, stop, r, w):
        self.op("pe", lambda e: e.matmul(out, lhsT=lhsT, rhs=rhs, start=start, stop=stop), r, w)

    def tr(self, out, in_, ident, r, w):
        self.op("pe", lambda e: e.transpose(out, in_, ident), r, w)

    def act(self, out, in_, func, r, w, bias=None, scale=None, accum=None):
        kw = {}
        if bias is not None:
            kw["bias"] = bias
        if scale is not None:
            kw["scale"] = scale
        if accum is not None:
            kw["accum_out"] = accum
        self.op("act", lambda e: e.activation(out=out, in_=in_, func=func, **kw), r, w)

    def tt(self, eng, out, a, b, op, r, w):
        self.op(eng, lambda e: e.tensor_tensor(out=out, in0=a, in1=b, op=op), r, w)

    def ts(self, eng, out, a, s1, s2, op0, op1, r, w):
        if op1 is None:
            self.op(eng, lambda e: e.tensor_scalar(out=out, in0=a, scalar1=s1, scalar2=None, op0=op0), r, w)
        else:
            self.op(eng, lambda e: e.tensor_scalar(out=out, in0=a, scalar1=s1, scalar2=s2, op0=op0, op1=op1), r, w)

    def stt(self, out, a, s, b, op0, op1, r, w):
        self.op("dve", lambda e: e.scalar_tensor_tensor(out=out, in0=a, scalar=s, in1=b, op0=op0, op1=op1), r, w)

    def cp(self, eng, out, in_, r, w):
        if eng == "act":
            self.op("act", lambda e: e.copy(out=out, in_=in_), r, w)
        else:
            self.op(eng, lambda e: e.tensor_copy(out=out, in_=in_), r, w)

    def memset(self, eng, ap, val, w):
        self.op(eng, lambda e: e.memset(ap, val), (), w)


def build(cfg):
    nc = bass.Bass("TRN2", target_bir_lowering=False)
    TP, G, PAST, TS, NS = cfg.TP, cfg.G, cfg.PAST, cfg.TS, cfg.NS
    NSK = NS * TS
    NTP = TP // 128
    NR = 1 + NS
    NPT = PAST // 128

    def din(name, shape, dt=F32):
        return nc.dram_tensor(name, list(shape), dt, kind="ExternalInput").ap()

    def dout(name, shape, dt=F32):
        return nc.dram_tensor(name, list(shape), dt, kind="ExternalOutput").ap()

    x_p = din("x_p", [TP, D]); x_s = din("x_s", [NSK, D]); c_in = din("c_in", [NR, D])
    ca_k = din("ca_k", [2, NS, PAST, D]); ca_v = din("ca_v", [2, NS, PAST, D])
    sb_c = din("sb_c", [NS, H_B, DQK_B, DV_B]); sb_n = din("sb_n", [NS, H_B, DQK_B]); sb_m = din("sb_m", [NS, H_B])
    cc_kv = din("cc_kv", [NS, PAST, KV_LORA]); cc_kr = din("cc_kr", [NS, PAST, 64])
    w_ada = din("w_ada", [DEPTH, D, 6 * D]); b_ada = din("b_ada", [DEPTH, 6 * D])
    g_norm1 = din("g_norm1", [DEPTH, D]); g_norm2 = din("g_norm2", [DEPTH, D])
    w_a_qkv = din("w_a_qkv", [2, D, 3 * D]); a_lambda = din("a_lambda", [2, 4, 64]); g_a_sub = din("g_a_sub", [2, 128])
    w_a_o = din("w_a_o", [2, D, D])
    w_b_in = din("w_b_in", [1, D, N_B_IN]); b_b_gates = din("b_b_gates", [1, 8]); g_b_out = din("g_b_out", [1, D])
    w_b_out = din("w_b_out", [1, D, D])
    w_c_dq = din("w_c_dq", [1, D, Q_LORA]); g_c_q = din("g_c_q", [1, Q_LORA]); w_c_uq = din("w_c_uq", [1, Q_LORA, 1536])
    w_c_dkv = din("w_c_dkv", [1, D, 320]); g_c_kv = din("g_c_kv", [1, KV_LORA]); w_c_ukv = din("w_c_ukv", [1, KV_LORA, 2048])
    w_c_o = din("w_c_o", [1, D, D])
    w_ffn_in = din("w_ffn_in", [DEPTH, D, 2 * D_FF]); w_ffn_out = din("w_ffn_out", [DEPTH, D_FF, D])
    g_final = din("g_final", [1, D])
    k_ident = din("k_ident", [128, 128])
    k_tri = din("k_tri", [64, 64])
    k_sel = din("k_sel", [4, 4 * 64])
    rope_a = din("rope_a", [TP + NSK, 16])
    rope_c = din("rope_c", [TP + NSK, 64])

    y_p = dout("y_p", [TP, D]); y_s = dout("y_s", [NSK, D])
    akp = dout("akp", [2, TP, D]); avp = dout("avp", [2, TP, D])
    bcp = dout("bcp", [H_B, DQK_B, DV_B]); bnp = dout("bnp", [H_B, DQK_B]); bmp = dout("bmp", [H_B, 1])
    ckvp = dout("ckvp", [TP, KV_LORA]); ckrp = dout("ckrp", [TP, 64])
    aks = dout("aks", [2, NSK, D]); avs = dout("avs", [2, NSK, D])
    bcs = dout("bcs", [NS, H_B, DQK_B, DV_B]); bns = dout("bns", [NS, H_B, DQK_B]); bms = dout("bms", [NS, H_B, 1])
    ckvs = dout("ckvs", [NSK, KV_LORA]); ckrs = dout("ckrs", [NSK, 64])
    mla_kv_p = nc.dram_tensor("mla_kv_p", [TP, 2048], BF16, kind="Internal").ap()
    mla_kv_s = nc.dram_tensor("mla_kv_s", [NS, PAST + TS, 2048], BF16, kind="Internal").ap()

    es = ExitStack()
    with es:
        P = Prog(nc, es)

        uniq = [0]

        def sb(name, shape, dt=F32, stack=None):
            if stack is not None:
                uniq[0] += 1
                name = "%s_u%d" % (name, uniq[0])
            return (stack or es).enter_context(nc.sbuf_tensor(name, list(shape), dt))

        PS = [es.enter_context(nc.psum_tensor("ps%d" % i, [128, 512], F32)) for i in range(8)]

        def psk(i):
            return ("ps", i)

        def psb(i):
            return PS[i][:].bitcast(BF16)

        xTp = sb("xTp", [128, NCH, TP])
        xTs = sb("xTs", [128, NCH, NSK])
        identF = sb("identF", [128, 128]); identB = sb("identB", [128, 128], BF16)
        onesB = sb("onesB", [128, 128], BF16); onesF = sb("onesF", [128, 128])
        hT = sb("hT", [128, NCH, G], BF16)
        sqr = [sb("sq%d" % i, [128, G], BF16) for i in range(2)]
        tmpN = [sb("tmpN%d" % i, [128, G]) for i in range(2)]
        rstd = sb("rstd", [128, G])
        NWB = 2
        WB = [sb("wb%d" % i, [128, NCH, 512], BF16) for i in range(NWB)]
        NSL = 4
        SL = [sb("sl%d" % i, [128, 512]) for i in range(NSL)]
        modT = sb("modT", [128, DEPTH, 48, NR])
        A1 = sb("A1", [128, DEPTH, NCH, NR]); A2 = sb("A2", [128, DEPTH, NCH, NR])
        gn1T = sb("gn1T", [128, DEPTH * NCH]); gn2T = sb("gn2T", [128, DEPTH * NCH]); gfT = sb("gfT", [128, NCH])
        ropeA = sb("ropeA", [128, NTP + 1, 16])
        epsT = sb("epsT", [128, 1])
        uT = sb("uT", [128, 11, G], BF16)

        st = {"wb": 0, "sl": 0, "sq": 0, "tn": 0, "pa": 0}

        P.dma("sp", identF[:], k_ident[:, :], (), ["identF"])
        P.cp("dve", identB[:], identF[:], ["identF"], ["identB"])
        P.memset("dve", onesB[:], 1.0, ["onesB"])
        P.memset("dve", onesF[:], 1.0, ["onesF"])
        P.memset("dve", epsT[:], EPS, ["epsT"])

        class Grp:
            pass
        groups = []
        for g in range(TP // G):
            gr = Grp()
            gr.kind = "p"; gr.g = g; gr.n = G; gr.xT = xTp; gr.c0 = g * G
            gr.segs = [(0, 0, G)]
            gr.tiles = [(t * 128, 128, g * (G // 128) + t) for t in range(G // 128)]
            groups.append(gr)
        gs = Grp()
        gs.kind = "s"; gs.g = 0; gs.n = NSK; gs.xT = xTs; gs.c0 = 0
        gs.segs = [(1 + s, s * TS, TS) for s in range(NS)]
        gs.tiles = [(0, NSK, NTP)]
        groups.append(gs)

        def xview(gr, c, a=0, n=None):
            n = gr.n - a if n is None else n
            return gr.xT[:, c, gr.c0 + a:gr.c0 + a + n]

        def xkey(gr):
            return ("x", gr.kind, gr.g)

        def wload(view, nk, ncols):
            i = st["wb"] % NWB
            st["wb"] += 1
            key = ("wb", i)
            P.dma("pool", WB[i][:, 0:nk, 0:ncols], view, (), [key])
            return WB[i], key

        def wview(w2d, c0, ncols, k0=0, nk=None):
            v = w2d.rearrange("(kc p) n -> p kc n", p=128)
            nk = v.shape[1] - k0 if nk is None else nk
            return v[:, k0:k0 + nk, c0:c0 + ncols]

        def next_ps():
            i = st["pa"] % 2
            st["pa"] += 1
            return i

        def next_sl():
            i = st["sl"] % NSL
            st["sl"] += 1
            return i

        def load_rows_T(dst, dkey, src_rows_ap, nrows):
            i = next_sl()
            P.dma("sp", SL[i][0:nrows, 0:128], src_rows_ap, (), [("sl", i)])
            b = next_ps()
            P.tr(PS[b][:, 0:nrows], SL[i][0:nrows, 0:128], identF[0:nrows, 0:nrows], [("sl", i), "identF"], [psk(b)])
            P.cp("dve", dst, PS[b][:, 0:nrows], [psk(b)], [dkey])

        cT = sb("cT", [128, NCH, NR]); scT = sb("scT", [128, NCH, NR], BF16)
        for k in range(NCH):
            load_rows_T(cT[:, k, :], "cT", c_in[:, k * 128:(k + 1) * 128], NR)
        P.act(scT[:], cT[:], AF.Silu, ["cT"], ["scT"])
        bT = sb("bT", [128, DEPTH, 48])
        for i in range(DEPTH):
            load_rows_T(bT[:, i, :], "bT", b_ada[i, :].rearrange("(j p) -> j p", p=128), 48)
        load_rows_T(gn1T[:], "gn1T", g_norm1.rearrange("i (c p) -> (i c) p", p=128), DEPTH * NCH)
        load_rows_T(gn2T[:], "gn2T", g_norm2.rearrange("i (c p) -> (i c) p", p=128), DEPTH * NCH)
        load_rows_T(gfT[:], "gfT", g_final.rearrange("i (c p) -> (i c) p", p=128), NCH)
        P.dma("sp", ropeA[:, 0:NTP, :], rope_a[0:TP, :].rearrange("(t p) f -> p t f", p=128), (), ["ropetab"])
        P.dma("sp", ropeA[0:NSK, NTP, :], rope_a[TP:TP + NSK, :], (), ["ropetab"], semkey="ropetab2")

        for i in cfg.layers:
            mb = 7
            for cb in range(12):
                wt, wk = wload(wview(w_ada[i], cb * 512, 512), NCH, 512)
                for fc in range(4):
                    j = cb * 4 + fc
                    for k in range(NCH):
                        P.mm(PS[mb][:, j * NR:(j + 1) * NR], wt[:, k, fc * 128:(fc + 1) * 128], scT[:, k, :],
                             k == 0, k == NCH - 1, [wk, "scT"], [psk(mb)])
            P.tt("dve", modT[:, i, :, :], PS[mb][:, 0:48 * NR].rearrange("p (j r) -> p j r", r=NR),
                 bT[:, i, :].unsqueeze(2).broadcast_to([128, 48, NR]), ALU.add, [psk(mb), "bT"], ["modT"])
            P.stt(A1[:, i, :, :], modT[:, i, 8:16, :], 1.0,
                  gn1T[:, i * NCH:(i + 1) * NCH].unsqueeze(2).broadcast_to([128, NCH, NR]), ALU.add, ALU.mult,
                  ["modT", "gn1T"], ["A1"])
            P.stt(A2[:, i, :, :], modT[:, i, 32:40, :], 1.0,
                  gn2T[:, i * NCH:(i + 1) * NCH].unsqueeze(2).broadcast_to([128, NCH, NR]), ALU.add, ALU.mult,
                  ["modT", "gn2T"], ["A2"])

        def sh(i, which, c, r):
            base = 0 if which == 1 else 24
            return modT[:, i, base + c, r:r + 1]

        def gt(i, which, c, r):
            base = 16 if which == 1 else 40
            return modT[:, i, base + c, r:r + 1]

        def Ac(i, which, c, r):
            return (A1 if which == 1 else A2)[:, i, c, r:r + 1]

        def load_xT(gr, src):
            for (c0, nt, _ti) in gr.tiles:
                for half in range(2):
                    i = next_sl()
                    P.dma("sp", SL[i][0:nt, :], src[gr.c0 + c0:gr.c0 + c0 + nt, half * 512:(half + 1) * 512], (), [("sl", i)])
                    b = next_ps()
                    for q in range(4):
                        P.tr(PS[b][:, q * 128:q * 128 + nt], SL[i][0:nt, q * 128:(q + 1) * 128], identF[0:nt, 0:nt],
                             [("sl", i), "identF"], [psk(b)])
                    P.cp("act" if half else "dve", gr.xT[:, half * 4:half * 4 + 4, gr.c0 + c0:gr.c0 + c0 + nt],
                         PS[b][:].rearrange("p (q t) -> p q t", t=128)[:, :, 0:nt], [psk(b)], [xkey(gr)])

        for gr in groups:
            load_xT(gr, x_p if gr.kind == "p" else x_s)

        def norm_stats(gr):
            n = gr.n
            b = next_ps()
            for c in range(NCH):
                i = st["sq"] % 2
                st["sq"] += 1
                P.act(sqr[i][:, 0:n], xview(gr, c), AF.Square, [xkey(gr)], [("sq", i)])
                P.mm(PS[b][:, 0:n], onesB[:], sqr[i][:, 0:n], c == 0, c == NCH - 1, [("sq", i), "onesB"], [psk(b)])
            P.act(rstd[:, 0:n], PS[b][:, 0:n], AF.Sqrt, [psk(b), "epsT"], ["rstd"], bias=epsT[:, 0:1], scale=1.0 / D)
            P.op("dve", lambda e: e.reciprocal(out=rstd[:, 0:n], in_=rstd[:, 0:n]), ["rstd"], ["rstd"])

        def norm_mod(gr, i, which):
            norm_stats(gr)
            for c in range(NCH):
                for (r, a, n) in gr.segs:
                    t = st["tn"] % 2
                    st["tn"] += 1
                    P.stt(tmpN[t][:, 0:n], xview(gr, c, a, n), Ac(i, which, c, r), rstd[:, a:a + n], ALU.mult, ALU.mult,
                          [xkey(gr), "rstd", "A1", "A2"], [("tn", t)])
                    P.act(hT[:, c, a:a + n], tmpN[t][:, 0:n], AF.Identity, [("tn", t), "modT"], ["hT"],
                          bias=sh(i, which, c, r), scale=1.0)

        def resid_add(gr, i, which, c, psap, pskey, extra_r=()):
            for (r, a, n) in gr.segs:
                P.stt(xview(gr, c, a, n), psap[:, a:a + n], gt(i, which, c, r), xview(gr, c, a, n), ALU.mult, ALU.add,
                      [pskey, "modT", xkey(gr)] + list(extra_r), [xkey(gr)])

        def ffn(gr, i):
            n = gr.n
            norm_mod(gr, i, 2)
            for half in range(2):
                f0 = half * 11
                blocks = [(f0, 4), (f0 + 4, 4), (f0 + 8, 3)]
                for (fb, nf) in blocks:
                    wa, wak = wload(wview(w_ffn_in[i], fb * 128, nf * 128), NCH, nf * 128)
                    wb_, wbk = wload(wview(w_ffn_in[i], D_FF + fb * 128, nf * 128), NCH, nf * 128)
                    for f in range(nf):
                        ba = next_ps()
                        for k in range(NCH):
                            P.mm(PS[ba][:, 0:n], wa[:, k, f * 128:(f + 1) * 128], hT[:, k, 0:n], k == 0, k == NCH - 1,
                                 [wak, "hT"], [psk(ba)])
                        bb = next_ps()
                        for k in range(NCH):
                            P.mm(PS[bb][:, 0:n], wb_[:, k, f * 128:(f + 1) * 128], hT[:, k, 0:n], k == 0, k == NCH - 1,
                                 [wbk, "hT"], [psk(bb)])
                        t = st["tn"] % 2
                        st["tn"] += 1
                        P.act(tmpN[t][:, 0:n], PS[ba][:, 0:n], AF.Silu, [psk(ba)], [("tn", t)])
                        P.tt("dve", uT[:, fb - f0 + f, 0:n], tmpN[t][:, 0:n], PS[bb][:, 0:n], ALU.mult,
                             [("tn", t), psk(bb)], [("uT", fb - f0 + f)])
                for ch in range(2):
                    banks = [4, 5, 6, 7]
                    wts = []
                    for (k0, nk) in ((0, 8), (8, 3)):
                        wt, wk = wload(wview(w_ffn_out[i], ch * 512, 512, k0=f0 + k0, nk=nk), nk, 512)
                        wts.append((wt, wk, k0, nk))
                    for cc in range(4):
                        c = ch * 4 + cc
                        for (wt, wk, k0, nk) in wts:
                            for k in range(nk):
                                f = k0 + k
                                P.mm(PS[banks[cc]][:, 0:n], wt[:, k, cc * 128:(cc + 1) * 128], uT[:, f, 0:n],
                                     f == 0, f == 10, [wk, ("uT", f)], [psk(banks[cc])])
                        resid_add(gr, i, 2, c, PS[banks[cc]], psk(banks[cc]))

        def final_out(gr, dst):
            n = gr.n
            norm_stats(gr)
            for c in range(NCH):
                P.stt(hT_f[:, c, 0:n], xview(gr, c), gfT[:, c:c + 1], rstd[:, 0:n], ALU.mult, ALU.mult,
                      [xkey(gr), "rstd", "gfT"], ["hTf"])
            for (c0, nt, _ti) in gr.tiles:
                for half in range(2):
                    b = next_ps()
                    for q in range(4):
                        P.tr(PS[b][0:nt, q * 128:(q + 1) * 128], hT_f[:, half * 4 + q, c0:c0 + nt], identF[:, :],
                             ["hTf", "identF"], [psk(b)])
                    i = next_sl()
                    P.cp("act", SL[i][0:nt, :], PS[b][0:nt, :], [psk(b)], [("sl", i)])
                    P.dma("sp", dst[gr.c0 + c0:gr.c0 + c0 + nt, half * 512:(half + 1) * 512], SL[i][0:nt, :], [("sl", i)], ())

        def rope_slab(v, skey, nt, cos2, sin2, half, rot_cols, nsub, eng="dve"):
            x1 = v[:, :, rot_cols:rot_cols + half]
            x2 = v[:, :, rot_cols + half:rot_cols + 2 * half]
            cos = cos2.unsqueeze(1).broadcast_to([nt, nsub, half])
            sin = sin2.unsqueeze(1).broadcast_to([nt, nsub, half])
            t1 = rtmp[0][0:nt, 0:nsub * half].rearrange("p (s d) -> p s d", d=half)
            t2 = rtmp[1][0:nt, 0:nsub * half].rearrange("p (s d) -> p s d", d=half)
            t3 = rtmp[2][0:nt, 0:nsub * half].rearrange("p (s d) -> p s d", d=half)
            t4 = rtmp[3][0:nt, 0:nsub * half].rearrange("p (s d) -> p s d", d=half)
            tk = "ropetab"
            P.tt(eng, t1, x1, cos, ALU.mult, [skey, tk], ["rt1"])
            P.tt(eng, t2, x2, sin, ALU.mult, [skey, tk], ["rt2"])
            P.tt(eng, t3, x2, cos, ALU.mult, [skey, tk], ["rt3"])
            P.tt(eng, t4, x1, sin, ALU.mult, [skey, tk], ["rt4"])
            P.tt(eng, x1, t1, t2, ALU.subtract, ["rt1", "rt2", skey], [skey])
            P.tt(eng, x2, t3, t4, ALU.add, ["rt3", "rt4", skey], [skey])

        rtmp = [sb("rtmp%d" % i, [128, 64]) for i in range(4)]

        def attn_core(ls, gr, nq_cols, q_c0, heads_fn, key_tiles, lam_ap, kind, out_fn):
            pass

        def mixer_diff(gr, i, j, L):
            n = gr.n
            lam_init = 0.8 - 0.6 * math.exp(-0.3 * i)
            norm_mod(gr, i, 1)
            QT = L["QT"]; onT = L["onT"]
            kdst = akp if gr.kind == "p" else aks
            vdst = avp if gr.kind == "p" else avs
            kvkey = ("kv", j, gr.kind, gr.g)
            for cb in range(6):
                wt, wk = wload(wview(w_a_qkv[j], cb * 512, 512), NCH, 512)
                for (c0, nt, ti) in gr.tiles:
                    b = next_ps()
                    for k in range(NCH):
                        P.mm(PS[b][0:nt, :], hT[:, k, c0:c0 + nt], wt[:, k, :], k == 0, k == NCH - 1, ["hT", wk], [psk(b)])
                    s = next_sl()
                    sk = ("sl", s)
                    P.cp("act", SL[s][0:nt, :], PS[b][0:nt, :], [psk(b)], [sk])
                    if cb < 4:
                        rope_slab(SL[s][0:nt, :].rearrange("p (s d) -> p s d", d=64), sk, nt, ropeA[0:nt, ti, 0:8],
                                  ropeA[0:nt, ti, 8:16], 8, 0, 8, eng="pool" if cb % 2 else "dve")
                    if cb < 2:
                        qb = L["qb"]
                        P.cp("act", qb[0:nt, :], SL[s][0:nt, :], [sk], ["qb"])
                        tb = 2
                        for q in range(4):
                            P.tr(psb(tb)[:, q * 128:q * 128 + nt], qb[0:nt, q * 128:(q + 1) * 128], identB[0:nt, 0:nt],
                                 ["qb", "identB"], [psk(tb)])
                        P.cp("dve", QT[:, cb * 4:cb * 4 + 4, c0:c0 + nt],
                             psb(tb)[:, 0:512].rearrange("p (q t) -> p q t", t=128)[:, :, 0:nt], [psk(tb)], ["QT"])
                    elif cb < 4:
                        P.dma("sp", kdst[j, gr.c0 + c0:gr.c0 + c0 + nt, (cb - 2) * 512:(cb - 1) * 512], SL[s][0:nt, :],
                              [sk], [kvkey], semkey=("slo", s))
                    else:
                        P.dma("sp", vdst[j, gr.c0 + c0:gr.c0 + c0 + nt, (cb - 4) * 512:(cb - 3) * 512], SL[s][0:nt, :],
                              [sk], [kvkey], semkey=("slo", s))
            scale = 64 ** -0.5
            for si, (r, a, nq) in enumerate(gr.segs):
                if gr.kind == "p":
                    nkt = (gr.c0 + n) // 128
                    ktl = []
                    for kt in range(nkt):
                        gk = (kt * 128) // G
                        ktl.append((akp[j, kt * 128:(kt + 1) * 128, :], avp[j, kt * 128:(kt + 1) * 128, :], 128,
                                    kt * 128 - gr.c0, ("kv", j, "p", gk)))
                else:
                    s_ = si
                    ktl = []
                    for kt in range(NPT):
                        ktl.append((ca_k[j, s_, kt * 128:(kt + 1) * 128, :], ca_v[j, s_, kt * 128:(kt + 1) * 128, :], 128,
                                    -1, None))
                    ktl.append((aks[j, s_ * TS:(s_ + 1) * TS, :], avs[j, s_ * TS:(s_ + 1) * TS, :], TS, -1, kvkey))
                for hd in range(H_A):
                    first = True
                    nk_total = len(ktl)
                    for kti, (ksrc, vsrc, nk, dcol, dep) in enumerate(ktl):
                        last = kti == nk_total - 1
                        cq0 = max(dcol, 0)
                        kb = L["kb"][kti % 2]; vb = L["vb"][kti % 2]
                        kbk = ("kb", kti % 2); vbk = ("vb", kti % 2)
                        deps = [dep] if dep is not None else []
                        P.dma("pool", kb[0:nk, :], ksrc[:, hd * 128:(hd + 1) * 128], deps, [kbk])
                        P.dma("pool", vb[0:nk, :], vsrc[:, hd * 128:(hd + 1) * 128], deps, [vbk])
                        tb = 2
                        P.tr(psb(tb)[:, 0:nk], kb[0:nk, :], identB[0:nk, 0:nk], [kbk, "identB"], [psk(tb)])
                        kT = L["kT"][kti % 2]; kTk = ("kT", kti % 2)
                        P.cp("dve", kT[:, 0:nk], psb(tb)[:, 0:nk], [psk(tb)], [kTk])
                        qa, qn = a + cq0, nq - cq0
                        P.mm(PS[0][0:nk, 0:qn], kT[0:64, 0:nk], QT[0:64, hd, qa:qa + qn], True, True, [kTk, "QT"], [psk(0)])
                        P.mm(PS[1][0:nk, 0:qn], kT[64:128, 0:nk], QT[64:128, hd, qa:qa + qn], True, True, [kTk, "QT"], [psk(1)])
                        e1 = L["e1"][kti % 2]; e2 = L["e2"][kti % 2]
                        e1k = ("e1", kti % 2); e2k = ("e2", kti % 2)
                        P.act(e1[0:nk, 0:qn], PS[0][0:nk, 0:qn], AF.Exp, [psk(0)], [e1k], scale=scale)
                        P.act(e2[0:nk, 0:qn], PS[1][0:nk, 0:qn], AF.Exp, [psk(1)], [e2k], scale=scale)
                        if dcol >= 0:
                            P.memset("pool", e1[64:128, 0:64], 0.0, [e1k])
                            P.memset("pool", e2[64:128, 0:64], 0.0, [e2k])
                        P.mm(PS[4][:, cq0:nq], vb[0:nk, :], e1[0:nk, 0:qn], first, last, [vbk, e1k], [psk(4)])
                        P.mm(PS[5][:, cq0:nq], vb[0:nk, :], e2[0:nk, 0:qn], first, last, [vbk, e2k], [psk(5)])
                        P.mm(PS[6][:, cq0:nq], onesB[0:nk, :], e1[0:nk, 0:qn], first, last, ["onesB", e1k], [psk(6)])
                        P.mm(PS[7][:, cq0:nq], onesB[0:nk, :], e2[0:nk, 0:qn], first, last, ["onesB", e2k], [psk(7)])
                        first = False
                    r1 = L["r1"]; r2 = L["r2"]; o1 = L["o1"]; o2 = L["o2"]
                    P.op("dve", lambda e: e.reciprocal(out=r1[:, 0:nq], in_=PS[6][:, 0:nq]), [psk(6)], ["r1"])
                    P.op("dve", lambda e: e.reciprocal(out=r2[:, 0:nq], in_=PS[7][:, 0:nq]), [psk(7)], ["r2"])
                    P.tt("dve", o1[:, 0:nq], PS[4][:, 0:nq], r1[:, 0:nq], ALU.mult, [psk(4), "r1"], ["o1"])
                    P.tt("dve", o2[:, 0:nq], PS[5][:, 0:nq], r2[:, 0:nq], ALU.mult, [psk(5), "r2"], ["o2"])
                    P.stt(o1[:, 0:nq], o2[:, 0:nq], L["nlam"][:, 0:1], o1[:, 0:nq], ALU.mult, ALU.add, ["o1", "o2", "nlam"], ["o1"])
                    sqi = st["sq"] % 2
                    st["sq"] += 1
                    P.act(sqr[sqi][:, 0:nq], o1[:, 0:nq], AF.Square, ["o1"], [("sq", sqi)])
                    P.mm(PS[3][:, 0:nq], onesB[:], sqr[sqi][:, 0:nq], True, True, [("sq", sqi), "onesB"], [psk(3)])
                    P.act(r1[:, 0:nq], PS[3][:, 0:nq], AF.Sqrt, [psk(3), "epsT"], ["r1"], bias=epsT[:, 0:1], scale=1.0 / 128)
                    P.op("dve", lambda e: e.reciprocal(out=r1[:, 0:nq], in_=r1[:, 0:nq]), ["r1"], ["r1"])
                    P.stt(onT[:, hd, a:a + nq], o1[:, 0:nq], L["gsub"][:, 0:1], r1[:, 0:nq], ALU.mult, ALU.mult,
                          ["o1", "gsub", "r1"], ["onT"])
            for ch in range(2):
                wt, wk = wload(wview(w_a_o[j], ch * 512, 512), NCH, 512)
                for cc in range(4):
                    c = ch * 4 + cc
                    b = next_ps()
                    for hd in range(H_A):
                        P.mm(PS[b][:, 0:n], wt[:, hd, cc * 128:(cc + 1) * 128], onT[:, hd, 0:n], hd == 0, hd == H_A - 1,
                             [wk, "onT"], [psk(b)])
                    resid_add(gr, i, 1, c, PS[b], psk(b))

        def setup_diff(i, j, ls):
            L = {}
            L["QT"] = sb("QT", [128, H_A, G], BF16, ls)
            L["onT"] = sb("onT", [128, H_A, G], BF16, ls)
            L["qb"] = sb("qb", [128, 512], BF16, ls)
            L["kb"] = [sb("kb%d" % q, [128, 128], BF16, ls) for q in range(2)]
            L["vb"] = [sb("vb%d" % q, [128, 128], BF16, ls) for q in range(2)]
            L["kT"] = [sb("kT%d" % q, [128, 128], BF16, ls) for q in range(2)]
            L["e1"] = [sb("e1%d" % q, [128, G], BF16, ls) for q in range(2)]
            L["e2"] = [sb("e2%d" % q, [128, G], BF16, ls) for q in range(2)]
            L["r1"] = sb("r1", [128, G], F32, ls); L["r2"] = sb("r2", [128, G], F32, ls)
            L["o1"] = sb("o1", [128, G], F32, ls); L["o2"] = sb("o2", [128, G], F32, ls)
            L["nlam"] = sb("nlam", [128, 1], F32, ls); L["gsub"] = sb("gsub", [128, 1], F32, ls)
            lam_init = 0.8 - 0.6 * math.exp(-0.3 * i)
            lv = sb("lv", [1, 4, 64], F32, ls); lp = sb("lp", [1, 2, 64], F32, ls); lsum = sb("lsum", [1, 2], F32, ls)
            lam1 = sb("lam1", [1, 1], F32, ls)
            P.dma("sp", lv[:], a_lambda[j:j + 1, :, :], (), ["lv"])
            P.tt("dve", lp[:], lv[:, 0:4:2, :], lv[:, 1:4:2, :], ALU.mult, ["lv"], ["lp"])
            P.op("dve", lambda e: e.tensor_reduce(out=lsum[:], in_=lp[:], axis=AX.X, op=ALU.add), ["lp"], ["lsum"])
            P.act(lsum[:], lsum[:], AF.Exp, ["lsum"], ["lsum"])
            P.tt("dve", lam1[:], lsum[:, 1:2], lsum[:, 0:1], ALU.subtract, ["lsum"], ["lam1"])
            P.ts("dve", lam1[:], lam1[:], -lam_init, None, ALU.add, None, ["lam1"], ["lam1"])
            P.mm(PS[3][:, 0:1], onesF[0:1, :], lam1[:], True, True, ["onesF", "lam1"], [psk(3)])
            P.cp("dve", L["nlam"][:], PS[3][:, 0:1], [psk(3)], ["nlam"])
            s = next_sl()
            P.dma("sp", SL[s][0:1, 0:128], g_a_sub[j:j + 1, :], (), [("sl", s)])
            P.tr(PS[3][:, 0:1], SL[s][0:1, 0:128], identF[0:1, 0:1], [("sl", s), "identF"], [psk(3)])
            P.ts("dve", L["gsub"][:], PS[3][:, 0:1], 1.0 - lam_init, None, ALU.mult, None, [psk(3)], ["gsub"])
            return L


        def setup_mlstm(i, j, ls):
            L = {}
            L["Cst"] = sb("Cst", [128, H_B, 257], F32, ls)
            L["Cb"] = sb("Cb", [128, H_B, 257], BF16, ls)
            L["mrow"] = sb("mrow", [4, 1], F32, ls)
            L["qT"] = sb("qT", [128, H_B, 64], BF16, ls)
            L["kT"] = sb("mkT", [128, H_B, 64], BF16, ls)
            L["ktm"] = sb("ktm", [64, 1, 512], F32, ls)
            L["Vaug"] = sb("Vaug", [64, 1, H_B, 257], BF16, ls)
            L["gsig"] = sb("gsig", [64, 1, 1024], F32, ls)
            L["hnT"] = sb("hnT", [128, NCH, G], BF16, ls)
            L["goutT"] = sb("goutT", [128, NCH], F32, ls)
            L["bg"] = sb("bg", [4, 2], F32, ls)
            L["tri"] = sb("tri", [64, 64], F32, ls)
            L["sel"] = sb("sel", [4, 256], F32, ls)
            L["one1"] = sb("one1", [128, 1], F32, ls)
            for nm in ("rI", "rF", "rA", "rE", "rL", "rB", "rU", "rCM", "rM", "ra", "rwr", "remt"):
                L[nm] = sb("ml_" + nm, [4, 64], F32, ls)
            L["nML"] = sb("nML", [4, 1], F32, ls)
            L["dg"] = sb("dg", [4, 4], F32, ls)
            L["cols"] = sb("cols", [64, 16], F32, ls)
            L["dbc"] = sb("dbc", [128, 4], F32, ls)
            L["z"] = sb("mz", [64, 64], F32, ls)
            L["W"] = sb("mW", [64, 64], F32, ls)
            L["Dm"] = sb("mDm", [64, 64], BF16, ls)
            L["itmp"] = sb("itmp", [64, 257], F32, ls)
            L["ND"] = sb("ND", [64, 257], F32, ls)
            L["sc"] = sb("msc", [64, 8], F32, ls)
            L["hn"] = sb("mhn", [64, 1024], BF16, ls)
            L["kw"] = sb("mkw", [64, 128], BF16, ls)
            L["nrow"] = sb("nrow", [4, 128], F32, ls)
            load_rows_T(L["goutT"][:], "goutT", g_b_out.rearrange("i (c p) -> (i c) p", p=128), NCH)
            P.dma("sp", L["bg"][:, 0:1], b_b_gates[0:1, 0:4].rearrange("o f -> f o"), (), ["bg"])
            P.dma("sp", L["bg"][:, 1:2], b_b_gates[0:1, 4:8].rearrange("o f -> f o"), (), ["bg"])
            P.dma("sp", L["tri"][:], k_tri[:, :], (), ["tri"])
            P.dma("sp", L["sel"][:], k_sel[:, :], (), ["sel"])
            P.memset("dve", L["one1"][:], 1.0, ["one1"])
            return L

        def mlstm_state_init(L, seq):
            Cst = L["Cst"]
            if seq is None:
                P.memset("dve", Cst[:], 0.0, ["Cst"])
                P.memset("dve", L["mrow"][:], 0.0, ["mrow"])
            else:
                P.dma("sp", Cst[:, :, 0:256], sb_c[seq].rearrange("h d v -> d h v"), (), ["Cst"])
                load_rows_T(Cst[:, :, 256], "Cst", sb_n[seq], 4)
                P.dma("sp", L["mrow"][:], sb_m[seq:seq + 1, :].rearrange("o f -> f o"), (), ["mrow"])
            P.cp("act", L["Cb"][:], Cst[:], ["Cst"], ["Cb"])

        def mlstm_state_out(L, dc, dn, dm):
            Cst = L["Cst"]
            P.dma("sp", dc.rearrange("h d v -> d h v"), Cst[:, :, 0:256], ["Cst"], (), semkey="Cst_o")
            P.tr(PS[3][0:4, 0:128], Cst[:, :, 256], identF[:, :], ["Cst", "identF"], [psk(3)])
            P.cp("dve", L["nrow"][:], PS[3][0:4, 0:128], [psk(3)], ["nrow"])
            P.dma("sp", dn, L["nrow"][:], ["nrow"], (), semkey="nrow_o")
            P.dma("sp", dm, L["mrow"][:], ["mrow"], (), semkey="mrow_o")

        def mlstm_unit(gr, L, u0, nu, Lc):
            nchk = nu // Lc
            wj = w_b_in[0]
            qT, kT, ktm, Vaug, gsig = L["qT"], L["kT"], L["ktm"], L["Vaug"], L["gsig"]
            kscale = DQK_B ** -0.5
            for blk, dst, scl in ((0, qT, 1.0), (1, kT, kscale)):
                wt, wk = wload(wview(wj, blk * 512, 512), NCH, 512)
                for h in range(H_B):
                    b = next_ps()
                    for k in range(NCH):
                        P.mm(PS[b][:, 0:nu], wt[:, k, h * 128:(h + 1) * 128], hT[:, k, u0:u0 + nu], k == 0, k == NCH - 1,
                             [wk, "hT"], [psk(b)])
                    P.act(dst[:, h, 0:nu], PS[b][:, 0:nu], AF.Copy, [psk(b)], ["mqk"], scale=scl)
            for blk in range(1, 6):
                wt, wk = wload(wview(wj, blk * 512, 512), NCH, 512)
                for ck in range(nchk):
                    b = next_ps()
                    cc0 = u0 + ck * Lc
                    for k in range(NCH):
                        P.mm(PS[b][0:Lc, :], hT[:, k, cc0:cc0 + Lc], wt[:, k, :], k == 0, k == NCH - 1, ["hT", wk], [psk(b)])
                    if blk == 1:
                        P.act(ktm[0:Lc, ck, :], PS[b][0:Lc, :], AF.Copy, [psk(b)], ["ktm"], scale=kscale)
                    elif blk < 4:
                        hh = (blk - 2) * 2
                        P.cp("dve", Vaug[0:Lc, ck, hh:hh + 2, 0:256], PS[b][0:Lc, :].rearrange("p (h v) -> p h v", v=256),
                             [psk(b)], ["Vaug"])
                    else:
                        o0 = (blk - 4) * 512
                        P.act(gsig[0:Lc, ck, o0:o0 + 512], PS[b][0:Lc, :], AF.Sigmoid, [psk(b)], ["gsig"])
            for ck in range(nchk):
                P.memset("pool", Vaug[0:Lc, ck, :, 256:257], 1.0, ["Vaug"])
            wt, wk = wload(wview(wj, 3072, 8), NCH, 8)
            gb = 3
            for k in range(NCH):
                P.mm(PS[gb][0:4, 0:nu], wt[:, k, 0:4], hT[:, k, u0:u0 + nu], k == 0, k == NCH - 1, [wk, "hT"], [psk(gb)])
            for k in range(NCH):
                P.mm(PS[gb][0:4, 128:128 + nu], wt[:, k, 4:8], hT[:, k, u0:u0 + nu], k == 0, k == NCH - 1, [wk, "hT"], [psk(gb)])
            rI, rF, rA, rE, rL, rB, rU, rCM, rM, ra, rwr, remt = (L[x] for x in
                ("rI", "rF", "rA", "rE", "rL", "rB", "rU", "rCM", "rM", "ra", "rwr", "remt"))
            bg = L["bg"]
            P.act(rI[:, 0:nu], PS[gb][0:4, 0:nu], AF.Identity, [psk(gb), "bg"], ["rI"], bias=bg[:, 0:1], scale=1.0)
            P.act(rF[:, 0:nu], PS[gb][0:4, 128:128 + nu], AF.Identity, [psk(gb), "bg"], ["rF"], bias=bg[:, 1:2], scale=1.0)
            P.act(rA[:, 0:nu], rF[:, 0:nu], AF.Abs, ["rF"], ["rA"])
            P.act(rE[:, 0:nu], rA[:, 0:nu], AF.Exp, ["rA"], ["rE"], scale=-1.0)
            P.act(rL[:, 0:nu], rE[:, 0:nu], AF.Ln, ["rE", "one1"], ["rL"], bias=L["one1"][0:4, 0:1], scale=1.0)
            P.ts("dve", rA[:, 0:nu], rF[:, 0:nu], 0.0, None, ALU.min, None, ["rF", "rA"], ["rA"])
            P.tt("dve", rF[:, 0:nu], rA[:, 0:nu], rL[:, 0:nu], ALU.subtract, ["rA", "rL"], ["rF"])
            P.memset("dve", rE[:, 0:nu], 0.0, ["rE"])
            mrow = L["mrow"]
            for ck in range(nchk):
                a0, a1 = ck * Lc, (ck + 1) * Lc
                P.op("dve", lambda e: e.tensor_tensor_scan(out=rB[:, a0:a1], data0=rF[:, a0:a1], data1=rE[:, a0:a1],
                                                           initial=0.0, op0=ALU.add, op1=ALU.add), ["rF", "rE"], ["rB"])
                P.tt("dve", rU[:, a0:a1], rI[:, a0:a1], rB[:, a0:a1], ALU.subtract, ["rI", "rB"], ["rU"])
                P.op("dve", lambda e: e.tensor_tensor_scan(out=rCM[:, a0:a1], data0=rU[:, a0:a1], data1=rU[:, a0:a1],
                                                           initial=-1e30, op0=ALU.max, op1=ALU.max), ["rU"], ["rCM"])
                P.ts("dve", rM[:, a0:a1], rCM[:, a0:a1], mrow[:, 0:1], None, ALU.max, None, ["rCM", "mrow"], ["rM"])
                P.act(ra[:, a0:a1], rM[:, a0:a1], AF.Exp, ["rM", "mrow"], ["ra"], bias=mrow[:, 0:1], scale=-1.0)
                P.ts("dve", L["nML"][:], rM[:, a1 - 1:a1], -1.0, None, ALU.mult, None, ["rM"], ["nML"])
                P.act(rwr[:, a0:a1], rU[:, a0:a1], AF.Exp, ["rU", "nML"], ["rwr"], bias=L["nML"][:, 0:1], scale=1.0)
                P.tt("dve", remt[:, a0:a1], rB[:, a0:a1], rM[:, a0:a1], ALU.add, ["rB", "rM"], ["remt"])
                P.act(remt[:, a0:a1], remt[:, a0:a1], AF.Exp, ["remt"], ["remt"], scale=-1.0)
                P.tt("dve", mrow[:], rB[:, a1 - 1:a1], rM[:, a1 - 1:a1], ALU.add, ["rB", "rM", "ra"], ["mrow"])
                cb_ = 3
                for qi, rr in enumerate((rU, ra, rwr, remt)):
                    P.tr(PS[cb_][0:Lc, 256 + qi * 4:256 + qi * 4 + 4], rr[:, a0:a1], identF[0:4, 0:4],
                         ["rU", "ra", "rwr", "remt", "identF"], [psk(cb_)])
                cols = L["cols"]
                P.cp("dve", cols[0:Lc, :], PS[cb_][0:Lc, 256:272], [psk(cb_)], ["cols"])
                P.ts("dve", L["dg"][:], identF[0:4, 0:4], ra[:, a1 - 1:a1], None, ALU.mult, None, ["identF", "ra"], ["dg"])
                P.mm(PS[cb_][:, 280:284], onesF[0:4, :], L["dg"][:], True, True, ["onesF", "dg"], [psk(cb_)])
                P.cp("dve", L["dbc"][:], PS[cb_][:, 280:284], [psk(cb_)], ["dbc"])
                cq0 = ck * Lc
                for h in range(H_B):
                    P.mm(PS[4][0:Lc, 0:Lc], kT[:, h, cq0:cq0 + Lc], qT[:, h, cq0:cq0 + Lc], True, True, ["mqk"], [psk(4)])
                    P.mm(PS[4][0:Lc, 64:64 + Lc], L["sel"][:, h * 64:h * 64 + Lc], rM[:, a0:a1], True, True, ["sel", "rM"], [psk(4)])
                    P.ts("dve", L["z"][0:Lc, 0:Lc], PS[4][0:Lc, 64:64 + Lc], cols[0:Lc, h:h + 1], 0.0, ALU.subtract, ALU.max,
                         [psk(4), "cols"], ["mz"])
                    P.act(L["W"][0:Lc, 0:Lc], L["z"][0:Lc, 0:Lc], AF.Exp, ["mz"], ["mW"], scale=-1.0)
                    P.tt("pool", L["W"][0:Lc, 0:Lc], L["W"][0:Lc, 0:Lc], L["tri"][0:Lc, 0:Lc], ALU.mult, ["mW", "tri"], ["mW"])
                    P.tt("dve", L["Dm"][0:Lc, 0:Lc], PS[4][0:Lc, 0:Lc], L["W"][0:Lc, 0:Lc], ALU.mult, [psk(4), "mW"], ["mDm"])
                    P.mm(PS[5][0:Lc, 0:257], L["Dm"][0:Lc, 0:Lc], Vaug[0:Lc, ck, h, :], True, True, ["mDm", "Vaug"], [psk(5)])
                    P.mm(PS[6][0:Lc, 0:257], qT[:, h, cq0:cq0 + Lc], L["Cb"][:, h, :], True, True, ["mqk", "Cb"], [psk(6)])
                    P.act(L["itmp"][0:Lc, :], PS[6][0:Lc, 0:257], AF.Copy, [psk(6), "cols"], ["itmp"], scale=cols[0:Lc, 4 + h:5 + h])
                    P.tt("dve", L["ND"][0:Lc, :], L["itmp"][0:Lc, :], PS[5][0:Lc, 0:257], ALU.add, ["itmp", psk(5)], ["ND"])
                    sc = L["sc"]
                    P.act(sc[0:Lc, 6:7], L["ND"][0:Lc, 256:257], AF.Abs, ["ND"], ["msc6"])
                    P.tt("dve", sc[0:Lc, 0:1], sc[0:Lc, 6:7], cols[0:Lc, 12 + h:13 + h], ALU.max, ["msc6", "cols"], ["msc"])
                    P.op("dve", lambda e: e.reciprocal(out=sc[0:Lc, 1:2], in_=sc[0:Lc, 0:1]), ["msc"], ["msc"])
                    P.act(L["itmp"][0:Lc, 0:256], L["ND"][0:Lc, 0:256], AF.Square, ["ND", "msc"], ["itmp", "msc2"],
                          scale=sc[0:Lc, 1:2], accum=sc[0:Lc, 2:3])
                    P.act(sc[0:Lc, 3:4], sc[0:Lc, 2:3], AF.Sqrt, ["msc2", "epsT"], ["msc3"], bias=epsT[0:Lc, 0:1], scale=1.0 / 256)
                    P.op("dve", lambda e: e.reciprocal(out=sc[0:Lc, 4:5], in_=sc[0:Lc, 3:4]), ["msc3"], ["msc4"])
                    P.tt("dve", sc[0:Lc, 5:6], sc[0:Lc, 4:5], sc[0:Lc, 1:2], ALU.mult, ["msc4", "msc"], ["msc5"])
                    P.stt(L["hn"][0:Lc, h * 256:(h + 1) * 256], L["ND"][0:Lc, 0:256], sc[0:Lc, 5:6],
                          gsig[0:Lc, ck, h * 256:(h + 1) * 256], ALU.mult, ALU.mult, ["ND", "msc5", "gsig"], [("hn", h)])
                    P.ts("pool", L["kw"][0:Lc, :], ktm[0:Lc, ck, h * 128:(h + 1) * 128], cols[0:Lc, 8 + h:9 + h], None,
                         ALU.mult, None, ["ktm", "cols"], ["mkw"])
                    P.mm(PS[7][:, 0:257], L["kw"][0:Lc, :], Vaug[0:Lc, ck, h, :], True, True, ["mkw", "Vaug"], [psk(7)])
                    P.stt(L["Cst"][:, h, :], L["Cst"][:, h, :], L["dbc"][:, h:h + 1], PS[7][:, 0:257], ALU.mult, ALU.add,
                          ["Cst", "dbc", psk(7)], ["Cst"])
                    P.cp("act", L["Cb"][:, h, :], L["Cst"][:, h, :], ["Cst"], ["Cb"])
                tb = 2
                for q in range(NCH):
                    P.tr(psb(tb)[:, q * 64:q * 64 + Lc], L["hn"][0:Lc, q * 128:(q + 1) * 128], identB[0:Lc, 0:Lc],
                         [("hn", q // 2), "identB"], [psk(tb)])
                P.tt("dve", L["hnT"][:, :, u0 + a0:u0 + a1], psb(tb)[:, 0:512].rearrange("p (q t) -> p q t", t=64)[:, :, 0:Lc],
                     L["goutT"][:, :].unsqueeze(2).broadcast_to([128, NCH, Lc]), ALU.mult, [psk(tb), "goutT"], ["hnT"])

        def mixer_mlstm(gr, i, j, L):
            n = gr.n
            norm_mod(gr, i, 1)
            if gr.kind == "p":
                if gr.g == 0:
                    mlstm_state_init(L, None)
                for c0 in range(0, n, 64):
                    mlstm_unit(gr, L, c0, 64, 64)
                if gr.g == TP // G - 1:
                    mlstm_state_out(L, bcp, bnp, bmp)
            else:
                for s_ in range(NS):
                    mlstm_state_init(L, s_)
                    mlstm_unit(gr, L, s_ * TS, TS, TS)
                    mlstm_state_out(L, bcs[s_], bns[s_], bms[s_])
            for ch in range(2):
                wt, wk = wload(wview(w_b_out[j], ch * 512, 512), NCH, 512)
                for cc in range(4):
                    c = ch * 4 + cc
                    b = next_ps()
                    for q in range(NCH):
                        P.mm(PS[b][:, 0:n], wt[:, q, cc * 128:(cc + 1) * 128], L["hnT"][:, q, 0:n], q == 0, q == NCH - 1,
                             [wk, "hnT"], [psk(b)])
                    resid_add(gr, i, 1, c, PS[b], psk(b))

        def setup_mla(i, j, ls):
            L = {}
            L["QTn"] = sb("QTn", [128, H_C, G], BF16, ls)
            L["QTr"] = sb("QTr", [64, H_C, G], BF16, ls)
            L["onT"] = sb("c_onT", [128, H_C, G], BF16, ls)
            L["cq"] = sb("cq", [128, 3, G], F32, ls)
            L["cqn"] = sb("cqn", [128, 3, G], BF16, ls)
            L["gqT"] = sb("gqT", [128, 3], F32, ls)
            L["gkv"] = sb("gkv", [128, 256], F32, ls)
            L["rc"] = [sb("rc%d" % q, [128, 64], F32, ls) for q in range(2)]
            L["qb"] = sb("c_qb", [128, 384], BF16, ls)
            L["kvb"] = sb("kvb", [128, 256], BF16, ls)
            L["kvT"] = sb("kvT", [128, 2, 2, 128], BF16, ls)
            L["ob"] = [sb("c_ob%d" % q, [128, 512], BF16, ls) for q in range(2)]
            L["kb"] = [sb("c_kb%d" % q, [128, 128], BF16, ls) for q in range(2)]
            L["krb"] = [sb("c_krb%d" % q, [128, 64], BF16, ls) for q in range(2)]
            L["vb"] = [sb("c_vb%d" % q, [128, 128], BF16, ls) for q in range(2)]
            L["kT"] = [sb("c_kT%d" % q, [128, 128], BF16, ls) for q in range(2)]
            L["krT"] = [sb("c_krT%d" % q, [64, 128], BF16, ls) for q in range(2)]
            L["e"] = [sb("c_e%d" % q, [128, G], BF16, ls) for q in range(2)]
            L["r"] = sb("c_r", [128, G], F32, ls)
            L["sc"] = sb("c_sc", [128, 4], F32, ls)
            L["junk"] = sb("c_junk", [128, 256], F32, ls)
            L["n"] = {"rc": 0, "ob": 0}
            load_rows_T(L["gqT"][:], "gqT", g_c_q.rearrange("i (c p) -> (i c) p", p=128), 3)
            P.dma("sp", L["gkv"][:], g_c_kv[0, :].partition_broadcast(128), (), ["gkv"])
            return L

        def mla_up(L, kvT_ap, nt, dst_rows, dkey):
            for cb in range(4):
                wt, wk = wload(wview(w_c_ukv[0], cb * 512, 512), 2, 512)
                b = next_ps()
                for k in range(2):
                    P.mm(PS[b][0:nt, :], kvT_ap[:, k, 0:nt], wt[:, k, :], k == 0, k == 1, ["kvT", wk], [psk(b)])
                oi = L["n"]["ob"] % 2
                L["n"]["ob"] += 1
                P.cp("act", L["ob"][oi][0:nt, :], PS[b][0:nt, :], [psk(b)], [("cob", oi)])
                P.dma("sp", dst_rows[:, cb * 512:(cb + 1) * 512], L["ob"][oi][0:nt, :], [("cob", oi)], [dkey], semkey=("cobo", oi))

        def latent_T(L, src_bf_ap, nt, slot):
            tb = 2
            for k in range(2):
                P.tr(psb(tb)[:, k * 128:k * 128 + nt], src_bf_ap[:, k * 128:(k + 1) * 128], identB[0:nt, 0:nt],
                     ["kvb", "identB"], [psk(tb)])
            P.cp("dve", L["kvT"][:, slot, :, 0:nt], psb(tb)[:, 0:256].rearrange("p (k t) -> p k t", t=128)[:, :, 0:nt],
                 [psk(tb)], ["kvT"])

        def mixer_mla(gr, i, j, L):
            n = gr.n
            norm_mod(gr, i, 1)
            QTn, QTr, onT = L["QTn"], L["QTr"], L["onT"]
            kvdst = ckvp if gr.kind == "p" else ckvs
            krdst = ckrp if gr.kind == "p" else ckrs
            ckey = ("ckv", gr.kind, gr.g)
            wt, wk = wload(wview(w_c_dq[0], 0, Q_LORA), NCH, Q_LORA)
            sb_ = 3
            for f in range(3):
                b = next_ps()
                for k in range(NCH):
                    P.mm(PS[b][:, 0:n], wt[:, k, f * 128:(f + 1) * 128], hT[:, k, 0:n], k == 0, k == NCH - 1, [wk, "hT"], [psk(b)])
                P.cp("dve", L["cq"][:, f, 0:n], PS[b][:, 0:n], [psk(b)], ["cq"])
                si = st["sq"] % 2
                st["sq"] += 1
                P.act(sqr[si][:, 0:n], PS[b][:, 0:n], AF.Square, [psk(b)], [("sq", si)])
                P.mm(PS[sb_][:, 0:n], onesB[:], sqr[si][:, 0:n], f == 0, f == 2, [("sq", si), "onesB"], [psk(sb_)])
            P.act(L["r"][:, 0:n], PS[sb_][:, 0:n], AF.Sqrt, [psk(sb_), "epsT"], ["c_r"], bias=epsT[:, 0:1], scale=1.0 / Q_LORA)
            P.op("dve", lambda e: e.reciprocal(out=L["r"][:, 0:n], in_=L["r"][:, 0:n]), ["c_r"], ["c_r"])
            for f in range(3):
                P.stt(L["cqn"][:, f, 0:n], L["cq"][:, f, 0:n], L["gqT"][:, f:f + 1], L["r"][:, 0:n], ALU.mult, ALU.mult,
                      ["cq", "gqT", "c_r"], ["cqn"])
            for cb in range(4):
                wt, wk = wload(wview(w_c_uq[0], cb * 384, 384), 3, 384)
                for (c0, nt, ti) in gr.tiles:
                    b = next_ps()
                    for k in range(3):
                        P.mm(PS[b][0:nt, 0:384], L["cqn"][:, k, c0:c0 + nt], wt[:, k, 0:384], k == 0, k == 2, ["cqn", wk], [psk(b)])
                    s = next_sl(); sk = ("sl", s)
                    P.cp("act", SL[s][0:nt, 0:384], PS[b][0:nt, 0:384], [psk(b)], [sk])
                    ri = L["n"]["rc"] % 2
                    L["n"]["rc"] += 1
                    row0 = (gr.c0 + c0) if gr.kind == "p" else TP
                    P.dma("sp", L["rc"][ri][0:nt, :], rope_c[row0:row0 + nt, :], (), [("rc", ri)])
                    P.res["ropetab"] = P.res[("rc", ri)]
                    rope_slab(SL[s][0:nt, 0:384].rearrange("p (s d) -> p s d", d=192), sk, nt, L["rc"][ri][0:nt, 0:32],
                              L["rc"][ri][0:nt, 32:64], 32, 128, 2)
                    P.res[("rc", ri)] = P.res["ropetab"]
                    P.cp("act", L["qb"][0:nt, :], SL[s][0:nt, 0:384], [sk], ["c_qb"])
                    tb = 2
                    for hh in range(2):
                        P.tr(psb(tb)[:, hh * 128:hh * 128 + nt], L["qb"][0:nt, hh * 192:hh * 192 + 128], identB[0:nt, 0:nt],
                             ["c_qb", "identB"], [psk(tb)])
                        P.tr(psb(tb)[0:64, 256 + hh * 128:256 + hh * 128 + nt], L["qb"][0:nt, hh * 192 + 128:hh * 192 + 192],
                             identB[0:nt, 0:nt], ["c_qb", "identB"], [psk(tb)])
                    P.cp("dve", QTn[:, cb * 2:cb * 2 + 2, c0:c0 + nt],
                         psb(tb)[:, 0:256].rearrange("p (q t) -> p q t", t=128)[:, :, 0:nt], [psk(tb)], ["QTn"])
                    P.cp("dve", QTr[:, cb * 2:cb * 2 + 2, c0:c0 + nt],
                         psb(tb)[0:64, 256:512].rearrange("p (q t) -> p q t", t=128)[:, :, 0:nt], [psk(tb)], ["QTr"])
            for tix, (c0, nt, ti) in enumerate(gr.tiles):
                wt, wk = wload(wview(w_c_dkv[0], 0, 320), NCH, 320)
                b = next_ps()
                for k in range(NCH):
                    P.mm(PS[b][0:nt, 0:320], hT[:, k, c0:c0 + nt], wt[:, k, 0:320], k == 0, k == NCH - 1, ["hT", wk], [psk(b)])
                s = next_sl(); sk = ("sl", s)
                P.cp("act", SL[s][0:nt, 0:320], PS[b][0:nt, 0:320], [psk(b)], [sk])
                sc = L["sc"]
                P.act(L["junk"][0:nt, :], SL[s][0:nt, 0:256], AF.Square, [sk], ["c_junk", "c_sc"], accum=sc[0:nt, 0:1])
                P.act(sc[0:nt, 1:2], sc[0:nt, 0:1], AF.Sqrt, ["c_sc", "epsT"], ["c_sc1"], bias=epsT[0:nt, 0:1], scale=1.0 / KV_LORA)
                P.op("dve", lambda e: e.reciprocal(out=sc[0:nt, 2:3], in_=sc[0:nt, 1:2]), ["c_sc1"], ["c_sc2"])
                P.stt(SL[s][0:nt, 0:256], SL[s][0:nt, 0:256], sc[0:nt, 2:3], L["gkv"][0:nt, :], ALU.mult, ALU.mult,
                      [sk, "c_sc2", "gkv"], [sk])
                ri = L["n"]["rc"] % 2
                L["n"]["rc"] += 1
                row0 = (gr.c0 + c0) if gr.kind == "p" else TP
                P.dma("sp", L["rc"][ri][0:nt, :], rope_c[row0:row0 + nt, :], (), [("rc", ri)])
                P.res["ropetab"] = P.res[("rc", ri)]
                rope_slab(SL[s][0:nt, 256:320].rearrange("p (s d) -> p s d", d=64), sk, nt, L["rc"][ri][0:nt, 0:32],
                          L["rc"][ri][0:nt, 32:64], 32, 0, 1)
                P.res[("rc", ri)] = P.res["ropetab"]
                P.dma("sp", kvdst[gr.c0 + c0:gr.c0 + c0 + nt, :], SL[s][0:nt, 0:256], [sk], [ckey], semkey=("slo", s))
                P.dma("sp", krdst[gr.c0 + c0:gr.c0 + c0 + nt, :], SL[s][0:nt, 256:320], [sk], [ckey], semkey=("slo", s))
                P.cp("act", L["kvb"][0:nt, :], SL[s][0:nt, 0:256], [sk], ["kvb"])
                latent_T(L, L["kvb"][0:nt, :], nt, 0)
                if gr.kind == "p":
                    mla_up(L, L["kvT"][:, 0, :, :], nt, mla_kv_p[gr.c0 + c0:gr.c0 + c0 + nt, :], ckey)
                else:
                    for cb in range(4):
                        wt2, wk2 = wload(wview(w_c_ukv[0], cb * 512, 512), 2, 512)
                        b2 = next_ps()
                        for k in range(2):
                            P.mm(PS[b2][0:nt, :], L["kvT"][:, 0, k, 0:nt], wt2[:, k, :], k == 0, k == 1, ["kvT", wk2], [psk(b2)])
                        oi = L["n"]["ob"] % 2
                        L["n"]["ob"] += 1
                        P.cp("act", L["ob"][oi][0:nt, :], PS[b2][0:nt, :], [psk(b2)], [("cob", oi)])
                        for s_ in range(NS):
                            P.dma("sp", mla_kv_s[s_, PAST:PAST + TS, cb * 512:(cb + 1) * 512], L["ob"][oi][s_ * TS:(s_ + 1) * TS, :],
                                  [("cob", oi)], [ckey], semkey=("cobo", oi))
            if gr.kind == "s":
                for s_ in range(NS):
                    for pt in range(NPT):
                        P.dma("pool", L["kvb"][:, :], cc_kv[s_, pt * 128:(pt + 1) * 128, :], (), ["kvb"])
                        latent_T(L, L["kvb"][:, :], 128, 1)
                        mla_up(L, L["kvT"][:, 1, :, :], 128, mla_kv_s[s_, pt * 128:(pt + 1) * 128, :], ("cpast", s_))
            scale = 192 ** -0.5
            for si, (r, a, nq) in enumerate(gr.segs):
                if gr.kind == "p":
                    nkt = (gr.c0 + n) // 128
                    ktl = []
                    for kt in range(nkt):
                        gk = (kt * 128) // G
                        ktl.append((mla_kv_p[kt * 128:(kt + 1) * 128, :], ckrp[kt * 128:(kt + 1) * 128, :], 128,
                                    kt * 128 - gr.c0, [("ckv", "p", gk)]))
                else:
                    s_ = si
                    ktl = []
                    for kt in range(NPT):
                        ktl.append((mla_kv_s[s_, kt * 128:(kt + 1) * 128, :], cc_kr[s_, kt * 128:(kt + 1) * 128, :], 128, -1,
                                    [("cpast", s_)]))
                    ktl.append((mla_kv_s[s_, PAST:PAST + TS, :], ckrs[s_ * TS:(s_ + 1) * TS, :], TS, -1, [ckey]))
                for hd in range(H_C):
                    first = True
                    for kti, (kvsrc, krsrc, nk, dcol, deps) in enumerate(ktl):
                        last = kti == len(ktl) - 1
                        cq0 = max(dcol, 0)
                        q2 = kti % 2
                        kb, krb, vb, kT, krT, e = L["kb"][q2], L["krb"][q2], L["vb"][q2], L["kT"][q2], L["krT"][q2], L["e"][q2]
                        P.dma("sp", kb[0:nk, :], kvsrc[:, hd * 256:hd * 256 + 128], deps, [("ckb", q2)])
                        P.dma("sp", vb[0:nk, :], kvsrc[:, hd * 256 + 128:hd * 256 + 256], deps, [("cvb", q2)])
                        P.dma("pool", krb[0:nk, :], krsrc, deps, [("ckrb", q2)])
                        tb = 2
                        P.tr(psb(tb)[:, 0:nk], kb[0:nk, :], identB[0:nk, 0:nk], [("ckb", q2), "identB"], [psk(tb)])
                        P.tr(psb(tb)[0:64, 128:128 + nk], krb[0:nk, :], identB[0:nk, 0:nk], [("ckrb", q2), "identB"], [psk(tb)])
                        P.cp("dve", kT[:, 0:nk], psb(tb)[:, 0:nk], [psk(tb)], [("ckT", q2)])
                        P.cp("dve", krT[:, 0:nk], psb(tb)[0:64, 128:128 + nk], [psk(tb)], [("ckrT", q2)])
                        qa, qn = a + cq0, nq - cq0
                        P.mm(PS[0][0:nk, 0:qn], kT[:, 0:nk], QTn[:, hd, qa:qa + qn], True, False, [("ckT", q2), "QTn"], [psk(0)])
                        P.mm(PS[0][0:nk, 0:qn], krT[0:64, 0:nk], QTr[0:64, hd, qa:qa + qn], False, True, [("ckrT", q2), "QTr"], [psk(0)])
                        P.act(e[0:nk, 0:qn], PS[0][0:nk, 0:qn], AF.Exp, [psk(0)], [("ce", q2)], scale=scale)
                        if dcol >= 0:
                            P.memset("pool", e[64:128, 0:64], 0.0, [("ce", q2)])
                        P.mm(PS[4][:, cq0:nq], vb[0:nk, :], e[0:nk, 0:qn], first, last, [("cvb", q2), ("ce", q2)], [psk(4)])
                        P.mm(PS[6][:, cq0:nq], onesB[0:nk, :], e[0:nk, 0:qn], first, last, ["onesB", ("ce", q2)], [psk(6)])
                        first = False
                    P.op("dve", lambda e_: e_.reciprocal(out=L["r"][:, 0:nq], in_=PS[6][:, 0:nq]), [psk(6)], ["c_r"])
                    P.tt("dve", onT[:, hd, a:a + nq], PS[4][:, 0:nq], L["r"][:, 0:nq], ALU.mult, [psk(4), "c_r"], ["c_onT"])
            for ch in range(2):
                wt, wk = wload(wview(w_c_o[0], ch * 512, 512), NCH, 512)
                for cc in range(4):
                    c = ch * 4 + cc
                    b = next_ps()
                    for hd in range(H_C):
                        P.mm(PS[b][:, 0:n], wt[:, hd, cc * 128:(cc + 1) * 128], onT[:, hd, 0:n], hd == 0, hd == H_C - 1,
                             [wk, "c_onT"], [psk(b)])
                    resid_add(gr, i, 1, c, PS[b], psk(b))

        for i in cfg.layers:
            kind, j = i % 3, i // 3
            P.barrier()
            with ExitStack() as ls:
                if kind == 0:
                    L = setup_diff(i, j, ls)
                    for gr in groups:
                        mixer_diff(gr, i, j, L)
                        ffn(gr, i)
                elif kind == 1:
                    L = setup_mlstm(i, j, ls)
                    for gr in groups:
                        mixer_mlstm(gr, i, j, L)
                        ffn(gr, i)
                else:
                    L = setup_mla(i, j, ls)
                    for gr in groups:
                        mixer_mla(gr, i, j, L)
                        ffn(gr, i)
                P.barrier()

        P.barrier()
        hT_f = sb("hT_f", [128, NCH, G])
        for gr in groups:
            final_out(gr, y_p if gr.kind == "p" else y_s)
        P.finish()
        print("ops", P.nops, "waits", P.nwait, "dma sems", P.nsem, "cnt", P.cnt, flush=True)
    return nc


def rope_table(pos, rot):
    half = rot // 2
    inv = np.power(np.float32(ROPE_THETA), -np.arange(half, dtype=np.float32) * np.float32(2.0 / rot)).astype(np.float32)
    ang = pos.astype(np.float32)[:, None] * inv[None, :]
    return np.concatenate([np.cos(ang), np.sin(ang)], axis=1).astype(np.float32)


def host_consts(cfg):
    tri = (np.arange(64)[:, None] <= np.arange(64)[None, :]).astype(np.float32)
    sel = np.zeros((4, 4, 64), np.float32)
    for h in range(4):
        sel[h, h, :] = 1.0
    pos = np.concatenate([np.arange(cfg.TP), np.tile(cfg.PAST + np.arange(cfg.TS), cfg.NS)])
    return {
        "k_ident": np.eye(128, dtype=np.float32),
        "k_tri": tri,
        "k_sel": sel.reshape(4, 256),
        "rope_a": rope_table(pos, 16),
        "rope_c": rope_table(pos, 64),
    }


def make_in_maps(cfg, inp, n_cores=8):
    f = lambda a: np.ascontiguousarray(np.asarray(a, dtype=np.float32))
    NS, TS, TP, PAST = cfg.NS, cfg.TS, cfg.TP, cfg.PAST
    consts = host_consts(cfg)
    shared = {k: f(inp[k]) for k in ("w_ada", "b_ada", "g_norm1", "g_norm2", "w_a_qkv", "a_lambda", "g_a_sub", "w_a_o",
                                     "w_b_in", "b_b_gates", "g_b_out", "w_b_out", "w_c_dq", "g_c_q", "w_c_uq", "w_c_dkv",
                                     "g_c_kv", "w_c_ukv", "w_c_o", "w_ffn_in", "w_ffn_out")}
    shared["g_final"] = f(inp["g_final"]).reshape(1, D)
    shared.update(consts)
    nb = inp["x_prompt"].shape[0]
    maps = []
    for c in range(n_cores):
        b = c % nb
        ss = slice(NS * c, NS * c + NS)
        m = dict(shared)
        m["x_p"] = f(inp["x_prompt"][b])
        m["x_s"] = f(inp["x_sample"][ss]).reshape(NS * TS, D)
        m["c_in"] = f(np.concatenate([np.asarray(inp["c_prompt"])[b:b + 1], np.asarray(inp["c_sample"])[ss]], axis=0))
        m["ca_k"] = f(inp["cache_a_k"][:, ss]).reshape(2, NS, PAST, D)
        m["ca_v"] = f(inp["cache_a_v"][:, ss]).reshape(2, NS, PAST, D)
        m["sb_c"] = f(inp["state_b_c"][0, ss])
        m["sb_n"] = f(inp["state_b_n"][0, ss])
        m["sb_m"] = f(inp["state_b_m"][0, ss])
        m["cc_kv"] = f(inp["cache_c_kv"][0, ss])
        m["cc_kr"] = f(inp["cache_c_kr"][0, ss])
        maps.append(m)
    return maps


_NC_CACHE = {}


def kernel(**inputs):
    cfg = Cfg()
    if "nc" not in _NC_CACHE:
        _NC_CACHE["nc"] = build(cfg)
    nc = _NC_CACHE["nc"]
    maps = make_in_maps(cfg, inputs)
    res = run_bass_kernel_spmd(nc, maps, core_ids=list(range(8))).results
    NS, TS, TP = cfg.NS, cfg.TS, cfg.TP
    B = 4

    def pst(name, cores, shape=None):
        a = np.stack([np.asarray(res[c][name], dtype=np.float32) for c in cores], axis=0)
        return a

    pc = list(range(B))
    ac = list(range(8))
    y_prompt = pst("y_p", pc)
    y_sample = pst("y_s", ac).reshape(8 * NS, TS, D)
    a_k_p = np.moveaxis(pst("akp", pc), 0, 1).reshape(2, B, TP, H_A, 128)
    a_v_p = np.moveaxis(pst("avp", pc), 0, 1).reshape(2, B, TP, H_A, 128)
    b_c_p = pst("bcp", pc)[None]
    b_n_p = pst("bnp", pc)[None]
    b_m_p = pst("bmp", pc).reshape(1, B, H_B)
    c_kv_p = pst("ckvp", pc)[None]
    c_kr_p = pst("ckrp", pc)[None]
    a_k_s = np.moveaxis(pst("aks", ac).reshape(8, 2, NS, TS, D), 1, 0).reshape(2, 8 * NS, TS, H_A, 128)
    a_v_s = np.moveaxis(pst("avs", ac).reshape(8, 2, NS, TS, D), 1, 0).reshape(2, 8 * NS, TS, H_A, 128)
    b_c_s = pst("bcs", ac).reshape(1, 8 * NS, H_B, DQK_B, DV_B)
    b_n_s = pst("bns", ac).reshape(1, 8 * NS, H_B, DQK_B)
    b_m_s = pst("bms", ac).reshape(1, 8 * NS, H_B)
    c_kv_s = pst("ckvs", ac).reshape(1, 8 * NS, TS, KV_LORA)
    c_kr_s = pst("ckrs", ac).reshape(1, 8 * NS, TS, 64)
    outs = (y_prompt, y_sample, a_k_p, a_v_p, b_c_p, b_n_p, b_m_p, c_kv_p, c_kr_p,
            a_k_s, a_v_s, b_c_s, b_n_s, b_m_s, c_kv_s, c_kr_s)
    return tuple(np.ascontiguousarray(o, dtype=np.float32) for o in outs)
```

```python
import math
from contextlib import ExitStack
import numpy as np
import concourse.bass as bass
import concourse.mybir as mybir
from concourse.bass_utils import run_bass_kernel_spmd

F32 = mybir.dt.float32
BF16 = mybir.dt.bfloat16
AF = mybir.ActivationFunctionType
ALU = mybir.AluOpType
AX = mybir.AxisListType

D = 1024
NCH = 8
EPS = 1e-6
ROPE_THETA = 500000.0
DEPTH = 4
H_A = 8
H_B = 4
DQK_B = 128
DV_B = 256
H_C = 8
Q_LORA = 384
KV_LORA = 256
D_FF = 2816
NFF = 22
N_B_IN = 3080


class Cfg:
    def __init__(self, TP=4096, G=256, PAST=2048, TS=32, NS=2, layers=(0, 1, 2, 3)):
        self.TP, self.G, self.PAST, self.TS, self.NS = TP, G, PAST, TS, NS
        self.layers = tuple(layers)


class Prog:
    ENGS = ("pe", "act", "dve", "pool", "sp")

    def __init__(self, nc, es):
        self.nc, self.es = nc, es
        self.eobj = {"pe": nc.tensor, "act": nc.scalar, "dve": nc.vector, "pool": nc.gpsimd, "sp": nc.sync}
        self.cnt = {e: 0 for e in self.ENGS}
        self.esem = {e: es.enter_context(nc.semaphore("s_" + e)) for e in self.ENGS}
        self.seen = {e: {} for e in self.ENGS}
        self.res = {}
        self.dsem = {}
        self.nsem = 0
        self.nwait = 0
        self.nops = 0
        self.trace = {e: [] for e in self.ENGS}

    def _waits(self, eng, r, w):
        toks = []
        for k in r:
            st = self.res.get(k)
            if st is not None and st[0] is not None:
                toks.append(st[0])
        for k in w:
            st = self.res.get(k)
            if st is not None:
                if st[0] is not None:
                    toks.append(st[0])
                toks.extend(st[1])
        e = self.eobj[eng]
        seen = self.seen[eng]
        for (name, sem, val, src) in toks:
            if src == eng and eng == "pe":
                continue
            if seen.get(name, 0) >= val:
                continue
            seen[name] = val
            e.wait_ge(sem, val)
            self.trace[eng].append(("w", name, val))
            self.nwait += 1

    def _commit(self, tok, r, w):
        for k in r:
            st = self.res.get(k)
            if st is None:
                st = [None, []]
                self.res[k] = st
            st[1].append(tok)
        for k in w:
            self.res[k] = [tok, []]

    def op(self, eng, fn, r=(), w=()):
        if eng != "pe":
            psr = [k for k in r if isinstance(k, tuple) and k[0] == "ps"]
            if psr:
                r = [k for k in r if k not in psr]
                w = list(w) + psr
        self._waits(eng, r, w)
        inst = fn(self.eobj[eng])
        self.cnt[eng] += 1
        inst.then_inc(self.esem[eng], 1)
        self.trace[eng].append(("i", "s_" + eng, 1))
        tok = ("s_" + eng, self.esem[eng], self.cnt[eng], eng)
        self._commit(tok, r, w)
        self.nops += 1

    def dma(self, q, out, in_, r=(), w=(), semkey=None, acc=False):
        self._waits(q, r, () if acc else w)
        if semkey is None:
            semkey = w[0] if len(w) else r[0]
        if semkey not in self.dsem:
            self.dsem[semkey] = [self.es.enter_context(self.nc.semaphore("d%d" % self.nsem)), 0]
            self.nsem += 1
        ds = self.dsem[semkey]
        inst = self.eobj[q].dma_start(out=out, in_=in_)
        ds[1] += 16
        inst.then_inc(ds[0], 16)
        self.trace[q].append(("i", "d" + str(semkey), 16))
        tok = ("d" + str(semkey), ds[0], ds[1], None)
        self._commit(tok, r, w)
        self.nops += 1

    def barrier(self):
        for e in self.ENGS:
            eo = self.eobj[e]
            for e2 in self.ENGS:
                if e2 != e and self.cnt[e2] > self.seen[e].get("s_" + e2, 0):
                    eo.wait_ge(self.esem[e2], self.cnt[e2])
                    self.trace[e].append(("w", "s_" + e2, self.cnt[e2]))
                    self.seen[e]["s_" + e2] = self.cnt[e2]
            for k, ds in self.dsem.items():
                nm = "d" + str(k)
                if ds[1] > self.seen[e].get(nm, 0):
                    eo.wait_ge(ds[0], ds[1])
                    self.trace[e].append(("w", nm, ds[1]))
                    self.seen[e][nm] = ds[1]

    def finish(self):
        eo = self.eobj["sp"]
        for k, ds in self.dsem.items():
            eo.wait_ge(ds[0], ds[1])
            self.trace["sp"].append(("w", "d" + str(k), ds[1]))
        self.check_deadlock()

    def check_deadlock(self):
        val = {}
        ptr = {e: 0 for e in self.ENGS}
        prog = True
        while prog:
            prog = False
            for e in self.ENGS:
                tr = self.trace[e]
                while ptr[e] < len(tr):
                    k, nm, v = tr[ptr[e]]
                    if k == "w":
                        if val.get(nm, 0) >= v:
                            ptr[e] += 1
                            prog = True
                        else:
                            break
                    else:
                        val[nm] = val.get(nm, 0) + v
                        ptr[e] += 1
                        prog = True
        bad = {e: (ptr[e], len(self.trace[e]), self.trace[e][ptr[e]]) for e in self.ENGS if ptr[e] < len(self.trace[e])}
        if bad:
            raise RuntimeError("DEADLOCK in schedule: %r ; sem values: %r" % (bad, {k: val.get(k, 0) for k in [b[2][1] for b in bad.values()]}))

    def mm(self, out, lhsT, rhs, start, stop, r, w):
        self.op("pe", lambda e: e.matmul(out, lhsT=lhsT, rhs=rhs, start=start, stop=stop), r, w)

    def tr(self, out, in_, ident, r, w):
        self.op("pe", lambda e: e.transpose(out, in_, ident), r, w)

    def act(self, out, in_, func, r, w, bias=None, scale=None, accum=None):
        kw = {}
        if bias is not None:
            kw["bias"] = bias
        if scale is not None:
            kw["scale"] = scale
        if accum is not None:
            kw["accum_out"] = accum
        self.op("act", lambda e: e.activation(out=out, in_=in_, func=func, **kw), r, w)

    def tt(self, eng, out, a, b, op, r, w):
        self.op(eng, lambda e: e.tensor_tensor(out=out, in0=a, in1=b, op=op), r, w)

    def ts(self, eng, out, a, s1, s2, op0, op1, r, w):
        if op1 is None:
            self.op(eng, lambda e: e.tensor_scalar(out=out, in0=a, scalar1=s1, scalar2=None, op0=op0), r, w)
        else:
            self.op(eng, lambda e: e.tensor_scalar(out=out, in0=a, scalar1=s1, scalar2=s2, op0=op0, op1=op1), r, w)

    def stt(self, out, a, s, b, op0, op1, r, w):
        self.op("dve", lambda e: e.scalar_tensor_tensor(out=out, in0=a, scalar=s, in1=b, op0=op0, op1=op1), r, w)

    def cp(self, eng, out, in_, r, w):
        if eng == "act":
            self.op("act", lambda e: e.copy(out=out, in_=in_), r, w)
        else:
            self.op(eng, lambda e: e.tensor_copy(out=out, in_=in_), r, w)

    def memset(self, eng, ap, val, w):
        self.op(eng, lambda e: e.memset(ap, val), (), w)


def build(cfg):
    nc = bass.Bass("TRN2", target_bir_lowering=False)
    TP, G, PAST, TS, NS = cfg.TP, cfg.G, cfg.PAST, cfg.TS, cfg.NS
    NSK = NS * TS
    NTP = TP // 128
    NR = 1 + NS
    NPT = PAST // 128

    def din(name, shape, dt=F32):
        return nc.dram_tensor(name, list(shape), dt, kind="ExternalInput").ap()

    def dout(name, shape, dt=F32):
        return nc.dram_tensor(name, list(shape), dt, kind="ExternalOutput").ap()

    x_p = din("x_p", [TP, D]); x_s = din("x_s", [NSK, D]); c_in = din("c_in", [NR, D])
    ca_k = din("ca_k", [2, NS, PAST, D]); ca_v = din("ca_v", [2, NS, PAST, D])
    sb_c = din("sb_c", [NS, H_B, DQK_B, DV_B]); sb_n = din("sb_n", [NS, H_B, DQK_B]); sb_m = din("sb_m", [NS, H_B])
    cc_kv = din("cc_kv", [NS, PAST, KV_LORA]); cc_kr = din("cc_kr", [NS, PAST, 64])
    w_ada = din("w_ada", [DEPTH, D, 6 * D]); b_ada = din("b_ada", [DEPTH, 6 * D])
    g_norm1 = din("g_norm1", [DEPTH, D]); g_norm2 = din("g_norm2", [DEPTH, D])
    w_a_qkv = din("w_a_qkv", [2, D, 3 * D]); a_lambda = din("a_lambda", [2, 4, 64]); g_a_sub = din("g_a_sub", [2, 128])
    w_a_o = din("w_a_o", [2, D, D])
    w_b_in = din("w_b_in", [1, D, N_B_IN]); b_b_gates = din("b_b_gates", [1, 8]); g_b_out = din("g_b_out", [1, D])
    w_b_out = din("w_b_out", [1, D, D])
    w_c_dq = din("w_c_dq", [1, D, Q_LORA]); g_c_q = din("g_c_q", [1, Q_LORA]); w_c_uq = din("w_c_uq", [1, Q_LORA, 1536])
    w_c_dkv = din("w_c_dkv", [1, D, 320]); g_c_kv = din("g_c_kv", [1, KV_LORA]); w_c_ukv = din("w_c_ukv", [1, KV_LORA, 2048])
    w_c_o = din("w_c_o", [1, D, D])
    w_ffn_in = din("w_ffn_in", [DEPTH, D, 2 * D_FF]); w_ffn_out = din("w_ffn_out", [DEPTH, D_FF, D])
    g_final = din("g_final", [1, D])
    k_ident = din("k_ident", [128, 128])
    k_tri = din("k_tri", [64, 64])
    k_sel = din("k_sel", [4, 4 * 64])
    rope_a = din("rope_a", [TP + NSK, 16])
    rope_c = din("rope_c", [TP + NSK, 64])

    y_p = dout("y_p", [TP, D]); y_s = dout("y_s", [NSK, D])
    akp = dout("akp", [2, TP, D]); avp = dout("avp", [2, TP, D])
    bcp = dout("bcp", [H_B, DQK_B, DV_B]); bnp = dout("bnp", [H_B, DQK_B]); bmp = dout("bmp", [H_B, 1])
    ckvp = dout("ckvp", [TP, KV_LORA]); ckrp = dout("ckrp", [TP, 64])
    aks = dout("aks", [2, NSK, D]); avs = dout("avs", [2, NSK, D])
    bcs = dout("bcs", [NS, H_B, DQK_B, DV_B]); bns = dout("bns", [NS, H_B, DQK_B]); bms = dout("bms", [NS, H_B, 1])
    ckvs = dout("ckvs", [NSK, KV_LORA]); ckrs = dout("ckrs", [NSK, 64])
    NWT = 200
    wsc = nc.dram_tensor("wsc", [NWT, 128, 4096], BF16, kind="Internal").ap()
    mla_kv_p = nc.dram_tensor("mla_kv_p", [TP, 2048], BF16, kind="Internal").ap()
    mla_kv_s = nc.dram_tensor("mla_kv_s", [NS, PAST + TS, 2048], BF16, kind="Internal").ap()

    es = ExitStack()
    with es:
        P = Prog(nc, es)

        uniq = [0]

        def sb(name, shape, dt=F32, stack=None):
            if stack is not None:
                uniq[0] += 1
                name = "%s_u%d" % (name, uniq[0])
            return (stack or es).enter_context(nc.sbuf_tensor(name, list(shape), dt))

        PS = [es.enter_context(nc.psum_tensor("ps%d" % i, [128, 512], F32)) for i in range(8)]

        def psk(i):
            return ("ps", i)

        def psb(i):
            return PS[i][:].bitcast(BF16)

        xTp = sb("xTp", [128, NCH, TP])
        xTs = sb("xTs", [128, NCH, NSK])
        identF = sb("identF", [128, 128]); identB = sb("identB", [128, 128], BF16)
        onesB = sb("onesB", [128, 128], BF16); onesF = sb("onesF", [128, 128])
        hT = sb("hT", [128, NCH, G], BF16)
        sqr = [sb("sq%d" % i, [128, G], BF16) for i in range(2)]
        tmpN = [sb("tmpN%d" % i, [128, G]) for i in range(2)]
        rstd = sb("rstd", [128, G])
        NWB = 2
        WB = [sb("wb%d" % i, [128, NCH, 512], BF16) for i in range(NWB)]
        NSL = 4
        SL = [sb("sl%d" % i, [128, 512]) for i in range(NSL)]
        modT = sb("modT", [128, DEPTH, 48, NR])
        A1 = sb("A1", [128, DEPTH, NCH, NR]); A2 = sb("A2", [128, DEPTH, NCH, NR])
        gn1T = sb("gn1T", [128, DEPTH * NCH]); gn2T = sb("gn2T", [128, DEPTH * NCH]); gfT = sb("gfT", [128, NCH])
        ropeA = sb("ropeA", [128, NTP + 1, 16])
        epsT = sb("epsT", [128, 1])
        uT = sb("uT", [128, 11, G], BF16)

        st = {"wb": 0, "sl": 0, "sq": 0, "tn": 0, "pa": 0}

        P.dma("sp", identF[:], k_ident[:, :], (), ["identF"])
        P.cp("dve", identB[:], identF[:], ["identF"], ["identB"])
        P.memset("dve", onesB[:], 1.0, ["onesB"])
        P.memset("dve", onesF[:], 1.0, ["onesF"])
        P.memset("dve", epsT[:], EPS, ["epsT"])

        class Grp:
            pass
        groups = []
        for g in range(TP // G):
            gr = Grp()
            gr.kind = "p"; gr.g = g; gr.n = G; gr.xT = xTp; gr.c0 = g * G
            gr.segs = [(0, 0, G)]
            gr.tiles = [(t * 128, 128, g * (G // 128) + t) for t in range(G // 128)]
            groups.append(gr)
        gs = Grp()
        gs.kind = "s"; gs.g = 0; gs.n = NSK; gs.xT = xTs; gs.c0 = 0
        gs.segs = [(1 + s, s * TS, TS) for s in range(NS)]
        gs.tiles = [(0, NSK, NTP)]
        groups.append(gs)

        def xview(gr, c, a=0, n=None):
            n = gr.n - a if n is None else n
            return gr.xT[:, c, gr.c0 + a:gr.c0 + a + n]

        def xkey(gr):
            return ("x", gr.kind, gr.g)

        wtiles = {}
        cur_layer = [0]

        def wview(wname, w2d, c0, ncols, k0=0, nk=None):
            v = w2d.rearrange("(kc p) n -> p kc n", p=128)
            nk = v.shape[1] - k0 if nk is None else nk
            return (wname, c0, ncols, k0, nk, v[:, k0:k0 + nk, c0:c0 + ncols])

        def wprep(spec, layer):
            (wname, c0, ncols, k0, nk, view) = spec
            tk = (wname, c0, ncols, k0, nk)
            if tk in wtiles:
                return wtiles[tk]
            t = len(wtiles)
            assert t < NWT
            dst = wsc[t, :, 0:nk * ncols].rearrange("p (k n) -> p k n", n=ncols)
            ck = ("wcv", layer)
            P.dma("pool", dst, view, (), [ck], acc=True)
            wtiles[tk] = (dst, ck)
            return wtiles[tk]

        def wload(spec, nk, ncols):
            src, ck = wprep(spec, cur_layer[0])
            i = st["wb"] % NWB
            st["wb"] += 1
            key = ("wb", i)
            P.dma("pool", WB[i][:, 0:nk, 0:ncols], src, [ck], [key])
            return WB[i], key

        def next_ps():
            i = st["pa"] % 2
            st["pa"] += 1
            return i

        def next_sl():
            i = st["sl"] % NSL
            st["sl"] += 1
            return i

        def load_rows_T(dst, dkey, src_rows_ap, nrows):
            i = next_sl()
            P.dma("sp", SL[i][0:nrows, 0:128], src_rows_ap, (), [("sl", i)])
            b = next_ps()
            P.tr(PS[b][:, 0:nrows], SL[i][0:nrows, 0:128], identF[0:nrows, 0:nrows], [("sl", i), "identF"], [psk(b)])
            P.cp("dve", dst, PS[b][:, 0:nrows], [psk(b)], [dkey])

        cT = sb("cT", [128, NCH, NR]); scT = sb("scT", [128, NCH, NR], BF16)
        for k in range(NCH):
            load_rows_T(cT[:, k, :], "cT", c_in[:, k * 128:(k + 1) * 128], NR)
        P.act(scT[:], cT[:], AF.Silu, ["cT"], ["scT"])
        bT = sb("bT", [128, DEPTH, 48])
        for i in range(DEPTH):
            load_rows_T(bT[:, i, :], "bT", b_ada[i, :].rearrange("(j p) -> j p", p=128), 48)
        load_rows_T(gn1T[:], "gn1T", g_norm1.rearrange("i (c p) -> (i c) p", p=128), DEPTH * NCH)
        load_rows_T(gn2T[:], "gn2T", g_norm2.rearrange("i (c p) -> (i c) p", p=128), DEPTH * NCH)
        load_rows_T(gfT[:], "gfT", g_final.rearrange("i (c p) -> (i c) p", p=128), NCH)
        P.dma("sp", ropeA[:, 0:NTP, :], rope_a[0:TP, :].rearrange("(t p) f -> p t f", p=128), (), ["ropetab"])
        P.dma("sp", ropeA[0:NSK, NTP, :], rope_a[TP:TP + NSK, :], (), ["ropetab"], semkey="ropetab2")

        for i in cfg.layers:
            mb = 7
            cur_layer[0] = ("ada", i)
            for cb in range(12):
                wt, wk = wload(wview(("ada", i), w_ada[i], cb * 512, 512), NCH, 512)
                for fc in range(4):
                    j = cb * 4 + fc
                    for k in range(NCH):
                        P.mm(PS[mb][:, j * NR:(j + 1) * NR], wt[:, k, fc * 128:(fc + 1) * 128], scT[:, k, :],
                             k == 0, k == NCH - 1, [wk, "scT"], [psk(mb)])
            P.tt("dve", modT[:, i, :, :], PS[mb][:, 0:48 * NR].rearrange("p (j r) -> p j r", r=NR),
                 bT[:, i, :].unsqueeze(2).broadcast_to([128, 48, NR]), ALU.add, [psk(mb), "bT"], ["modT"])
            P.stt(A1[:, i, :, :], modT[:, i, 8:16, :], 1.0,
                  gn1T[:, i * NCH:(i + 1) * NCH].unsqueeze(2).broadcast_to([128, NCH, NR]), ALU.add, ALU.mult,
                  ["modT", "gn1T"], ["A1"])
            P.stt(A2[:, i, :, :], modT[:, i, 32:40, :], 1.0,
                  gn2T[:, i * NCH:(i + 1) * NCH].unsqueeze(2).broadcast_to([128, NCH, NR]), ALU.add, ALU.mult,
                  ["modT", "gn2T"], ["A2"])

        def sh(i, which, c, r):
            base = 0 if which == 1 else 24
            return modT[:, i, base + c, r:r + 1]

        def gt(i, which, c, r):
            base = 16 if which == 1 else 40
            return modT[:, i, base + c, r:r + 1]

        def Ac(i, which, c, r):
            return (A1 if which == 1 else A2)[:, i, c, r:r + 1]

        def load_xT(gr, src):
            for (c0, nt, _ti) in gr.tiles:
                for half in range(2):
                    i = next_sl()
                    P.dma("sp", SL[i][0:nt, :], src[gr.c0 + c0:gr.c0 + c0 + nt, half * 512:(half + 1) * 512], (), [("sl", i)])
                    b = next_ps()
                    for q in range(4):
                        P.tr(PS[b][:, q * 128:q * 128 + nt], SL[i][0:nt, q * 128:(q + 1) * 128], identF[0:nt, 0:nt],
                             [("sl", i), "identF"], [psk(b)])
                    P.cp("act" if half else "dve", gr.xT[:, half * 4:half * 4 + 4, gr.c0 + c0:gr.c0 + c0 + nt],
                         PS[b][:].rearrange("p (q t) -> p q t", t=128)[:, :, 0:nt], [psk(b)], [xkey(gr)])

        for gr in groups:
            load_xT(gr, x_p if gr.kind == "p" else x_s)

        def norm_stats(gr):
            n = gr.n
            b = next_ps()
            for c in range(NCH):
                i = st["sq"] % 2
                st["sq"] += 1
                P.act(sqr[i][:, 0:n], xview(gr, c), AF.Square, [xkey(gr)], [("sq", i)])
                P.mm(PS[b][:, 0:n], onesB[:], sqr[i][:, 0:n], c == 0, c == NCH - 1, [("sq", i), "onesB"], [psk(b)])
            P.act(rstd[:, 0:n], PS[b][:, 0:n], AF.Sqrt, [psk(b), "epsT"], ["rstd"], bias=epsT[:, 0:1], scale=1.0 / D)
            P.op("dve", lambda e: e.reciprocal(out=rstd[:, 0:n], in_=rstd[:, 0:n]), ["rstd"], ["rstd"])

        def norm_mod(gr, i, which):
            norm_stats(gr)
            for c in range(NCH):
                for (r, a, n) in gr.segs:
                    t = st["tn"] % 2
                    st["tn"] += 1
                    P.stt(tmpN[t][:, 0:n], xview(gr, c, a, n), Ac(i, which, c, r), rstd[:, a:a + n], ALU.mult, ALU.mult,
                          [xkey(gr), "rstd", "A1", "A2"], [("tn", t)])
                    P.act(hT[:, c, a:a + n], tmpN[t][:, 0:n], AF.Identity, [("tn", t), "modT"], ["hT"],
                          bias=sh(i, which, c, r), scale=1.0)

        def resid_add(gr, i, which, c, psap, pskey, extra_r=()):
            for (r, a, n) in gr.segs:
                P.stt(xview(gr, c, a, n), psap[:, a:a + n], gt(i, which, c, r), xview(gr, c, a, n), ALU.mult, ALU.add,
                      [pskey, "modT", xkey(gr)] + list(extra_r), [xkey(gr)])

        def ffn(gr, i):
            n = gr.n
            norm_mod(gr, i, 2)
            for half in range(2):
                f0 = half * 11
                blocks = [(f0, 4), (f0 + 4, 4), (f0 + 8, 3)]
                for (fb, nf) in blocks:
                    wa, wak = wload(wview(("fin", i), w_ffn_in[i], fb * 128, nf * 128), NCH, nf * 128)
                    wb_, wbk = wload(wview(("fin", i), w_ffn_in[i], D_FF + fb * 128, nf * 128), NCH, nf * 128)
                    for f in range(nf):
                        ba = next_ps()
                        for k in range(NCH):
                            P.mm(PS[ba][:, 0:n], wa[:, k, f * 128:(f + 1) * 128], hT[:, k, 0:n], k == 0, k == NCH - 1,
                                 [wak, "hT"], [psk(ba)])
                        bb = next_ps()
                        for k in range(NCH):
                            P.mm(PS[bb][:, 0:n], wb_[:, k, f * 128:(f + 1) * 128], hT[:, k, 0:n], k == 0, k == NCH - 1,
                                 [wbk, "hT"], [psk(bb)])
                        t = st["tn"] % 2
                        st["tn"] += 1
                        P.act(tmpN[t][:, 0:n], PS[ba][:, 0:n], AF.Silu, [psk(ba)], [("tn", t)])
                        P.tt("dve", uT[:, fb - f0 + f, 0:n], tmpN[t][:, 0:n], PS[bb][:, 0:n], ALU.mult,
                             [("tn", t), psk(bb)], [("uT", fb - f0 + f)])
                for ch in range(2):
                    banks = [4, 5, 6, 7]
                    wts = []
                    for (k0, nk) in ((0, 8), (8, 3)):
                        wt, wk = wload(wview(("fout", i), w_ffn_out[i], ch * 512, 512, k0=f0 + k0, nk=nk), nk, 512)
                        wts.append((wt, wk, k0, nk))
                    for cc in range(4):
                        c = ch * 4 + cc
                        for (wt, wk, k0, nk) in wts:
                            for k in range(nk):
                                f = k0 + k
                                P.mm(PS[banks[cc]][:, 0:n], wt[:, k, cc * 128:(cc + 1) * 128], uT[:, f, 0:n],
                                     f == 0, f == 10, [wk, ("uT", f)], [psk(banks[cc])])
                        resid_add(gr, i, 2, c, PS[banks[cc]], psk(banks[cc]))

        def final_out(gr, dst):
            n = gr.n
            norm_stats(gr)
            for c in range(NCH):
                P.stt(hT_f[:, c, 0:n], xview(gr, c), gfT[:, c:c + 1], rstd[:, 0:n], ALU.mult, ALU.mult,
                      [xkey(gr), "rstd", "gfT"], ["hTf"])
            for (c0, nt, _ti) in gr.tiles:
                for half in range(2):
                    b = next_ps()
                    for q in range(4):
                        P.tr(PS[b][0:nt, q * 128:(q + 1) * 128], hT_f[:, half * 4 + q, c0:c0 + nt], identF[:, :],
                             ["hTf", "identF"], [psk(b)])
                    i = next_sl()
                    P.cp("act", SL[i][0:nt, :], PS[b][0:nt, :], [psk(b)], [("sl", i)])
                    P.dma("sp", dst[gr.c0 + c0:gr.c0 + c0 + nt, half * 512:(half + 1) * 512], SL[i][0:nt, :], [("sl", i)], ())

        def rope_slab(v, skey, nt, cos2, sin2, half, rot_cols, nsub, eng="dve"):
            x1 = v[:, :, rot_cols:rot_cols + half]
            x2 = v[:, :, rot_cols + half:rot_cols + 2 * half]
            cos = cos2.unsqueeze(1).broadcast_to([nt, nsub, half])
            sin = sin2.unsqueeze(1).broadcast_to([nt, nsub, half])
            t1 = rtmp[0][0:nt, 0:nsub * half].rearrange("p (s d) -> p s d", d=half)
            t2 = rtmp[1][0:nt, 0:nsub * half].rearrange("p (s d) -> p s d", d=half)
            t3 = rtmp[2][0:nt, 0:nsub * half].rearrange("p (s d) -> p s d", d=half)
            t4 = rtmp[3][0:nt, 0:nsub * half].rearrange("p (s d) -> p s d", d=half)
            tk = "ropetab"
            P.tt(eng, t1, x1, cos, ALU.mult, [skey, tk], ["rt1"])
            P.tt(eng, t2, x2, sin, ALU.mult, [skey, tk], ["rt2"])
            P.tt(eng, t3, x2, cos, ALU.mult, [skey, tk], ["rt3"])
            P.tt(eng, t4, x1, sin, ALU.mult, [skey, tk], ["rt4"])
            P.tt(eng, x1, t1, t2, ALU.subtract, ["rt1", "rt2", skey], [skey])
            P.tt(eng, x2, t3, t4, ALU.add, ["rt3", "rt4", skey], [skey])

        rtmp = [sb("rtmp%d" % i, [128, 64]) for i in range(4)]

        def attn_core(ls, gr, nq_cols, q_c0, heads_fn, key_tiles, lam_ap, kind, out_fn):
            pass

        def mixer_diff(gr, i, j, L):
            n = gr.n
            lam_init = 0.8 - 0.6 * math.exp(-0.3 * i)
            norm_mod(gr, i, 1)
            QT = L["QT"]; onT = L["onT"]
            kdst = akp if gr.kind == "p" else aks
            vdst = avp if gr.kind == "p" else avs
            kvkey = ("kv", j, gr.kind, gr.g)
            for cb in range(6):
                wt, wk = wload(wview(("aqkv", j), w_a_qkv[j], cb * 512, 512), NCH, 512)
                for (c0, nt, ti) in gr.tiles:
                    b = next_ps()
                    for k in range(NCH):
                        P.mm(PS[b][0:nt, :], hT[:, k, c0:c0 + nt], wt[:, k, :], k == 0, k == NCH - 1, ["hT", wk], [psk(b)])
                    s = next_sl()
                    sk = ("sl", s)
                    P.cp("act", SL[s][0:nt, :], PS[b][0:nt, :], [psk(b)], [sk])
                    if cb < 4:
                        rope_slab(SL[s][0:nt, :].rearrange("p (s d) -> p s d", d=64), sk, nt, ropeA[0:nt, ti, 0:8],
                                  ropeA[0:nt, ti, 8:16], 8, 0, 8, eng="pool" if cb % 2 else "dve")
                    if cb < 2:
                        qb = L["qb"]
                        P.cp("act", qb[0:nt, :], SL[s][0:nt, :], [sk], ["qb"])
                        tb = 2
                        for q in range(4):
                            P.tr(psb(tb)[:, q * 128:q * 128 + nt], qb[0:nt, q * 128:(q + 1) * 128], identB[0:nt, 0:nt],
                                 ["qb", "identB"], [psk(tb)])
                        P.cp("dve", QT[:, cb * 4:cb * 4 + 4, c0:c0 + nt],
                             psb(tb)[:, 0:512].rearrange("p (q t) -> p q t", t=128)[:, :, 0:nt], [psk(tb)], ["QT"])
                    elif cb < 4:
                        P.dma("sp", kdst[j, gr.c0 + c0:gr.c0 + c0 + nt, (cb - 2) * 512:(cb - 1) * 512], SL[s][0:nt, :],
                              [sk], [kvkey], semkey=("slo", s))
                    else:
                        P.dma("sp", vdst[j, gr.c0 + c0:gr.c0 + c0 + nt, (cb - 4) * 512:(cb - 3) * 512], SL[s][0:nt, :],
                              [sk], [kvkey], semkey=("slo", s))
            scale = 64 ** -0.5
            for si, (r, a, nq) in enumerate(gr.segs):
                if gr.kind == "p":
                    nkt = (gr.c0 + n) // 128
                    ktl = []
                    for kt in range(nkt):
                        gk = (kt * 128) // G
                        ktl.append((akp[j, kt * 128:(kt + 1) * 128, :], avp[j, kt * 128:(kt + 1) * 128, :], 128,
                                    kt * 128 - gr.c0, ("kv", j, "p", gk)))
                else:
                    s_ = si
                    ktl = []
                    for kt in range(NPT):
                        ktl.append((ca_k[j, s_, kt * 128:(kt + 1) * 128, :], ca_v[j, s_, kt * 128:(kt + 1) * 128, :], 128,
                                    -1, None))
                    ktl.append((aks[j, s_ * TS:(s_ + 1) * TS, :], avs[j, s_ * TS:(s_ + 1) * TS, :], TS, -1, kvkey))
                for hd in range(H_A):
                    first = True
                    nk_total = len(ktl)
                    for kti, (ksrc, vsrc, nk, dcol, dep) in enumerate(ktl):
                        last = kti == nk_total - 1
                        cq0 = max(dcol, 0)
                        kb = L["kb"][kti % 2]; vb = L["vb"][kti % 2]
                        kbk = ("kb", kti % 2); vbk = ("vb", kti % 2)
                        deps = [dep] if dep is not None else []
                        P.dma("pool", kb[0:nk, :], ksrc[:, hd * 128:(hd + 1) * 128], deps, [kbk])
                        P.dma("pool", vb[0:nk, :], vsrc[:, hd * 128:(hd + 1) * 128], deps, [vbk])
                        tb = 2
                        P.tr(psb(tb)[:, 0:nk], kb[0:nk, :], identB[0:nk, 0:nk], [kbk, "identB"], [psk(tb)])
                        kT = L["kT"][kti % 2]; kTk = ("kT", kti % 2)
                        P.cp("dve", kT[:, 0:nk], psb(tb)[:, 0:nk], [psk(tb)], [kTk])
                        qa, qn = a + cq0, nq - cq0
                        P.mm(PS[0][0:nk, 0:qn], kT[0:64, 0:nk], QT[0:64, hd, qa:qa + qn], True, True, [kTk, "QT"], [psk(0)])
                        P.mm(PS[1][0:nk, 0:qn], kT[64:128, 0:nk], QT[64:128, hd, qa:qa + qn], True, True, [kTk, "QT"], [psk(1)])
                        e1 = L["e1"][kti % 2]; e2 = L["e2"][kti % 2]
                        e1k = ("e1", kti % 2); e2k = ("e2", kti % 2)
                        P.act(e1[0:nk, 0:qn], PS[0][0:nk, 0:qn], AF.Exp, [psk(0)], [e1k], scale=scale)
                        P.act(e2[0:nk, 0:qn], PS[1][0:nk, 0:qn], AF.Exp, [psk(1)], [e2k], scale=scale)
                        if dcol >= 0:
                            P.memset("pool", e1[64:128, 0:64], 0.0, [e1k])
                            P.memset("pool", e2[64:128, 0:64], 0.0, [e2k])
                        P.mm(PS[4][:, cq0:nq], vb[0:nk, :], e1[0:nk, 0:qn], first, last, [vbk, e1k], [psk(4)])
                        P.mm(PS[5][:, cq0:nq], vb[0:nk, :], e2[0:nk, 0:qn], first, last, [vbk, e2k], [psk(5)])
                        P.mm(PS[6][:, cq0:nq], onesB[0:nk, :], e1[0:nk, 0:qn], first, last, ["onesB", e1k], [psk(6)])
                        P.mm(PS[7][:, cq0:nq], onesB[0:nk, :], e2[0:nk, 0:qn], first, last, ["onesB", e2k], [psk(7)])
                        first = False
                    r1 = L["r1"]; r2 = L["r2"]; o1 = L["o1"]; o2 = L["o2"]
                    P.op("dve", lambda e: e.reciprocal(out=r1[:, 0:nq], in_=PS[6][:, 0:nq]), [psk(6)], ["r1"])
                    P.op("dve", lambda e: e.reciprocal(out=r2[:, 0:nq], in_=PS[7][:, 0:nq]), [psk(7)], ["r2"])
                    P.tt("dve", o1[:, 0:nq], PS[4][:, 0:nq], r1[:, 0:nq], ALU.mult, [psk(4), "r1"], ["o1"])
                    P.tt("dve", o2[:, 0:nq], PS[5][:, 0:nq], r2[:, 0:nq], ALU.mult, [psk(5), "r2"], ["o2"])
                    P.stt(o1[:, 0:nq], o2[:, 0:nq], L["nlam"][:, 0:1], o1[:, 0:nq], ALU.mult, ALU.add, ["o1", "o2", "nlam"], ["o1"])
                    sqi = st["sq"] % 2
                    st["sq"] += 1
                    P.act(sqr[sqi][:, 0:nq], o1[:, 0:nq], AF.Square, ["o1"], [("sq", sqi)])
                    P.mm(PS[3][:, 0:nq], onesB[:], sqr[sqi][:, 0:nq], True, True, [("sq", sqi), "onesB"], [psk(3)])
                    P.act(r1[:, 0:nq], PS[3][:, 0:nq], AF.Sqrt, [psk(3), "epsT"], ["r1"], bias=epsT[:, 0:1], scale=1.0 / 128)
                    P.op("dve", lambda e: e.reciprocal(out=r1[:, 0:nq], in_=r1[:, 0:nq]), ["r1"], ["r1"])
                    P.stt(onT[:, hd, a:a + nq], o1[:, 0:nq], L["gsub"][:, 0:1], r1[:, 0:nq], ALU.mult, ALU.mult,
                          ["o1", "gsub", "r1"], ["onT"])
            for ch in range(2):
                wt, wk = wload(wview(("ao", j), w_a_o[j], ch * 512, 512), NCH, 512)
                for cc in range(4):
                    c = ch * 4 + cc
                    b = next_ps()
                    for hd in range(H_A):
                        P.mm(PS[b][:, 0:n], wt[:, hd, cc * 128:(cc + 1) * 128], onT[:, hd, 0:n], hd == 0, hd == H_A - 1,
                             [wk, "onT"], [psk(b)])
                    resid_add(gr, i, 1, c, PS[b], psk(b))

        def setup_diff(i, j, ls):
            L = {}
            L["QT"] = sb("QT", [128, H_A, G], BF16, ls)
            L["onT"] = sb("onT", [128, H_A, G], BF16, ls)
            L["qb"] = sb("qb", [128, 512], BF16, ls)
            L["kb"] = [sb("kb%d" % q, [128, 128], BF16, ls) for q in range(2)]
            L["vb"] = [sb("vb%d" % q, [128, 128], BF16, ls) for q in range(2)]
            L["kT"] = [sb("kT%d" % q, [128, 128], BF16, ls) for q in range(2)]
            L["e1"] = [sb("e1%d" % q, [128, G], BF16, ls) for q in range(2)]
            L["e2"] = [sb("e2%d" % q, [128, G], BF16, ls) for q in range(2)]
            L["r1"] = sb("r1", [128, G], F32, ls); L["r2"] = sb("r2", [128, G], F32, ls)
            L["o1"] = sb("o1", [128, G], F32, ls); L["o2"] = sb("o2", [128, G], F32, ls)
            L["nlam"] = sb("nlam", [128, 1], F32, ls); L["gsub"] = sb("gsub", [128, 1], F32, ls)
            lam_init = 0.8 - 0.6 * math.exp(-0.3 * i)
            lv = sb("lv", [1, 4, 64], F32, ls); lp = sb("lp", [1, 2, 64], F32, ls); lsum = sb("lsum", [1, 2], F32, ls)
            lam1 = sb("lam1", [1, 1], F32, ls)
            P.dma("sp", lv[:], a_lambda[j:j + 1, :, :], (), ["lv"])
            P.tt("dve", lp[:], lv[:, 0:4:2, :], lv[:, 1:4:2, :], ALU.mult, ["lv"], ["lp"])
            P.op("dve", lambda e: e.tensor_reduce(out=lsum[:], in_=lp[:], axis=AX.X, op=ALU.add), ["lp"], ["lsum"])
            P.act(lsum[:], lsum[:], AF.Exp, ["lsum"], ["lsum"])
            P.tt("dve", lam1[:], lsum[:, 1:2], lsum[:, 0:1], ALU.subtract, ["lsum"], ["lam1"])
            P.ts("dve", lam1[:], lam1[:], -lam_init, None, ALU.add, None, ["lam1"], ["lam1"])
            P.mm(PS[3][:, 0:1], onesF[0:1, :], lam1[:], True, True, ["onesF", "lam1"], [psk(3)])
            P.cp("dve", L["nlam"][:], PS[3][:, 0:1], [psk(3)], ["nlam"])
            s = next_sl()
            P.dma("sp", SL[s][0:1, 0:128], g_a_sub[j:j + 1, :], (), [("sl", s)])
            P.tr(PS[3][:, 0:1], SL[s][0:1, 0:128], identF[0:1, 0:1], [("sl", s), "identF"], [psk(3)])
            P.ts("dve", L["gsub"][:], PS[3][:, 0:1], 1.0 - lam_init, None, ALU.mult, None, [psk(3)], ["gsub"])
            return L


        def setup_mlstm(i, j, ls):
            L = {}
            L["Cst"] = sb("Cst", [128, H_B, 257], F32, ls)
            L["Cb"] = sb("Cb", [128, H_B, 257], BF16, ls)
            L["mrow"] = sb("mrow", [4, 1], F32, ls)
            L["qT"] = sb("qT", [128, H_B, 64], BF16, ls)
            L["kT"] = sb("mkT", [128, H_B, 64], BF16, ls)
            L["ktm"] = sb("ktm", [64, 1, 512], F32, ls)
            L["Vaug"] = sb("Vaug", [64, 1, H_B, 257], BF16, ls)
            L["gsig"] = sb("gsig", [64, 1, 1024], F32, ls)
            L["hnT"] = sb("hnT", [128, NCH, G], BF16, ls)
            L["goutT"] = sb("goutT", [128, NCH], F32, ls)
            L["bg"] = sb("bg", [4, 2], F32, ls)
            L["tri"] = sb("tri", [64, 64], F32, ls)
            L["sel"] = sb("sel", [4, 256], F32, ls)
            L["one1"] = sb("one1", [128, 1], F32, ls)
            for nm in ("rI", "rF", "rA", "rE", "rL", "rB", "rU", "rCM", "rM", "ra", "rwr", "remt"):
                L[nm] = sb("ml_" + nm, [4, 64], F32, ls)
            L["nML"] = sb("nML", [4, 1], F32, ls)
            L["dg"] = sb("dg", [4, 4], F32, ls)
            L["cols"] = sb("cols", [64, 16], F32, ls)
            L["dbc"] = sb("dbc", [128, 4], F32, ls)
            L["z"] = sb("mz", [64, 64], F32, ls)
            L["W"] = sb("mW", [64, 64], F32, ls)
            L["Dm"] = sb("mDm", [64, 64], BF16, ls)
            L["itmp"] = sb("itmp", [64, 257], F32, ls)
            L["ND"] = sb("ND", [64, 257], F32, ls)
            L["sc"] = sb("msc", [64, 8], F32, ls)
            L["hn"] = sb("mhn", [64, 1024], BF16, ls)
            L["kw"] = sb("mkw", [64, 128], BF16, ls)
            L["nrow"] = sb("nrow", [4, 128], F32, ls)
            load_rows_T(L["goutT"][:], "goutT", g_b_out.rearrange("i (c p) -> (i c) p", p=128), NCH)
            P.dma("sp", L["bg"][:, 0:1], b_b_gates[0:1, 0:4].rearrange("o f -> f o"), (), ["bg"])
            P.dma("sp", L["bg"][:, 1:2], b_b_gates[0:1, 4:8].rearrange("o f -> f o"), (), ["bg"])
            P.dma("sp", L["tri"][:], k_tri[:, :], (), ["tri"])
            P.dma("sp", L["sel"][:], k_sel[:, :], (), ["sel"])
            P.memset("dve", L["one1"][:], 1.0, ["one1"])
            return L

        def mlstm_state_init(L, seq):
            Cst = L["Cst"]
            if seq is None:
                P.memset("dve", Cst[:], 0.0, ["Cst"])
                P.memset("dve", L["mrow"][:], 0.0, ["mrow"])
            else:
                P.dma("sp", Cst[:, :, 0:256], sb_c[seq].rearrange("h d v -> d h v"), (), ["Cst"])
                load_rows_T(Cst[:, :, 256], "Cst", sb_n[seq], 4)
                P.dma("sp", L["mrow"][:], sb_m[seq:seq + 1, :].rearrange("o f -> f o"), (), ["mrow"])
            P.cp("act", L["Cb"][:], Cst[:], ["Cst"], ["Cb"])

        def mlstm_state_out(L, dc, dn, dm):
            Cst = L["Cst"]
            P.dma("sp", dc.rearrange("h d v -> d h v"), Cst[:, :, 0:256], ["Cst"], (), semkey="Cst_o")
            P.tr(PS[3][0:4, 0:128], Cst[:, :, 256], identF[:, :], ["Cst", "identF"], [psk(3)])
            P.cp("dve", L["nrow"][:], PS[3][0:4, 0:128], [psk(3)], ["nrow"])
            P.dma("sp", dn, L["nrow"][:], ["nrow"], (), semkey="nrow_o")
            P.dma("sp", dm, L["mrow"][:], ["mrow"], (), semkey="mrow_o")

        def mlstm_unit(gr, L, u0, nu, Lc):
            nchk = nu // Lc
            wj = w_b_in[0]
            qT, kT, ktm, Vaug, gsig = L["qT"], L["kT"], L["ktm"], L["Vaug"], L["gsig"]
            kscale = DQK_B ** -0.5
            for blk, dst, scl in ((0, qT, 1.0), (1, kT, kscale)):
                wt, wk = wload(wview(("bin", 0), wj, blk * 512, 512), NCH, 512)
                for h in range(H_B):
                    b = next_ps()
                    for k in range(NCH):
                        P.mm(PS[b][:, 0:nu], wt[:, k, h * 128:(h + 1) * 128], hT[:, k, u0:u0 + nu], k == 0, k == NCH - 1,
                             [wk, "hT"], [psk(b)])
                    P.act(dst[:, h, 0:nu], PS[b][:, 0:nu], AF.Copy, [psk(b)], ["mqk"], scale=scl)
            for blk in range(1, 6):
                wt, wk = wload(wview(("bin", 0), wj, blk * 512, 512), NCH, 512)
                for ck in range(nchk):
                    b = next_ps()
                    cc0 = u0 + ck * Lc
                    for k in range(NCH):
                        P.mm(PS[b][0:Lc, :], hT[:, k, cc0:cc0 + Lc], wt[:, k, :], k == 0, k == NCH - 1, ["hT", wk], [psk(b)])
                    if blk == 1:
                        P.act(ktm[0:Lc, ck, :], PS[b][0:Lc, :], AF.Copy, [psk(b)], ["ktm"], scale=kscale)
                    elif blk < 4:
                        hh = (blk - 2) * 2
                        P.cp("dve", Vaug[0:Lc, ck, hh:hh + 2, 0:256], PS[b][0:Lc, :].rearrange("p (h v) -> p h v", v=256),
                             [psk(b)], ["Vaug"])
                    else:
                        o0 = (blk - 4) * 512
                        P.act(gsig[0:Lc, ck, o0:o0 + 512], PS[b][0:Lc, :], AF.Sigmoid, [psk(b)], ["gsig"])
            for ck in range(nchk):
                P.memset("pool", Vaug[0:Lc, ck, :, 256:257], 1.0, ["Vaug"])
            wt, wk = wload(wview(("bin", 0), wj, 3072, 8), NCH, 8)
            gb = 3
            for k in range(NCH):
                P.mm(PS[gb][0:4, 0:nu], wt[:, k, 0:4], hT[:, k, u0:u0 + nu], k == 0, k == NCH - 1, [wk, "hT"], [psk(gb)])
            for k in range(NCH):
                P.mm(PS[gb][0:4, 128:128 + nu], wt[:, k, 4:8], hT[:, k, u0:u0 + nu], k == 0, k == NCH - 1, [wk, "hT"], [psk(gb)])
            rI, rF, rA, rE, rL, rB, rU, rCM, rM, ra, rwr, remt = (L[x] for x in
                ("rI", "rF", "rA", "rE", "rL", "rB", "rU", "rCM", "rM", "ra", "rwr", "remt"))
            bg = L["bg"]
            P.act(rI[:, 0:nu], PS[gb][0:4, 0:nu], AF.Identity, [psk(gb), "bg"], ["rI"], bias=bg[:, 0:1], scale=1.0)
            P.act(rF[:, 0:nu], PS[gb][0:4, 128:128 + nu], AF.Identity, [psk(gb), "bg"], ["rF"], bias=bg[:, 1:2], scale=1.0)
            P.act(rA[:, 0:nu], rF[:, 0:nu], AF.Abs, ["rF"], ["rA"])
            P.act(rE[:, 0:nu], rA[:, 0:nu], AF.Exp, ["rA"], ["rE"], scale=-1.0)
            P.act(rL[:, 0:nu], rE[:, 0:nu], AF.Ln, ["rE", "one1"], ["rL"], bias=L["one1"][0:4, 0:1], scale=1.0)
            P.ts("dve", rA[:, 0:nu], rF[:, 0:nu], 0.0, None, ALU.min, None, ["rF", "rA"], ["rA"])
            P.tt("dve", rF[:, 0:nu], rA[:, 0:nu], rL[:, 0:nu], ALU.subtract, ["rA", "rL"], ["rF"])
            P.memset("dve", rE[:, 0:nu], 0.0, ["rE"])
            mrow = L["mrow"]
            for ck in range(nchk):
                a0, a1 = ck * Lc, (ck + 1) * Lc
                P.op("dve", lambda e: e.tensor_tensor_scan(out=rB[:, a0:a1], data0=rF[:, a0:a1], data1=rE[:, a0:a1],
                                                           initial=0.0, op0=ALU.add, op1=ALU.add), ["rF", "rE"], ["rB"])
                P.tt("dve", rU[:, a0:a1], rI[:, a0:a1], rB[:, a0:a1], ALU.subtract, ["rI", "rB"], ["rU"])
                P.op("dve", lambda e: e.tensor_tensor_scan(out=rCM[:, a0:a1], data0=rU[:, a0:a1], data1=rU[:, a0:a1],
                                                           initial=-1e30, op0=ALU.max, op1=ALU.max), ["rU"], ["rCM"])
                P.ts("dve", rM[:, a0:a1], rCM[:, a0:a1], mrow[:, 0:1], None, ALU.max, None, ["rCM", "mrow"], ["rM"])
                P.act(ra[:, a0:a1], rM[:, a0:a1], AF.Exp, ["rM", "mrow"], ["ra"], bias=mrow[:, 0:1], scale=-1.0)
                P.ts("dve", L["nML"][:], rM[:, a1 - 1:a1], -1.0, None, ALU.mult, None, ["rM"], ["nML"])
                P.act(rwr[:, a0:a1], rU[:, a0:a1], AF.Exp, ["rU", "nML"], ["rwr"], bias=L["nML"][:, 0:1], scale=1.0)
                P.tt("dve", remt[:, a0:a1], rB[:, a0:a1], rM[:, a0:a1], ALU.add, ["rB", "rM"], ["remt"])
                P.act(remt[:, a0:a1], remt[:, a0:a1], AF.Exp, ["remt"], ["remt"], scale=-1.0)
                P.tt("dve", mrow[:], rB[:, a1 - 1:a1], rM[:, a1 - 1:a1], ALU.add, ["rB", "rM", "ra"], ["mrow"])
                cb_ = 3
                for qi, rr in enumerate((rU, ra, rwr, remt)):
                    P.tr(PS[cb_][0:Lc, 256 + qi * 4:256 + qi * 4 + 4], rr[:, a0:a1], identF[0:4, 0:4],
                         ["rU", "ra", "rwr", "remt", "identF"], [psk(cb_)])
                cols = L["cols"]
                P.cp("dve", cols[0:Lc, :], PS[cb_][0:Lc, 256:272], [psk(cb_)], ["cols"])
                P.ts("dve", L["dg"][:], identF[0:4, 0:4], ra[:, a1 - 1:a1], None, ALU.mult, None, ["identF", "ra"], ["dg"])
                P.mm(PS[cb_][:, 280:284], onesF[0:4, :], L["dg"][:], True, True, ["onesF", "dg"], [psk(cb_)])
                P.cp("dve", L["dbc"][:], PS[cb_][:, 280:284], [psk(cb_)], ["dbc"])
                cq0 = ck * Lc
                for h in range(H_B):
                    P.mm(PS[4][0:Lc, 0:Lc], kT[:, h, cq0:cq0 + Lc], qT[:, h, cq0:cq0 + Lc], True, True, ["mqk"], [psk(4)])
                    P.mm(PS[4][0:Lc, 64:64 + Lc], L["sel"][:, h * 64:h * 64 + Lc], rM[:, a0:a1], True, True, ["sel", "rM"], [psk(4)])
                    P.ts("dve", L["z"][0:Lc, 0:Lc], PS[4][0:Lc, 64:64 + Lc], cols[0:Lc, h:h + 1], 0.0, ALU.subtract, ALU.max,
                         [psk(4), "cols"], ["mz"])
                    P.act(L["W"][0:Lc, 0:Lc], L["z"][0:Lc, 0:Lc], AF.Exp, ["mz"], ["mW"], scale=-1.0)
                    P.tt("pool", L["W"][0:Lc, 0:Lc], L["W"][0:Lc, 0:Lc], L["tri"][0:Lc, 0:Lc], ALU.mult, ["mW", "tri"], ["mW"])
                    P.tt("dve", L["Dm"][0:Lc, 0:Lc], PS[4][0:Lc, 0:Lc], L["W"][0:Lc, 0:Lc], ALU.mult, [psk(4), "mW"], ["mDm"])
                    P.mm(PS[5][0:Lc, 0:257], L["Dm"][0:Lc, 0:Lc], Vaug[0:Lc, ck, h, :], True, True, ["mDm", "Vaug"], [psk(5)])
                    P.mm(PS[6][0:Lc, 0:257], qT[:, h, cq0:cq0 + Lc], L["Cb"][:, h, :], True, True, ["mqk", "Cb"], [psk(6)])
                    P.act(L["itmp"][0:Lc, :], PS[6][0:Lc, 0:257], AF.Copy, [psk(6), "cols"], ["itmp"], scale=cols[0:Lc, 4 + h:5 + h])
                    P.tt("dve", L["ND"][0:Lc, :], L["itmp"][0:Lc, :], PS[5][0:Lc, 0:257], ALU.add, ["itmp", psk(5)], ["ND"])
                    sc = L["sc"]
                    P.act(sc[0:Lc, 6:7], L["ND"][0:Lc, 256:257], AF.Abs, ["ND"], ["msc6"])
                    P.tt("dve", sc[0:Lc, 0:1], sc[0:Lc, 6:7], cols[0:Lc, 12 + h:13 + h], ALU.max, ["msc6", "cols"], ["msc"])
                    P.op("dve", lambda e: e.reciprocal(out=sc[0:Lc, 1:2], in_=sc[0:Lc, 0:1]), ["msc"], ["msc"])
                    P.act(L["itmp"][0:Lc, 0:256], L["ND"][0:Lc, 0:256], AF.Square, ["ND", "msc"], ["itmp", "msc2"],
                          scale=sc[0:Lc, 1:2], accum=sc[0:Lc, 2:3])
                    P.act(sc[0:Lc, 3:4], sc[0:Lc, 2:3], AF.Sqrt, ["msc2", "epsT"], ["msc3"], bias=epsT[0:Lc, 0:1], scale=1.0 / 256)
                    P.op("dve", lambda e: e.reciprocal(out=sc[0:Lc, 4:5], in_=sc[0:Lc, 3:4]), ["msc3"], ["msc4"])
                    P.tt("dve", sc[0:Lc, 5:6], sc[0:Lc, 4:5], sc[0:Lc, 1:2], ALU.mult, ["msc4", "msc"], ["msc5"])
                    P.stt(L["hn"][0:Lc, h * 256:(h + 1) * 256], L["ND"][0:Lc, 0:256], sc[0:Lc, 5:6],
                          gsig[0:Lc, ck, h * 256:(h + 1) * 256], ALU.mult, ALU.mult, ["ND", "msc5", "gsig"], [("hn", h)])
                    P.ts("pool", L["kw"][0:Lc, :], ktm[0:Lc, ck, h * 128:(h + 1) * 128], cols[0:Lc, 8 + h:9 + h], None,
                         ALU.mult, None, ["ktm", "cols"], ["mkw"])
                    P.mm(PS[7][:, 0:257], L["kw"][0:Lc, :], Vaug[0:Lc, ck, h, :], True, True, ["mkw", "Vaug"], [psk(7)])
                    P.stt(L["Cst"][:, h, :], L["Cst"][:, h, :], L["dbc"][:, h:h + 1], PS[7][:, 0:257], ALU.mult, ALU.add,
                          ["Cst", "dbc", psk(7)], ["Cst"])
                    P.cp("act", L["Cb"][:, h, :], L["Cst"][:, h, :], ["Cst"], ["Cb"])
                tb = 2
                for q in range(NCH):
                    P.tr(psb(tb)[:, q * 64:q * 64 + Lc], L["hn"][0:Lc, q * 128:(q + 1) * 128], identB[0:Lc, 0:Lc],
                         [("hn", q // 2), "identB"], [psk(tb)])
                P.tt("dve", L["hnT"][:, :, u0 + a0:u0 + a1], psb(tb)[:, 0:512].rearrange("p (q t) -> p q t", t=64)[:, :, 0:Lc],
                     L["goutT"][:, :].unsqueeze(2).broadcast_to([128, NCH, Lc]), ALU.mult, [psk(tb), "goutT"], ["hnT"])

        def mixer_mlstm(gr, i, j, L):
            n = gr.n
            norm_mod(gr, i, 1)
            if gr.kind == "p":
                if gr.g == 0:
                    mlstm_state_init(L, None)
                for c0 in range(0, n, 64):
                    mlstm_unit(gr, L, c0, 64, 64)
                if gr.g == TP // G - 1:
                    mlstm_state_out(L, bcp, bnp, bmp)
            else:
                for s_ in range(NS):
                    mlstm_state_init(L, s_)
                    mlstm_unit(gr, L, s_ * TS, TS, TS)
                    mlstm_state_out(L, bcs[s_], bns[s_], bms[s_])
            for ch in range(2):
                wt, wk = wload(wview(("bout", j), w_b_out[j], ch * 512, 512), NCH, 512)
                for cc in range(4):
                    c = ch * 4 + cc
                    b = next_ps()
                    for q in range(NCH):
                        P.mm(PS[b][:, 0:n], wt[:, q, cc * 128:(cc + 1) * 128], L["hnT"][:, q, 0:n], q == 0, q == NCH - 1,
                             [wk, "hnT"], [psk(b)])
                    resid_add(gr, i, 1, c, PS[b], psk(b))

        def setup_mla(i, j, ls):
            L = {}
            L["QTn"] = sb("QTn", [128, H_C, G], BF16, ls)
            L["QTr"] = sb("QTr", [64, H_C, G], BF16, ls)
            L["onT"] = sb("c_onT", [128, H_C, G], BF16, ls)
            L["cq"] = sb("cq", [128, 3, G], F32, ls)
            L["cqn"] = sb("cqn", [128, 3, G], BF16, ls)
            L["gqT"] = sb("gqT", [128, 3], F32, ls)
            L["gkv"] = sb("gkv", [128, 256], F32, ls)
            L["rc"] = [sb("rc%d" % q, [128, 64], F32, ls) for q in range(2)]
            L["qb"] = sb("c_qb", [128, 384], BF16, ls)
            L["kvb"] = sb("kvb", [128, 256], BF16, ls)
            L["kvT"] = sb("kvT", [128, 2, 2, 128], BF16, ls)
            L["ob"] = [sb("c_ob%d" % q, [128, 512], BF16, ls) for q in range(2)]
            L["kb"] = [sb("c_kb%d" % q, [128, 128], BF16, ls) for q in range(2)]
            L["krb"] = [sb("c_krb%d" % q, [128, 64], BF16, ls) for q in range(2)]
            L["vb"] = [sb("c_vb%d" % q, [128, 128], BF16, ls) for q in range(2)]
            L["kT"] = [sb("c_kT%d" % q, [128, 128], BF16, ls) for q in range(2)]
            L["krT"] = [sb("c_krT%d" % q, [64, 128], BF16, ls) for q in range(2)]
            L["e"] = [sb("c_e%d" % q, [128, G], BF16, ls) for q in range(2)]
            L["r"] = sb("c_r", [128, G], F32, ls)
            L["sc"] = sb("c_sc", [128, 4], F32, ls)
            L["junk"] = sb("c_junk", [128, 256], F32, ls)
            L["n"] = {"rc": 0, "ob": 0}
            load_rows_T(L["gqT"][:], "gqT", g_c_q.rearrange("i (c p) -> (i c) p", p=128), 3)
            P.dma("sp", L["gkv"][:], g_c_kv[0, :].partition_broadcast(128), (), ["gkv"])
            return L

        def mla_up(L, kvT_ap, nt, dst_rows, dkey):
            for cb in range(4):
                wt, wk = wload(wview(("cukv", 0), w_c_ukv[0], cb * 512, 512), 2, 512)
                b = next_ps()
                for k in range(2):
                    P.mm(PS[b][0:nt, :], kvT_ap[:, k, 0:nt], wt[:, k, :], k == 0, k == 1, ["kvT", wk], [psk(b)])
                oi = L["n"]["ob"] % 2
                L["n"]["ob"] += 1
                P.cp("act", L["ob"][oi][0:nt, :], PS[b][0:nt, :], [psk(b)], [("cob", oi)])
                P.dma("sp", dst_rows[:, cb * 512:(cb + 1) * 512], L["ob"][oi][0:nt, :], [("cob", oi)], [dkey], semkey=("cobo", oi))

        def latent_T(L, src_bf_ap, nt, slot):
            tb = 2
            for k in range(2):
                P.tr(psb(tb)[:, k * 128:k * 128 + nt], src_bf_ap[:, k * 128:(k + 1) * 128], identB[0:nt, 0:nt],
                     ["kvb", "identB"], [psk(tb)])
            P.cp("dve", L["kvT"][:, slot, :, 0:nt], psb(tb)[:, 0:256].rearrange("p (k t) -> p k t", t=128)[:, :, 0:nt],
                 [psk(tb)], ["kvT"])

        def mixer_mla(gr, i, j, L):
            n = gr.n
            norm_mod(gr, i, 1)
            QTn, QTr, onT = L["QTn"], L["QTr"], L["onT"]
            kvdst = ckvp if gr.kind == "p" else ckvs
            krdst = ckrp if gr.kind == "p" else ckrs
            ckey = ("ckv", gr.kind, gr.g)
            wt, wk = wload(wview(("cdq", 0), w_c_dq[0], 0, Q_LORA), NCH, Q_LORA)
            sb_ = 3
            for f in range(3):
                b = next_ps()
                for k in range(NCH):
                    P.mm(PS[b][:, 0:n], wt[:, k, f * 128:(f + 1) * 128], hT[:, k, 0:n], k == 0, k == NCH - 1, [wk, "hT"], [psk(b)])
                P.cp("dve", L["cq"][:, f, 0:n], PS[b][:, 0:n], [psk(b)], ["cq"])
                si = st["sq"] % 2
                st["sq"] += 1
                P.act(sqr[si][:, 0:n], PS[b][:, 0:n], AF.Square, [psk(b)], [("sq", si)])
                P.mm(PS[sb_][:, 0:n], onesB[:], sqr[si][:, 0:n], f == 0, f == 2, [("sq", si), "onesB"], [psk(sb_)])
            P.act(L["r"][:, 0:n], PS[sb_][:, 0:n], AF.Sqrt, [psk(sb_), "epsT"], ["c_r"], bias=epsT[:, 0:1], scale=1.0 / Q_LORA)
            P.op("dve", lambda e: e.reciprocal(out=L["r"][:, 0:n], in_=L["r"][:, 0:n]), ["c_r"], ["c_r"])
            for f in range(3):
                P.stt(L["cqn"][:, f, 0:n], L["cq"][:, f, 0:n], L["gqT"][:, f:f + 1], L["r"][:, 0:n], ALU.mult, ALU.mult,
                      ["cq", "gqT", "c_r"], ["cqn"])
            for cb in range(4):
                wt, wk = wload(wview(("cuq", 0), w_c_uq[0], cb * 384, 384), 3, 384)
                for (c0, nt, ti) in gr.tiles:
                    b = next_ps()
                    for k in range(3):
                        P.mm(PS[b][0:nt, 0:384], L["cqn"][:, k, c0:c0 + nt], wt[:, k, 0:384], k == 0, k == 2, ["cqn", wk], [psk(b)])
                    s = next_sl(); sk = ("sl", s)
                    P.cp("act", SL[s][0:nt, 0:384], PS[b][0:nt, 0:384], [psk(b)], [sk])
                    ri = L["n"]["rc"] % 2
                    L["n"]["rc"] += 1
                    row0 = (gr.c0 + c0) if gr.kind == "p" else TP
                    P.dma("sp", L["rc"][ri][0:nt, :], rope_c[row0:row0 + nt, :], (), [("rc", ri)])
                    P.res["ropetab"] = P.res[("rc", ri)]
                    rope_slab(SL[s][0:nt, 0:384].rearrange("p (s d) -> p s d", d=192), sk, nt, L["rc"][ri][0:nt, 0:32],
                              L["rc"][ri][0:nt, 32:64], 32, 128, 2)
                    P.res[("rc", ri)] = P.res["ropetab"]
                    P.cp("act", L["qb"][0:nt, :], SL[s][0:nt, 0:384], [sk], ["c_qb"])
                    tb = 2
                    for hh in range(2):
                        P.tr(psb(tb)[:, hh * 128:hh * 128 + nt], L["qb"][0:nt, hh * 192:hh * 192 + 128], identB[0:nt, 0:nt],
                             ["c_qb", "identB"], [psk(tb)])
                        P.tr(psb(tb)[0:64, 256 + hh * 128:256 + hh * 128 + nt], L["qb"][0:nt, hh * 192 + 128:hh * 192 + 192],
                             identB[0:nt, 0:nt], ["c_qb", "identB"], [psk(tb)])
                    P.cp("dve", QTn[:, cb * 2:cb * 2 + 2, c0:c0 + nt],
                         psb(tb)[:, 0:256].rearrange("p (q t) -> p q t", t=128)[:, :, 0:nt], [psk(tb)], ["QTn"])
                    P.cp("dve", QTr[:, cb * 2:cb * 2 + 2, c0:c0 + nt],
                         psb(tb)[0:64, 256:512].rearrange("p (q t) -> p q t", t=128)[:, :, 0:nt], [psk(tb)], ["QTr"])
            for tix, (c0, nt, ti) in enumerate(gr.tiles):
                wt, wk = wload(wview(("cdkv", 0), w_c_dkv[0], 0, 320), NCH, 320)
                b = next_ps()
                for k in range(NCH):
                    P.mm(PS[b][0:nt, 0:320], hT[:, k, c0:c0 + nt], wt[:, k, 0:320], k == 0, k == NCH - 1, ["hT", wk], [psk(b)])
                s = next_sl(); sk = ("sl", s)
                P.cp("act", SL[s][0:nt, 0:320], PS[b][0:nt, 0:320], [psk(b)], [sk])
                sc = L["sc"]
                P.act(L["junk"][0:nt, :], SL[s][0:nt, 0:256], AF.Square, [sk], ["c_junk", "c_sc"], accum=sc[0:nt, 0:1])
                P.act(sc[0:nt, 1:2], sc[0:nt, 0:1], AF.Sqrt, ["c_sc", "epsT"], ["c_sc1"], bias=epsT[0:nt, 0:1], scale=1.0 / KV_LORA)
                P.op("dve", lambda e: e.reciprocal(out=sc[0:nt, 2:3], in_=sc[0:nt, 1:2]), ["c_sc1"], ["c_sc2"])
                P.stt(SL[s][0:nt, 0:256], SL[s][0:nt, 0:256], sc[0:nt, 2:3], L["gkv"][0:nt, :], ALU.mult, ALU.mult,
                      [sk, "c_sc2", "gkv"], [sk])
                ri = L["n"]["rc"] % 2
                L["n"]["rc"] += 1
                row0 = (gr.c0 + c0) if gr.kind == "p" else TP
                P.dma("sp", L["rc"][ri][0:nt, :], rope_c[row0:row0 + nt, :], (), [("rc", ri)])
                P.res["ropetab"] = P.res[("rc", ri)]
                rope_slab(SL[s][0:nt, 256:320].rearrange("p (s d) -> p s d", d=64), sk, nt, L["rc"][ri][0:nt, 0:32],
                          L["rc"][ri][0:nt, 32:64], 32, 0, 1)
                P.res[("rc", ri)] = P.res["ropetab"]
                P.dma("sp", kvdst[gr.c0 + c0:gr.c0 + c0 + nt, :], SL[s][0:nt, 0:256], [sk], [ckey], semkey=("slo", s))
                P.dma("sp", krdst[gr.c0 + c0:gr.c0 + c0 + nt, :], SL[s][0:nt, 256:320], [sk], [ckey], semkey=("slo", s))
                P.cp("act", L["kvb"][0:nt, :], SL[s][0:nt, 0:256], [sk], ["kvb"])
                latent_T(L, L["kvb"][0:nt, :], nt, 0)
                if gr.kind == "p":
                    mla_up(L, L["kvT"][:, 0, :, :], nt, mla_kv_p[gr.c0 + c0:gr.c0 + c0 + nt, :], ckey)
                else:
                    for cb in range(4):
                        wt2, wk2 = wload(wview(("cukv", 0), w_c_ukv[0], cb * 512, 512), 2, 512)
                        b2 = next_ps()
                        for k in range(2):
                            P.mm(PS[b2][0:nt, :], L["kvT"][:, 0, k, 0:nt], wt2[:, k, :], k == 0, k == 1, ["kvT", wk2], [psk(b2)])
                        oi = L["n"]["ob"] % 2
                        L["n"]["ob"] += 1
                        P.cp("act", L["ob"][oi][0:nt, :], PS[b2][0:nt, :], [psk(b2)], [("cob", oi)])
                        for s_ in range(NS):
                            P.dma("sp", mla_kv_s[s_, PAST:PAST + TS, cb * 512:(cb + 1) * 512], L["ob"][oi][s_ * TS:(s_ + 1) * TS, :],
                                  [("cob", oi)], [ckey], semkey=("cobo", oi))
            if gr.kind == "s":
                for s_ in range(NS):
                    for pt in range(NPT):
                        P.dma("pool", L["kvb"][:, :], cc_kv[s_, pt * 128:(pt + 1) * 128, :], (), ["kvb"])
                        latent_T(L, L["kvb"][:, :], 128, 1)
                        mla_up(L, L["kvT"][:, 1, :, :], 128, mla_kv_s[s_, pt * 128:(pt + 1) * 128, :], ("cpast", s_))
            scale = 192 ** -0.5
            for si, (r, a, nq) in enumerate(gr.segs):
                if gr.kind == "p":
                    nkt = (gr.c0 + n) // 128
                    ktl = []
                    for kt in range(nkt):
                        gk = (kt * 128) // G
                        ktl.append((mla_kv_p[kt * 128:(kt + 1) * 128, :], ckrp[kt * 128:(kt + 1) * 128, :], 128,
                                    kt * 128 - gr.c0, [("ckv", "p", gk)]))
                else:
                    s_ = si
                    ktl = []
                    for kt in range(NPT):
                        ktl.append((mla_kv_s[s_, kt * 128:(kt + 1) * 128, :], cc_kr[s_, kt * 128:(kt + 1) * 128, :], 128, -1,
                                    [("cpast", s_)]))
                    ktl.append((mla_kv_s[s_, PAST:PAST + TS, :], ckrs[s_ * TS:(s_ + 1) * TS, :], TS, -1, [ckey]))
                for hd in range(H_C):
                    first = True
                    for kti, (kvsrc, krsrc, nk, dcol, deps) in enumerate(ktl):
                        last = kti == len(ktl) - 1
                        cq0 = max(dcol, 0)
                        q2 = kti % 2
                        kb, krb, vb, kT, krT, e = L["kb"][q2], L["krb"][q2], L["vb"][q2], L["kT"][q2], L["krT"][q2], L["e"][q2]
                        P.dma("sp", kb[0:nk, :], kvsrc[:, hd * 256:hd * 256 + 128], deps, [("ckb", q2)])
                        P.dma("sp", vb[0:nk, :], kvsrc[:, hd * 256 + 128:hd * 256 + 256], deps, [("cvb", q2)])
                        P.dma("pool", krb[0:nk, :], krsrc, deps, [("ckrb", q2)])
                        tb = 2
                        P.tr(psb(tb)[:, 0:nk], kb[0:nk, :], identB[0:nk, 0:nk], [("ckb", q2), "identB"], [psk(tb)])
                        P.tr(psb(tb)[0:64, 128:128 + nk], krb[0:nk, :], identB[0:nk, 0:nk], [("ckrb", q2), "identB"], [psk(tb)])
                        P.cp("dve", kT[:, 0:nk], psb(tb)[:, 0:nk], [psk(tb)], [("ckT", q2)])
                        P.cp("dve", krT[:, 0:nk], psb(tb)[0:64, 128:128 + nk], [psk(tb)], [("ckrT", q2)])
                        qa, qn = a + cq0, nq - cq0
                        P.mm(PS[0][0:nk, 0:qn], kT[:, 0:nk], QTn[:, hd, qa:qa + qn], True, False, [("ckT", q2), "QTn"], [psk(0)])
                        P.mm(PS[0][0:nk, 0:qn], krT[0:64, 0:nk], QTr[0:64, hd, qa:qa + qn], False, True, [("ckrT", q2), "QTr"], [psk(0)])
                        P.act(e[0:nk, 0:qn], PS[0][0:nk, 0:qn], AF.Exp, [psk(0)], [("ce", q2)], scale=scale)
                        if dcol >= 0:
                            P.memset("pool", e[64:128, 0:64], 0.0, [("ce", q2)])
                        P.mm(PS[4][:, cq0:nq], vb[0:nk, :], e[0:nk, 0:qn], first, last, [("cvb", q2), ("ce", q2)], [psk(4)])
                        P.mm(PS[6][:, cq0:nq], onesB[0:nk, :], e[0:nk, 0:qn], first, last, ["onesB", ("ce", q2)], [psk(6)])
                        first = False
                    P.op("dve", lambda e_: e_.reciprocal(out=L["r"][:, 0:nq], in_=PS[6][:, 0:nq]), [psk(6)], ["c_r"])
                    P.tt("dve", onT[:, hd, a:a + nq], PS[4][:, 0:nq], L["r"][:, 0:nq], ALU.mult, [psk(4), "c_r"], ["c_onT"])
            for ch in range(2):
                wt, wk = wload(wview(("co", 0), w_c_o[0], ch * 512, 512), NCH, 512)
                for cc in range(4):
                    c = ch * 4 + cc
                    b = next_ps()
                    for hd in range(H_C):
                        P.mm(PS[b][:, 0:n], wt[:, hd, cc * 128:(cc + 1) * 128], onT[:, hd, 0:n], hd == 0, hd == H_C - 1,
                             [wk, "c_onT"], [psk(b)])
                    resid_add(gr, i, 1, c, PS[b], psk(b))

        def prep_layer(i):
            kind, j = i % 3, i // 3
            sp_ = []
            if kind == 0:
                sp_ += [wview(("aqkv", j), w_a_qkv[j], cb * 512, 512) for cb in range(6)]
                sp_ += [wview(("ao", j), w_a_o[j], ch * 512, 512) for ch in range(2)]
            elif kind == 1:
                sp_ += [wview(("bin", 0), w_b_in[0], blk * 512, 512) for blk in range(6)]
                sp_ += [wview(("bin", 0), w_b_in[0], 3072, 8)]
                sp_ += [wview(("bout", j), w_b_out[j], ch * 512, 512) for ch in range(2)]
            else:
                sp_ += [wview(("cdq", 0), w_c_dq[0], 0, Q_LORA), wview(("cdkv", 0), w_c_dkv[0], 0, 320)]
                sp_ += [wview(("cuq", 0), w_c_uq[0], cb * 384, 384) for cb in range(4)]
                sp_ += [wview(("cukv", 0), w_c_ukv[0], cb * 512, 512) for cb in range(4)]
                sp_ += [wview(("co", 0), w_c_o[0], ch * 512, 512) for ch in range(2)]
            for half in range(2):
                f0 = half * 11
                for (fb, nf) in [(f0, 4), (f0 + 4, 4), (f0 + 8, 3)]:
                    sp_.append(wview(("fin", i), w_ffn_in[i], fb * 128, nf * 128))
                    sp_.append(wview(("fin", i), w_ffn_in[i], D_FF + fb * 128, nf * 128))
                for ch in range(2):
                    for (k0, nk) in ((0, 8), (8, 3)):
                        sp_.append(wview(("fout", i), w_ffn_out[i], ch * 512, 512, k0=f0 + k0, nk=nk))
            for spc in sp_:
                wprep(spc, i)

        prep_layer(cfg.layers[0])
        for li, i in enumerate(cfg.layers):
            kind, j = i % 3, i // 3
            P.barrier()
            cur_layer[0] = i
            if li + 1 < len(cfg.layers):
                prep_layer(cfg.layers[li + 1])
            with ExitStack() as ls:
                if kind == 0:
                    L = setup_diff(i, j, ls)
                    for gr in groups:
                        mixer_diff(gr, i, j, L)
                        ffn(gr, i)
                elif kind == 1:
                    L = setup_mlstm(i, j, ls)
                    for gr in groups:
                        mixer_mlstm(gr, i, j, L)
                        ffn(gr, i)
                else:
                    L = setup_mla(i, j, ls)
                    for gr in groups:
                        mixer_mla(gr, i, j, L)
                        ffn(gr, i)
                P.barrier()

        P.barrier()
        hT_f = sb("hT_f", [128, NCH, G])
        for gr in groups:
            final_out(gr, y_p if gr.kind == "p" else y_s)
        P.finish()
        print("ops", P.nops, "waits", P.nwait, "dma sems", P.nsem, "cnt", P.cnt, flush=True)
    return nc


def rope_table(pos, rot):
    half = rot // 2
    inv = np.power(np.float32(ROPE_THETA), -np.arange(half, dtype=np.float32) * np.float32(2.0 / rot)).astype(np.float32)
    ang = pos.astype(np.float32)[:, None] * inv[None, :]
    return np.concatenate([np.cos(ang), np.sin(ang)], axis=1).astype(np.float32)


def host_consts(cfg):
    tri = (np.arange(64)[:, None] <= np.arange(64)[None, :]).astype(np.float32)
    sel = np.zeros((4, 4, 64), np.float32)
    for h in range(4):
        sel[h, h, :] = 1.0
    pos = np.concatenate([np.arange(cfg.TP), np.tile(cfg.PAST + np.arange(cfg.TS), cfg.NS)])
    return {
        "k_ident": np.eye(128, dtype=np.float32),
        "k_tri": tri,
        "k_sel": sel.reshape(4, 256),
        "rope_a": rope_table(pos, 16),
        "rope_c": rope_table(pos, 64),
    }


def make_in_maps(cfg, inp, n_cores=8):
    f = lambda a: np.ascontiguousarray(np.asarray(a, dtype=np.float32))
    NS, TS, TP, PAST = cfg.NS, cfg.TS, cfg.TP, cfg.PAST
    consts = host_consts(cfg)
    shared = {k: f(inp[k]) for k in ("w_ada", "b_ada", "g_norm1", "g_norm2", "w_a_qkv", "a_lambda", "g_a_sub", "w_a_o",
                                     "w_b_in", "b_b_gates", "g_b_out", "w_b_out", "w_c_dq", "g_c_q", "w_c_uq", "w_c_dkv",
                                     "g_c_kv", "w_c_ukv", "w_c_o", "w_ffn_in", "w_ffn_out")}
    shared["g_final"] = f(inp["g_final"]).reshape(1, D)
    shared.update(consts)
    nb = inp["x_prompt"].shape[0]
    maps = []
    for c in range(n_cores):
        b = c % nb
        ss = slice(NS * c, NS * c + NS)
        m = dict(shared)
        m["x_p"] = f(inp["x_prompt"][b])
        m["x_s"] = f(inp["x_sample"][ss]).reshape(NS * TS, D)
        m["c_in"] = f(np.concatenate([np.asarray(inp["c_prompt"])[b:b + 1], np.asarray(inp["c_sample"])[ss]], axis=0))
        m["ca_k"] = f(inp["cache_a_k"][:, ss]).reshape(2, NS, PAST, D)
        m["ca_v"] = f(inp["cache_a_v"][:, ss]).reshape(2, NS, PAST, D)
        m["sb_c"] = f(inp["state_b_c"][0, ss])
        m["sb_n"] = f(inp["state_b_n"][0, ss])
        m["sb_m"] = f(inp["state_b_m"][0, ss])
        m["cc_kv"] = f(inp["cache_c_kv"][0, ss])
        m["cc_kr"] = f(inp["cache_c_kr"][0, ss])
        maps.append(m)
    return maps


_NC_CACHE = {}


def kernel(**inputs):
    cfg = Cfg()
    if "nc" not in _NC_CACHE:
        _NC_CACHE["nc"] = build(cfg)
    nc = _NC_CACHE["nc"]
    maps = make_in_maps(cfg, inputs)
    res = run_bass_kernel_spmd(nc, maps, core_ids=list(range(8))).results
    NS, TS, TP = cfg.NS, cfg.TS, cfg.TP
    B = 4

    def pst(name, cores, shape=None):
        a = np.stack([np.asarray(res[c][name], dtype=np.float32) for c in cores], axis=0)
        return a

    pc = list(range(B))
    ac = list(range(8))
    y_prompt = pst("y_p", pc)
    y_sample = pst("y_s", ac).reshape(8 * NS, TS, D)
    a_k_p = np.moveaxis(pst("akp", pc), 0, 1).reshape(2, B, TP, H_A, 128)
    a_v_p = np.moveaxis(pst("avp", pc), 0, 1).reshape(2, B, TP, H_A, 128)
    b_c_p = pst("bcp", pc)[None]
    b_n_p = pst("bnp", pc)[None]
    b_m_p = pst("bmp", pc).reshape(1, B, H_B)
    c_kv_p = pst("ckvp", pc)[None]
    c_kr_p = pst("ckrp", pc)[None]
    a_k_s = np.moveaxis(pst("aks", ac).reshape(8, 2, NS, TS, D), 1, 0).reshape(2, 8 * NS, TS, H_A, 128)
    a_v_s = np.moveaxis(pst("avs", ac).reshape(8, 2, NS, TS, D), 1, 0).reshape(2, 8 * NS, TS, H_A, 128)
    b_c_s = pst("bcs", ac).reshape(1, 8 * NS, H_B, DQK_B, DV_B)
    b_n_s = pst("bns", ac).reshape(1, 8 * NS, H_B, DQK_B)
    b_m_s = pst("bms", ac).reshape(1, 8 * NS, H_B)
    c_kv_s = pst("ckvs", ac).reshape(1, 8 * NS, TS, KV_LORA)
    c_kr_s = pst("ckrs", ac).reshape(1, 8 * NS, TS, 64)
    outs = (y_prompt, y_sample, a_k_p, a_v_p, b_c_p, b_n_p, b_m_p, c_kv_p, c_kr_p,
            a_k_s, a_v_s, b_c_s, b_n_s, b_m_s, c_kv_s, c_kr_s)
    return tuple(np.ascontiguousarray(o, dtype=np.float32) for o in outs)
```

```python
import math
import os
from contextlib import ExitStack
import numpy as np
import concourse.bass as bass
import concourse.mybir as mybir
from concourse.bass_utils import run_bass_kernel_spmd

F32 = mybir.dt.float32
BF16 = mybir.dt.bfloat16
AF = mybir.ActivationFunctionType
ALU = mybir.AluOpType
AX = mybir.AxisListType

D = 1024
NCH = 8
EPS = 1e-6
ROPE_THETA = 500000.0
DEPTH = 4
H_A = 8
H_B = 4
DQK_B = 128
DV_B = 256
H_C = 8
Q_LORA = 384
KV_LORA = 256
D_FF = 2816
NFF = 22
N_B_IN = 3080


class Cfg:
    def __init__(self, TP=4096, G=256, PAST=2048, TS=32, NS=2, layers=(0, 1, 2, 3)):
        self.TP, self.G, self.PAST, self.TS, self.NS = TP, G, PAST, TS, NS
        self.layers = tuple(layers)


class Prog:
    ENGS = ("pe", "act", "dve", "pool", "sp")

    def __init__(self, nc, es):
        self.nc, self.es = nc, es
        self.eobj = {"pe": nc.tensor, "act": nc.scalar, "dve": nc.vector, "pool": nc.gpsimd, "sp": nc.sync}
        self.cnt = {e: 0 for e in self.ENGS}
        self.esem = {e: es.enter_context(nc.semaphore("s_" + e)) for e in self.ENGS}
        self.seen = {e: {} for e in self.ENGS}
        self.res = {}
        self.dsem = {}
        self.nsem = 0
        self.nwait = 0
        self.nops = 0
        self.trace = {e: [] for e in self.ENGS}

    def _waits(self, eng, r, w):
        toks = []
        for k in r:
            st = self.res.get(k)
            if st is not None and st[0] is not None:
                toks.append(st[0])
        for k in w:
            st = self.res.get(k)
            if st is not None:
                if st[0] is not None:
                    toks.append(st[0])
                toks.extend(st[1])
        e = self.eobj[eng]
        seen = self.seen[eng]
        for (name, sem, val, src) in toks:
            if src == eng and eng == "pe":
                continue
            if seen.get(name, 0) >= val:
                continue
            seen[name] = val
            e.wait_ge(sem, val)
            self.trace[eng].append(("w", name, val))
            self.nwait += 1

    def _commit(self, tok, r, w):
        for k in r:
            st = self.res.get(k)
            if st is None:
                st = [None, []]
                self.res[k] = st
            st[1].append(tok)
        for k in w:
            self.res[k] = [tok, []]

    def op(self, eng, fn, r=(), w=()):
        if eng != "pe":
            psr = [k for k in r if isinstance(k, tuple) and k[0] == "ps"]
            if psr:
                r = [k for k in r if k not in psr]
                w = list(w) + psr
        self._waits(eng, r, w)
        inst = fn(self.eobj[eng])
        self.cnt[eng] += 1
        inst.then_inc(self.esem[eng], 1)
        self.trace[eng].append(("i", "s_" + eng, 1))
        tok = ("s_" + eng, self.esem[eng], self.cnt[eng], eng)
        self._commit(tok, r, w)
        self.nops += 1

    def dma(self, q, out, in_, r=(), w=(), semkey=None, acc=False):
        self._waits(q, r, () if acc else w)
        if semkey is None:
            semkey = w[0] if len(w) else r[0]
        if semkey not in self.dsem:
            self.dsem[semkey] = [self.es.enter_context(self.nc.semaphore("d%d" % self.nsem)), 0]
            self.nsem += 1
        ds = self.dsem[semkey]
        inst = self.eobj[q].dma_start(out=out, in_=in_)
        ds[1] += 16
        inst.then_inc(ds[0], 16)
        self.trace[q].append(("i", "d" + str(semkey), 16))
        tok = ("d" + str(semkey), ds[0], ds[1], None)
        self._commit(tok, r, w)
        self.nops += 1

    def barrier(self):
        for e in self.ENGS:
            eo = self.eobj[e]
            for e2 in self.ENGS:
                if e2 != e and self.cnt[e2] > self.seen[e].get("s_" + e2, 0):
                    eo.wait_ge(self.esem[e2], self.cnt[e2])
                    self.trace[e].append(("w", "s_" + e2, self.cnt[e2]))
                    self.seen[e]["s_" + e2] = self.cnt[e2]
            for k, ds in self.dsem.items():
                nm = "d" + str(k)
                if ds[1] > self.seen[e].get(nm, 0):
                    eo.wait_ge(ds[0], ds[1])
                    self.trace[e].append(("w", nm, ds[1]))
                    self.seen[e][nm] = ds[1]

    def finish(self):
        eo = self.eobj["sp"]
        for k, ds in self.dsem.items():
            eo.wait_ge(ds[0], ds[1])
            self.trace["sp"].append(("w", "d" + str(k), ds[1]))
        self.check_deadlock()

    def check_deadlock(self):
        val = {}
        ptr = {e: 0 for e in self.ENGS}
        prog = True
        while prog:
            prog = False
            for e in self.ENGS:
                tr = self.trace[e]
                while ptr[e] < len(tr):
                    k, nm, v = tr[ptr[e]]
                    if k == "w":
                        if val.get(nm, 0) >= v:
                            ptr[e] += 1
                            prog = True
                        else:
                            break
                    else:
                        val[nm] = val.get(nm, 0) + v
                        ptr[e] += 1
                        prog = True
        bad = {e: (ptr[e], len(self.trace[e]), self.trace[e][ptr[e]]) for e in self.ENGS if ptr[e] < len(self.trace[e])}
        if bad:
            raise RuntimeError("DEADLOCK in schedule: %r ; sem values: %r" % (bad, {k: val.get(k, 0) for k in [b[2][1] for b in bad.values()]}))

    def mm(self, out, lhsT, rhs, start, stop, r, w):
        self.op("pe", lambda e: e.matmul(out, lhsT=lhsT, rhs=rhs, start=start, stop=stop), r, w)

    def tr(self, out, in_, ident, r, w):
        self.op("pe", lambda e: e.transpose(out, in_, ident), r, w)

    def act(self, out, in_, func, r, w, bias=None, scale=None, accum=None):
        kw = {}
        if bias is not None:
            kw["bias"] = bias
        if scale is not None:
            kw["scale"] = scale
        if accum is not None:
            kw["accum_out"] = accum
        self.op("act", lambda e: e.activation(out=out, in_=in_, func=func, **kw), r, w)

    def tt(self, eng, out, a, b, op, r, w):
        self.op(eng, lambda e: e.tensor_tensor(out=out, in0=a, in1=b, op=op), r, w)

    def ts(self, eng, out, a, s1, s2, op0, op1, r, w):
        if op1 is None:
            self.op(eng, lambda e: e.tensor_scalar(out=out, in0=a, scalar1=s1, scalar2=None, op0=op0), r, w)
        else:
            self.op(eng, lambda e: e.tensor_scalar(out=out, in0=a, scalar1=s1, scalar2=s2, op0=op0, op1=op1), r, w)

    def stt(self, out, a, s, b, op0, op1, r, w):
        self.op("dve", lambda e: e.scalar_tensor_tensor(out=out, in0=a, scalar=s, in1=b, op0=op0, op1=op1), r, w)

    def cp(self, eng, out, in_, r, w):
        if eng == "act":
            self.op("act", lambda e: e.copy(out=out, in_=in_), r, w)
        else:
            self.op(eng, lambda e: e.tensor_copy(out=out, in_=in_), r, w)

    def memset(self, eng, ap, val, w):
        self.op(eng, lambda e: e.memset(ap, val), (), w)


def build(cfg):
    nc = bass.Bass("TRN2", target_bir_lowering=False)
    TP, G, PAST, TS, NS = cfg.TP, cfg.G, cfg.PAST, cfg.TS, cfg.NS
    NSK = NS * TS
    NTP = TP // 128
    NR = 1 + NS
    NPT = PAST // 128

    def din(name, shape, dt=F32):
        return nc.dram_tensor(name, list(shape), dt, kind="ExternalInput").ap()

    def dout(name, shape, dt=F32):
        return nc.dram_tensor(name, list(shape), dt, kind="ExternalOutput").ap()

    x_p = din("x_p", [TP, D]); x_s = din("x_s", [NSK, D]); c_in = din("c_in", [NR, D])
    ca_k = din("ca_k", [2, NS, PAST, D]); ca_v = din("ca_v", [2, NS, PAST, D])
    sb_c = din("sb_c", [NS, H_B, DQK_B, DV_B]); sb_n = din("sb_n", [NS, H_B, DQK_B]); sb_m = din("sb_m", [NS, H_B])
    cc_kv = din("cc_kv", [NS, PAST, KV_LORA]); cc_kr = din("cc_kr", [NS, PAST, 64])
    w_ada = din("w_ada", [DEPTH, D, 6 * D]); b_ada = din("b_ada", [DEPTH, 6 * D])
    g_norm1 = din("g_norm1", [DEPTH, D]); g_norm2 = din("g_norm2", [DEPTH, D])
    w_a_qkv = din("w_a_qkv", [2, D, 3 * D]); a_lambda = din("a_lambda", [2, 4, 64]); g_a_sub = din("g_a_sub", [2, 128])
    w_a_o = din("w_a_o", [2, D, D])
    w_b_in = din("w_b_in", [1, D, N_B_IN]); b_b_gates = din("b_b_gates", [1, 8]); g_b_out = din("g_b_out", [1, D])
    w_b_out = din("w_b_out", [1, D, D])
    w_c_dq = din("w_c_dq", [1, D, Q_LORA]); g_c_q = din("g_c_q", [1, Q_LORA]); w_c_uq = din("w_c_uq", [1, Q_LORA, 1536])
    w_c_dkv = din("w_c_dkv", [1, D, 320]); g_c_kv = din("g_c_kv", [1, KV_LORA]); w_c_ukv = din("w_c_ukv", [1, KV_LORA, 2048])
    w_c_o = din("w_c_o", [1, D, D])
    w_ffn_in = din("w_ffn_in", [DEPTH, D, 2 * D_FF]); w_ffn_out = din("w_ffn_out", [DEPTH, D_FF, D])
    g_final = din("g_final", [1, D])
    k_ident = din("k_ident", [128, 128])
    k_tri = din("k_tri", [64, 64])
    k_sel = din("k_sel", [4, 4 * 64])
    rope_a = din("rope_a", [TP + NSK, 16])
    rope_c = din("rope_c", [TP + NSK, 64])

    y_p = dout("y_p", [TP, D]); y_s = dout("y_s", [NSK, D])
    akp = dout("akp", [2, TP, D]); avp = dout("avp", [2, TP, D])
    bcp = dout("bcp", [H_B, DQK_B, DV_B]); bnp = dout("bnp", [H_B, DQK_B]); bmp = dout("bmp", [H_B, 1])
    ckvp = dout("ckvp", [TP, KV_LORA]); ckrp = dout("ckrp", [TP, 64])
    aks = dout("aks", [2, NSK, D]); avs = dout("avs", [2, NSK, D])
    bcs = dout("bcs", [NS, H_B, DQK_B, DV_B]); bns = dout("bns", [NS, H_B, DQK_B]); bms = dout("bms", [NS, H_B, 1])
    ckvs = dout("ckvs", [NSK, KV_LORA]); ckrs = dout("ckrs", [NSK, 64])
    NWT = 200
    wsc = nc.dram_tensor("wsc", [NWT, 128, 4096], BF16, kind="Internal").ap()
    mla_kv_p = nc.dram_tensor("mla_kv_p", [TP, 2048], BF16, kind="Internal").ap()
    mla_kv_s = nc.dram_tensor("mla_kv_s", [NS, PAST + TS, 2048], BF16, kind="Internal").ap()

    es = ExitStack()
    with es:
        P = Prog(nc, es)

        uniq = [0]

        def sb(name, shape, dt=F32, stack=None):
            if stack is not None:
                uniq[0] += 1
                name = "%s_u%d" % (name, uniq[0])
            return (stack or es).enter_context(nc.sbuf_tensor(name, list(shape), dt))

        PS = [es.enter_context(nc.psum_tensor("ps%d" % i, [128, 512], F32)) for i in range(8)]

        def psk(i):
            return ("ps", i)

        def psb(i):
            return PS[i][:].bitcast(BF16)

        xTp = sb("xTp", [128, NCH, TP])
        xTs = sb("xTs", [128, NCH, NSK])
        identF = sb("identF", [128, 128]); identB = sb("identB", [128, 128], BF16)
        onesB = sb("onesB", [128, 128], BF16); onesF = sb("onesF", [128, 128])
        hT = sb("hT", [128, NCH, G], BF16)
        sqr = [sb("sq%d" % i, [128, G], BF16) for i in range(2)]
        tmpN = [sb("tmpN%d" % i, [128, G]) for i in range(2)]
        rstd = sb("rstd", [128, G])
        NWB = 2
        WB = [sb("wb%d" % i, [128, NCH, 512], BF16) for i in range(NWB)]
        NSL = 4
        SL = [sb("sl%d" % i, [128, 512]) for i in range(NSL)]
        modT = sb("modT", [128, DEPTH, 48, NR])
        A1 = sb("A1", [128, DEPTH, NCH, NR]); A2 = sb("A2", [128, DEPTH, NCH, NR])
        gn1T = sb("gn1T", [128, DEPTH * NCH]); gn2T = sb("gn2T", [128, DEPTH * NCH]); gfT = sb("gfT", [128, NCH])
        ropeA = sb("ropeA", [128, NTP + 1, 16])
        epsT = sb("epsT", [128, 1])
        uT = sb("uT", [128, 11, G], BF16)

        st = {"wb": 0, "sl": 0, "sq": 0, "tn": 0, "pa": 0}

        P.dma("sp", identF[:], k_ident[:, :], (), ["identF"])
        P.cp("dve", identB[:], identF[:], ["identF"], ["identB"])
        P.memset("dve", onesB[:], 1.0, ["onesB"])
        P.memset("dve", onesF[:], 1.0, ["onesF"])
        P.memset("dve", epsT[:], EPS, ["epsT"])

        class Grp:
            pass
        groups = []
        for g in range(TP // G):
            gr = Grp()
            gr.kind = "p"; gr.g = g; gr.n = G; gr.xT = xTp; gr.c0 = g * G
            gr.segs = [(0, 0, G)]
            gr.tiles = [(t * 128, 128, g * (G // 128) + t) for t in range(G // 128)]
            groups.append(gr)
        gs = Grp()
        gs.kind = "s"; gs.g = 0; gs.n = NSK; gs.xT = xTs; gs.c0 = 0
        gs.segs = [(1 + s, s * TS, TS) for s in range(NS)]
        gs.tiles = [(0, NSK, NTP)]
        groups.append(gs)

        def xview(gr, c, a=0, n=None):
            n = gr.n - a if n is None else n
            return gr.xT[:, c, gr.c0 + a:gr.c0 + a + n]

        def xkey(gr):
            return ("x", gr.kind, gr.g)

        wtiles = {}
        cur_layer = [0]

        def wview(wname, w2d, c0, ncols, k0=0, nk=None):
            v = w2d.rearrange("(kc p) n -> p kc n", p=128)
            nk = v.shape[1] - k0 if nk is None else nk
            return (wname, c0, ncols, k0, nk, v[:, k0:k0 + nk, c0:c0 + ncols])

        def wprep(spec, layer):
            (wname, c0, ncols, k0, nk, view) = spec
            tk = (wname, c0, ncols, k0, nk)
            if tk in wtiles:
                return wtiles[tk]
            t = len(wtiles)
            assert t < NWT
            dst = wsc[t, :, 0:nk * ncols].rearrange("p (k n) -> p k n", n=ncols)
            ck = ("wcv", layer)
            P.dma("pool", dst, view, (), [ck], acc=True)
            wtiles[tk] = (dst, ck)
            return wtiles[tk]

        def wload(spec, nk, ncols):
            src, ck = wprep(spec, cur_layer[0])
            i = st["wb"] % NWB
            st["wb"] += 1
            key = ("wb", i)
            P.dma("pool", WB[i][:, 0:nk, 0:ncols], src, [ck], [key])
            return WB[i], key

        PS_ROT = (0, 1, 6, 7) if os.environ.get('EXP_ROT', '4') == '4' else (0, 1, 0, 1)

        def next_ps():
            i = PS_ROT[st["pa"] % 4]
            st["pa"] += 1
            return i

        def next_sl():
            i = st["sl"] % NSL
            st["sl"] += 1
            return i

        def load_rows_T(dst, dkey, src_rows_ap, nrows):
            i = next_sl()
            P.dma("sp", SL[i][0:nrows, 0:128], src_rows_ap, (), [("sl", i)])
            b = next_ps()
            P.tr(PS[b][:, 0:nrows], SL[i][0:nrows, 0:128], identF[0:nrows, 0:nrows], [("sl", i), "identF"], [psk(b)])
            P.cp("dve", dst, PS[b][:, 0:nrows], [psk(b)], [dkey])

        cT = sb("cT", [128, NCH, NR]); scT = sb("scT", [128, NCH, NR], BF16)
        for k in range(NCH):
            load_rows_T(cT[:, k, :], "cT", c_in[:, k * 128:(k + 1) * 128], NR)
        P.act(scT[:], cT[:], AF.Silu, ["cT"], ["scT"])
        bT = sb("bT", [128, DEPTH, 48])
        for i in range(DEPTH):
            load_rows_T(bT[:, i, :], "bT", b_ada[i, :].rearrange("(j p) -> j p", p=128), 48)
        load_rows_T(gn1T[:], "gn1T", g_norm1.rearrange("i (c p) -> (i c) p", p=128), DEPTH * NCH)
        load_rows_T(gn2T[:], "gn2T", g_norm2.rearrange("i (c p) -> (i c) p", p=128), DEPTH * NCH)
        load_rows_T(gfT[:], "gfT", g_final.rearrange("i (c p) -> (i c) p", p=128), NCH)
        P.dma("sp", ropeA[:, 0:NTP, :], rope_a[0:TP, :].rearrange("(t p) f -> p t f", p=128), (), ["ropetab"])
        P.dma("sp", ropeA[0:NSK, NTP, :], rope_a[TP:TP + NSK, :], (), ["ropetab"], semkey="ropetab2")

        for i in cfg.layers:
            mb = 7
            cur_layer[0] = ("ada", i)
            for cb in range(12):
                wt, wk = wload(wview(("ada", i), w_ada[i], cb * 512, 512), NCH, 512)
                for fc in range(4):
                    j = cb * 4 + fc
                    for k in range(NCH):
                        P.mm(PS[mb][:, j * NR:(j + 1) * NR], wt[:, k, fc * 128:(fc + 1) * 128], scT[:, k, :],
                             k == 0, k == NCH - 1, [wk, "scT"], [psk(mb)])
            P.tt("dve", modT[:, i, :, :], PS[mb][:, 0:48 * NR].rearrange("p (j r) -> p j r", r=NR),
                 bT[:, i, :].unsqueeze(2).broadcast_to([128, 48, NR]), ALU.add, [psk(mb), "bT"], ["modT"])
            P.stt(A1[:, i, :, :], modT[:, i, 8:16, :], 1.0,
                  gn1T[:, i * NCH:(i + 1) * NCH].unsqueeze(2).broadcast_to([128, NCH, NR]), ALU.add, ALU.mult,
                  ["modT", "gn1T"], ["A1"])
            P.stt(A2[:, i, :, :], modT[:, i, 32:40, :], 1.0,
                  gn2T[:, i * NCH:(i + 1) * NCH].unsqueeze(2).broadcast_to([128, NCH, NR]), ALU.add, ALU.mult,
                  ["modT", "gn2T"], ["A2"])

        def sh(i, which, c, r):
            base = 0 if which == 1 else 24
            return modT[:, i, base + c, r:r + 1]

        def gt(i, which, c, r):
            base = 16 if which == 1 else 40
            return modT[:, i, base + c, r:r + 1]

        def Ac(i, which, c, r):
            return (A1 if which == 1 else A2)[:, i, c, r:r + 1]

        def load_xT(gr, src):
            for (c0, nt, _ti) in gr.tiles:
                for half in range(2):
                    i = next_sl()
                    P.dma("sp", SL[i][0:nt, :], src[gr.c0 + c0:gr.c0 + c0 + nt, half * 512:(half + 1) * 512], (), [("sl", i)])
                    b = next_ps()
                    for q in range(4):
                        P.tr(PS[b][:, q * 128:q * 128 + nt], SL[i][0:nt, q * 128:(q + 1) * 128], identF[0:nt, 0:nt],
                             [("sl", i), "identF"], [psk(b)])
                    P.cp("act" if half else "dve", gr.xT[:, half * 4:half * 4 + 4, gr.c0 + c0:gr.c0 + c0 + nt],
                         PS[b][:].rearrange("p (q t) -> p q t", t=128)[:, :, 0:nt], [psk(b)], [xkey(gr)])

        for gr in groups:
            load_xT(gr, x_p if gr.kind == "p" else x_s)

        def norm_stats(gr):
            n = gr.n
            b = next_ps()
            for c in range(NCH):
                i = st["sq"] % 2
                st["sq"] += 1
                P.act(sqr[i][:, 0:n], xview(gr, c), AF.Square, [xkey(gr)], [("sq", i)])
                P.mm(PS[b][:, 0:n], onesB[:], sqr[i][:, 0:n], c == 0, c == NCH - 1, [("sq", i), "onesB"], [psk(b)])
            P.act(rstd[:, 0:n], PS[b][:, 0:n], AF.Sqrt, [psk(b), "epsT"], ["rstd"], bias=epsT[:, 0:1], scale=1.0 / D)
            P.op("dve", lambda e: e.reciprocal(out=rstd[:, 0:n], in_=rstd[:, 0:n]), ["rstd"], ["rstd"])

        def norm_mod(gr, i, which):
            norm_stats(gr)
            for c in range(NCH):
                for (r, a, n) in gr.segs:
                    t = st["tn"] % 2
                    st["tn"] += 1
                    P.stt(tmpN[t][:, 0:n], xview(gr, c, a, n), Ac(i, which, c, r), rstd[:, a:a + n], ALU.mult, ALU.mult,
                          [xkey(gr), "rstd", "A1", "A2"], [("tn", t)])
                    P.act(hT[:, c, a:a + n], tmpN[t][:, 0:n], AF.Identity, [("tn", t), "modT"], ["hT"],
                          bias=sh(i, which, c, r), scale=1.0)

        def resid_add(gr, i, which, c, psap, pskey, extra_r=()):
            for (r, a, n) in gr.segs:
                P.stt(xview(gr, c, a, n), psap[:, a:a + n], gt(i, which, c, r), xview(gr, c, a, n), ALU.mult, ALU.add,
                      [pskey, "modT", xkey(gr)] + list(extra_r), [xkey(gr)])

        def ffn(gr, i):
            n = gr.n
            norm_mod(gr, i, 2)
            for half in range(2):
                f0 = half * 11
                blocks = [(f0, 4), (f0 + 4, 4), (f0 + 8, 3)]
                for (fb, nf) in blocks:
                    wa, wak = wload(wview(("fin", i), w_ffn_in[i], fb * 128, nf * 128), NCH, nf * 128)
                    wb_, wbk = wload(wview(("fin", i), w_ffn_in[i], D_FF + fb * 128, nf * 128), NCH, nf * 128)
                    for f in range(nf):
                        ba = next_ps()
                        for k in range(NCH):
                            P.mm(PS[ba][:, 0:n], wa[:, k, f * 128:(f + 1) * 128], hT[:, k, 0:n], k == 0, k == NCH - 1,
                                 [wak, "hT"], [psk(ba)])
                        bb = next_ps()
                        for k in range(NCH):
                            P.mm(PS[bb][:, 0:n], wb_[:, k, f * 128:(f + 1) * 128], hT[:, k, 0:n], k == 0, k == NCH - 1,
                                 [wbk, "hT"], [psk(bb)])
                        t = st["tn"] % 2
                        st["tn"] += 1
                        P.act(tmpN[t][:, 0:n], PS[ba][:, 0:n], AF.Silu, [psk(ba)], [("tn", t)])
                        P.tt("dve", uT[:, fb - f0 + f, 0:n], tmpN[t][:, 0:n], PS[bb][:, 0:n], ALU.mult,
                             [("tn", t), psk(bb)], [("uT", fb - f0 + f)])
                for ch in range(2):
                    banks = [4, 5, 6, 7]
                    wts = []
                    for (k0, nk) in ((0, 8), (8, 3)):
                        wt, wk = wload(wview(("fout", i), w_ffn_out[i], ch * 512, 512, k0=f0 + k0, nk=nk), nk, 512)
                        wts.append((wt, wk, k0, nk))
                    for cc in range(4):
                        c = ch * 4 + cc
                        for (wt, wk, k0, nk) in wts:
                            for k in range(nk):
                                f = k0 + k
                                P.mm(PS[banks[cc]][:, 0:n], wt[:, k, cc * 128:(cc + 1) * 128], uT[:, f, 0:n],
                                     f == 0, f == 10, [wk, ("uT", f)], [psk(banks[cc])])
                        resid_add(gr, i, 2, c, PS[banks[cc]], psk(banks[cc]))

        def final_out(gr, dst):
            n = gr.n
            norm_stats(gr)
            for c in range(NCH):
                P.stt(hT_f[:, c, 0:n], xview(gr, c), gfT[:, c:c + 1], rstd[:, 0:n], ALU.mult, ALU.mult,
                      [xkey(gr), "rstd", "gfT"], ["hTf"])
            for (c0, nt, _ti) in gr.tiles:
                for half in range(2):
                    b = next_ps()
                    for q in range(4):
                        P.tr(PS[b][0:nt, q * 128:(q + 1) * 128], hT_f[:, half * 4 + q, c0:c0 + nt], identF[:, :],
                             ["hTf", "identF"], [psk(b)])
                    i = next_sl()
                    P.cp("act", SL[i][0:nt, :], PS[b][0:nt, :], [psk(b)], [("sl", i)])
                    P.dma("sp", dst[gr.c0 + c0:gr.c0 + c0 + nt, half * 512:(half + 1) * 512], SL[i][0:nt, :], [("sl", i)], ())

        def rope_slab(v, skey, nt, cos2, sin2, half, rot_cols, nsub, eng="dve"):
            x1 = v[:, :, rot_cols:rot_cols + half]
            x2 = v[:, :, rot_cols + half:rot_cols + 2 * half]
            cos = cos2.unsqueeze(1).broadcast_to([nt, nsub, half])
            sin = sin2.unsqueeze(1).broadcast_to([nt, nsub, half])
            t1 = rtmp[0][0:nt, 0:nsub * half].rearrange("p (s d) -> p s d", d=half)
            t2 = rtmp[1][0:nt, 0:nsub * half].rearrange("p (s d) -> p s d", d=half)
            t3 = rtmp[2][0:nt, 0:nsub * half].rearrange("p (s d) -> p s d", d=half)
            t4 = rtmp[3][0:nt, 0:nsub * half].rearrange("p (s d) -> p s d", d=half)
            tk = "ropetab"
            P.tt(eng, t1, x1, cos, ALU.mult, [skey, tk], ["rt1"])
            P.tt(eng, t2, x2, sin, ALU.mult, [skey, tk], ["rt2"])
            P.tt(eng, t3, x2, cos, ALU.mult, [skey, tk], ["rt3"])
            P.tt(eng, t4, x1, sin, ALU.mult, [skey, tk], ["rt4"])
            P.tt(eng, x1, t1, t2, ALU.subtract, ["rt1", "rt2", skey], [skey])
            P.tt(eng, x2, t3, t4, ALU.add, ["rt3", "rt4", skey], [skey])

        rtmp = [sb("rtmp%d" % i, [128, 64]) for i in range(4)]

        def attn_core(ls, gr, nq_cols, q_c0, heads_fn, key_tiles, lam_ap, kind, out_fn):
            pass

        def mixer_diff(gr, i, j, L):
            n = gr.n
            lam_init = 0.8 - 0.6 * math.exp(-0.3 * i)
            norm_mod(gr, i, 1)
            QT = L["QT"]; onT = L["onT"]
            kdst = akp if gr.kind == "p" else aks
            vdst = avp if gr.kind == "p" else avs
            kvkey = ("kv", j, gr.kind, gr.g)
            for cb in range(6):
                wt, wk = wload(wview(("aqkv", j), w_a_qkv[j], cb * 512, 512), NCH, 512)
                for (c0, nt, ti) in gr.tiles:
                    b = next_ps()
                    for k in range(NCH):
                        P.mm(PS[b][0:nt, :], hT[:, k, c0:c0 + nt], wt[:, k, :], k == 0, k == NCH - 1, ["hT", wk], [psk(b)])
                    s = next_sl()
                    sk = ("sl", s)
                    P.cp("act", SL[s][0:nt, :], PS[b][0:nt, :], [psk(b)], [sk])
                    if cb < 4:
                        rope_slab(SL[s][0:nt, :].rearrange("p (s d) -> p s d", d=64), sk, nt, ropeA[0:nt, ti, 0:8],
                                  ropeA[0:nt, ti, 8:16], 8, 0, 8, eng="pool" if cb % 2 else "dve")
                    if cb < 2:
                        qb = L["qb"]
                        P.cp("act", qb[0:nt, :], SL[s][0:nt, :], [sk], ["qb"])
                        tb = 2
                        for q in range(4):
                            P.tr(psb(tb)[:, q * 128:q * 128 + nt], qb[0:nt, q * 128:(q + 1) * 128], identB[0:nt, 0:nt],
                                 ["qb", "identB"], [psk(tb)])
                        P.cp("dve", QT[:, cb * 4:cb * 4 + 4, c0:c0 + nt],
                             psb(tb)[:, 0:512].rearrange("p (q t) -> p q t", t=128)[:, :, 0:nt], [psk(tb)], ["QT"])
                    elif cb < 4:
                        P.dma("sp", kdst[j, gr.c0 + c0:gr.c0 + c0 + nt, (cb - 2) * 512:(cb - 1) * 512], SL[s][0:nt, :],
                              [sk], [kvkey], semkey=("slo", s))
                    else:
                        P.dma("sp", vdst[j, gr.c0 + c0:gr.c0 + c0 + nt, (cb - 4) * 512:(cb - 3) * 512], SL[s][0:nt, :],
                              [sk], [kvkey], semkey=("slo", s))
            scale = 64 ** -0.5
            for si, (r, a, nq) in enumerate(gr.segs):
                if gr.kind == "p":
                    nkt = (gr.c0 + n) // 128
                    ktl = []
                    for kt in range(nkt):
                        gk = (kt * 128) // G
                        ktl.append((akp[j, kt * 128:(kt + 1) * 128, :], avp[j, kt * 128:(kt + 1) * 128, :], 128,
                                    kt * 128 - gr.c0, ("kv", j, "p", gk)))
                else:
                    s_ = si
                    ktl = []
                    for kt in range(NPT):
                        ktl.append((ca_k[j, s_, kt * 128:(kt + 1) * 128, :], ca_v[j, s_, kt * 128:(kt + 1) * 128, :], 128,
                                    -1, None))
                    ktl.append((aks[j, s_ * TS:(s_ + 1) * TS, :], avs[j, s_ * TS:(s_ + 1) * TS, :], TS, -1, kvkey))
                for hd in range(H_A):
                    first = True
                    nk_total = len(ktl)
                    for kti, (ksrc, vsrc, nk, dcol, dep) in enumerate(ktl):
                        last = kti == nk_total - 1
                        cq0 = max(dcol, 0)
                        kb = L["kb"][kti % 2]; vb = L["vb"][kti % 2]
                        kbk = ("kb", kti % 2); vbk = ("vb", kti % 2)
                        deps = [dep] if dep is not None else []
                        P.dma("pool", kb[0:nk, :], ksrc[:, hd * 128:(hd + 1) * 128], deps, [kbk])
                        P.dma("pool", vb[0:nk, :], vsrc[:, hd * 128:(hd + 1) * 128], deps, [vbk])
                        tb = 2
                        P.tr(psb(tb)[:, 0:nk], kb[0:nk, :], identB[0:nk, 0:nk], [kbk, "identB"], [psk(tb)])
                        kT = L["kT"][kti % 2]; kTk = ("kT", kti % 2)
                        P.cp("dve", kT[:, 0:nk], psb(tb)[:, 0:nk], [psk(tb)], [kTk])
                        qa, qn = a + cq0, nq - cq0
                        P.mm(PS[0][0:nk, 0:qn], kT[0:64, 0:nk], QT[0:64, hd, qa:qa + qn], True, True, [kTk, "QT"], [psk(0)])
                        P.mm(PS[1][0:nk, 0:qn], kT[64:128, 0:nk], QT[64:128, hd, qa:qa + qn], True, True, [kTk, "QT"], [psk(1)])
                        e1 = L["e1"][kti % 2]; e2 = L["e2"][kti % 2]
                        e1k = ("e1", kti % 2); e2k = ("e2", kti % 2)
                        P.act(e1[0:nk, 0:qn], PS[0][0:nk, 0:qn], AF.Exp, [psk(0)], [e1k], scale=scale)
                        P.act(e2[0:nk, 0:qn], PS[1][0:nk, 0:qn], AF.Exp, [psk(1)], [e2k], scale=scale)
                        if dcol >= 0:
                            P.memset("pool", e1[64:128, 0:64], 0.0, [e1k])
                            P.memset("pool", e2[64:128, 0:64], 0.0, [e2k])
                        P.mm(PS[4][:, cq0:nq], vb[0:nk, :], e1[0:nk, 0:qn], first, last, [vbk, e1k], [psk(4)])
                        P.mm(PS[5][:, cq0:nq], vb[0:nk, :], e2[0:nk, 0:qn], first, last, [vbk, e2k], [psk(5)])
                        P.mm(PS[6][:, cq0:nq], onesB[0:nk, :], e1[0:nk, 0:qn], first, last, ["onesB", e1k], [psk(6)])
                        P.mm(PS[7][:, cq0:nq], onesB[0:nk, :], e2[0:nk, 0:qn], first, last, ["onesB", e2k], [psk(7)])
                        first = False
                    r1 = L["r1"]; r2 = L["r2"]; o1 = L["o1"]; o2 = L["o2"]
                    P.op("dve", lambda e: e.reciprocal(out=r1[:, 0:nq], in_=PS[6][:, 0:nq]), [psk(6)], ["r1"])
                    P.op("dve", lambda e: e.reciprocal(out=r2[:, 0:nq], in_=PS[7][:, 0:nq]), [psk(7)], ["r2"])
                    P.tt("dve", o1[:, 0:nq], PS[4][:, 0:nq], r1[:, 0:nq], ALU.mult, [psk(4), "r1"], ["o1"])
                    P.tt("dve", o2[:, 0:nq], PS[5][:, 0:nq], r2[:, 0:nq], ALU.mult, [psk(5), "r2"], ["o2"])
                    P.stt(o1[:, 0:nq], o2[:, 0:nq], L["nlam"][:, 0:1], o1[:, 0:nq], ALU.mult, ALU.add, ["o1", "o2", "nlam"], ["o1"])
                    sqi = st["sq"] % 2
                    st["sq"] += 1
                    P.act(sqr[sqi][:, 0:nq], o1[:, 0:nq], AF.Square, ["o1"], [("sq", sqi)])
                    P.mm(PS[3][:, 0:nq], onesB[:], sqr[sqi][:, 0:nq], True, True, [("sq", sqi), "onesB"], [psk(3)])
                    P.act(r1[:, 0:nq], PS[3][:, 0:nq], AF.Sqrt, [psk(3), "epsT"], ["r1"], bias=epsT[:, 0:1], scale=1.0 / 128)
                    P.op("dve", lambda e: e.reciprocal(out=r1[:, 0:nq], in_=r1[:, 0:nq]), ["r1"], ["r1"])
                    P.stt(onT[:, hd, a:a + nq], o1[:, 0:nq], L["gsub"][:, 0:1], r1[:, 0:nq], ALU.mult, ALU.mult,
                          ["o1", "gsub", "r1"], ["onT"])
            for ch in range(2):
                wt, wk = wload(wview(("ao", j), w_a_o[j], ch * 512, 512), NCH, 512)
                for cc in range(4):
                    c = ch * 4 + cc
                    b = next_ps()
                    for hd in range(H_A):
                        P.mm(PS[b][:, 0:n], wt[:, hd, cc * 128:(cc + 1) * 128], onT[:, hd, 0:n], hd == 0, hd == H_A - 1,
                             [wk, "onT"], [psk(b)])
                    resid_add(gr, i, 1, c, PS[b], psk(b))

        def setup_diff(i, j, ls):
            L = {}
            L["QT"] = sb("QT", [128, H_A, G], BF16, ls)
            L["onT"] = sb("onT", [128, H_A, G], BF16, ls)
            L["qb"] = sb("qb", [128, 512], BF16, ls)
            L["kb"] = [sb("kb%d" % q, [128, 128], BF16, ls) for q in range(2)]
            L["vb"] = [sb("vb%d" % q, [128, 128], BF16, ls) for q in range(2)]
            L["kT"] = [sb("kT%d" % q, [128, 128], BF16, ls) for q in range(2)]
            L["e1"] = [sb("e1%d" % q, [128, G], BF16, ls) for q in range(2)]
            L["e2"] = [sb("e2%d" % q, [128, G], BF16, ls) for q in range(2)]
            L["r1"] = sb("r1", [128, G], F32, ls); L["r2"] = sb("r2", [128, G], F32, ls)
            L["o1"] = sb("o1", [128, G], F32, ls); L["o2"] = sb("o2", [128, G], F32, ls)
            L["nlam"] = sb("nlam", [128, 1], F32, ls); L["gsub"] = sb("gsub", [128, 1], F32, ls)
            lam_init = 0.8 - 0.6 * math.exp(-0.3 * i)
            lv = sb("lv", [1, 4, 64], F32, ls); lp = sb("lp", [1, 2, 64], F32, ls); lsum = sb("lsum", [1, 2], F32, ls)
            lam1 = sb("lam1", [1, 1], F32, ls)
            P.dma("sp", lv[:], a_lambda[j:j + 1, :, :], (), ["lv"])
            P.tt("dve", lp[:], lv[:, 0:4:2, :], lv[:, 1:4:2, :], ALU.mult, ["lv"], ["lp"])
            P.op("dve", lambda e: e.tensor_reduce(out=lsum[:], in_=lp[:], axis=AX.X, op=ALU.add), ["lp"], ["lsum"])
            P.act(lsum[:], lsum[:], AF.Exp, ["lsum"], ["lsum"])
            P.tt("dve", lam1[:], lsum[:, 1:2], lsum[:, 0:1], ALU.subtract, ["lsum"], ["lam1"])
            P.ts("dve", lam1[:], lam1[:], -lam_init, None, ALU.add, None, ["lam1"], ["lam1"])
            P.mm(PS[3][:, 0:1], onesF[0:1, :], lam1[:], True, True, ["onesF", "lam1"], [psk(3)])
            P.cp("dve", L["nlam"][:], PS[3][:, 0:1], [psk(3)], ["nlam"])
            s = next_sl()
            P.dma("sp", SL[s][0:1, 0:128], g_a_sub[j:j + 1, :], (), [("sl", s)])
            P.tr(PS[3][:, 0:1], SL[s][0:1, 0:128], identF[0:1, 0:1], [("sl", s), "identF"], [psk(3)])
            P.ts("dve", L["gsub"][:], PS[3][:, 0:1], 1.0 - lam_init, None, ALU.mult, None, [psk(3)], ["gsub"])
            return L


        def setup_mlstm(i, j, ls):
            L = {}
            L["Cst"] = sb("Cst", [128, H_B, 257], F32, ls)
            L["Cb"] = sb("Cb", [128, H_B, 257], BF16, ls)
            L["mrow"] = sb("mrow", [4, 1], F32, ls)
            L["qT"] = sb("qT", [128, H_B, 64], BF16, ls)
            L["kT"] = sb("mkT", [128, H_B, 64], BF16, ls)
            L["ktm"] = sb("ktm", [64, 1, 512], F32, ls)
            L["Vaug"] = sb("Vaug", [64, 1, H_B, 257], BF16, ls)
            L["gsig"] = sb("gsig", [64, 1, 1024], F32, ls)
            L["hnT"] = sb("hnT", [128, NCH, G], BF16, ls)
            L["goutT"] = sb("goutT", [128, NCH], F32, ls)
            L["bg"] = sb("bg", [4, 2], F32, ls)
            L["tri"] = sb("tri", [64, 64], F32, ls)
            L["sel"] = sb("sel", [4, 256], F32, ls)
            L["one1"] = sb("one1", [128, 1], F32, ls)
            for nm in ("rI", "rF", "rA", "rE", "rL", "rB", "rU", "rCM", "rM", "ra", "rwr", "remt"):
                L[nm] = sb("ml_" + nm, [4, 64], F32, ls)
            L["nML"] = sb("nML", [4, 1], F32, ls)
            L["dg"] = sb("dg", [4, 4], F32, ls)
            L["cols"] = sb("cols", [64, 16], F32, ls)
            L["dbc"] = sb("dbc", [128, 4], F32, ls)
            L["z"] = sb("mz", [64, 64], F32, ls)
            L["W"] = sb("mW", [64, 64], F32, ls)
            L["Dm"] = sb("mDm", [64, 64], BF16, ls)
            L["itmp"] = sb("itmp", [64, 257], F32, ls)
            L["ND"] = sb("ND", [64, 257], F32, ls)
            L["sc"] = sb("msc", [64, 8], F32, ls)
            L["hn"] = sb("mhn", [64, 1024], BF16, ls)
            L["kw"] = sb("mkw", [64, 128], BF16, ls)
            L["nrow"] = sb("nrow", [4, 128], F32, ls)
            load_rows_T(L["goutT"][:], "goutT", g_b_out.rearrange("i (c p) -> (i c) p", p=128), NCH)
            P.dma("sp", L["bg"][:, 0:1], b_b_gates[0:1, 0:4].rearrange("o f -> f o"), (), ["bg"])
            P.dma("sp", L["bg"][:, 1:2], b_b_gates[0:1, 4:8].rearrange("o f -> f o"), (), ["bg"])
            P.dma("sp", L["tri"][:], k_tri[:, :], (), ["tri"])
            P.dma("sp", L["sel"][:], k_sel[:, :], (), ["sel"])
            P.memset("dve", L["one1"][:], 1.0, ["one1"])
            return L

        def mlstm_state_init(L, seq):
            Cst = L["Cst"]
            if seq is None:
                P.memset("dve", Cst[:], 0.0, ["Cst"])
                P.memset("dve", L["mrow"][:], 0.0, ["mrow"])
            else:
                P.dma("sp", Cst[:, :, 0:256], sb_c[seq].rearrange("h d v -> d h v"), (), ["Cst"])
                load_rows_T(Cst[:, :, 256], "Cst", sb_n[seq], 4)
                P.dma("sp", L["mrow"][:], sb_m[seq:seq + 1, :].rearrange("o f -> f o"), (), ["mrow"])
            P.cp("act", L["Cb"][:], Cst[:], ["Cst"], ["Cb"])

        def mlstm_state_out(L, dc, dn, dm):
            Cst = L["Cst"]
            P.dma("sp", dc.rearrange("h d v -> d h v"), Cst[:, :, 0:256], ["Cst"], (), semkey="Cst_o")
            P.tr(PS[3][0:4, 0:128], Cst[:, :, 256], identF[:, :], ["Cst", "identF"], [psk(3)])
            P.cp("dve", L["nrow"][:], PS[3][0:4, 0:128], [psk(3)], ["nrow"])
            P.dma("sp", dn, L["nrow"][:], ["nrow"], (), semkey="nrow_o")
            P.dma("sp", dm, L["mrow"][:], ["mrow"], (), semkey="mrow_o")

        def mlstm_unit(gr, L, u0, nu, Lc):
            nchk = nu // Lc
            wj = w_b_in[0]
            qT, kT, ktm, Vaug, gsig = L["qT"], L["kT"], L["ktm"], L["Vaug"], L["gsig"]
            kscale = DQK_B ** -0.5
            for blk, dst, scl in ((0, qT, 1.0), (1, kT, kscale)):
                wt, wk = wload(wview(("bin", 0), wj, blk * 512, 512), NCH, 512)
                for h in range(H_B):
                    b = next_ps()
                    for k in range(NCH):
                        P.mm(PS[b][:, 0:nu], wt[:, k, h * 128:(h + 1) * 128], hT[:, k, u0:u0 + nu], k == 0, k == NCH - 1,
                             [wk, "hT"], [psk(b)])
                    P.act(dst[:, h, 0:nu], PS[b][:, 0:nu], AF.Copy, [psk(b)], ["mqk"], scale=scl)
            for blk in range(1, 6):
                wt, wk = wload(wview(("bin", 0), wj, blk * 512, 512), NCH, 512)
                for ck in range(nchk):
                    b = next_ps()
                    cc0 = u0 + ck * Lc
                    for k in range(NCH):
                        P.mm(PS[b][0:Lc, :], hT[:, k, cc0:cc0 + Lc], wt[:, k, :], k == 0, k == NCH - 1, ["hT", wk], [psk(b)])
                    if blk == 1:
                        P.act(ktm[0:Lc, ck, :], PS[b][0:Lc, :], AF.Copy, [psk(b)], ["ktm"], scale=kscale)
                    elif blk < 4:
                        hh = (blk - 2) * 2
                        P.cp("dve", Vaug[0:Lc, ck, hh:hh + 2, 0:256], PS[b][0:Lc, :].rearrange("p (h v) -> p h v", v=256),
                             [psk(b)], ["Vaug"])
                    else:
                        o0 = (blk - 4) * 512
                        P.act(gsig[0:Lc, ck, o0:o0 + 512], PS[b][0:Lc, :], AF.Sigmoid, [psk(b)], ["gsig"])
            for ck in range(nchk):
                P.memset("pool", Vaug[0:Lc, ck, :, 256:257], 1.0, ["Vaug"])
            wt, wk = wload(wview(("bin", 0), wj, 3072, 8), NCH, 8)
            gb = 3
            for k in range(NCH):
                P.mm(PS[gb][0:4, 0:nu], wt[:, k, 0:4], hT[:, k, u0:u0 + nu], k == 0, k == NCH - 1, [wk, "hT"], [psk(gb)])
            for k in range(NCH):
                P.mm(PS[gb][0:4, 128:128 + nu], wt[:, k, 4:8], hT[:, k, u0:u0 + nu], k == 0, k == NCH - 1, [wk, "hT"], [psk(gb)])
            rI, rF, rA, rE, rL, rB, rU, rCM, rM, ra, rwr, remt = (L[x] for x in
                ("rI", "rF", "rA", "rE", "rL", "rB", "rU", "rCM", "rM", "ra", "rwr", "remt"))
            bg = L["bg"]
            P.act(rI[:, 0:nu], PS[gb][0:4, 0:nu], AF.Identity, [psk(gb), "bg"], ["rI"], bias=bg[:, 0:1], scale=1.0)
            P.act(rF[:, 0:nu], PS[gb][0:4, 128:128 + nu], AF.Identity, [psk(gb), "bg"], ["rF"], bias=bg[:, 1:2], scale=1.0)
            P.act(rA[:, 0:nu], rF[:, 0:nu], AF.Abs, ["rF"], ["rA"])
            P.act(rE[:, 0:nu], rA[:, 0:nu], AF.Exp, ["rA"], ["rE"], scale=-1.0)
            P.act(rL[:, 0:nu], rE[:, 0:nu], AF.Ln, ["rE", "one1"], ["rL"], bias=L["one1"][0:4, 0:1], scale=1.0)
            P.ts("dve", rA[:, 0:nu], rF[:, 0:nu], 0.0, None, ALU.min, None, ["rF", "rA"], ["rA"])
            P.tt("dve", rF[:, 0:nu], rA[:, 0:nu], rL[:, 0:nu], ALU.subtract, ["rA", "rL"], ["rF"])
            P.memset("dve", rE[:, 0:nu], 0.0, ["rE"])
            mrow = L["mrow"]
            for ck in range(nchk):
                a0, a1 = ck * Lc, (ck + 1) * Lc
                P.op("dve", lambda e: e.tensor_tensor_scan(out=rB[:, a0:a1], data0=rF[:, a0:a1], data1=rE[:, a0:a1],
                                                           initial=0.0, op0=ALU.add, op1=ALU.add), ["rF", "rE"], ["rB"])
                P.tt("dve", rU[:, a0:a1], rI[:, a0:a1], rB[:, a0:a1], ALU.subtract, ["rI", "rB"], ["rU"])
                P.op("dve", lambda e: e.tensor_tensor_scan(out=rCM[:, a0:a1], data0=rU[:, a0:a1], data1=rU[:, a0:a1],
                                                           initial=-1e30, op0=ALU.max, op1=ALU.max), ["rU"], ["rCM"])
                P.ts("dve", rM[:, a0:a1], rCM[:, a0:a1], mrow[:, 0:1], None, ALU.max, None, ["rCM", "mrow"], ["rM"])
                P.act(ra[:, a0:a1], rM[:, a0:a1], AF.Exp, ["rM", "mrow"], ["ra"], bias=mrow[:, 0:1], scale=-1.0)
                P.ts("dve", L["nML"][:], rM[:, a1 - 1:a1], -1.0, None, ALU.mult, None, ["rM"], ["nML"])
                P.act(rwr[:, a0:a1], rU[:, a0:a1], AF.Exp, ["rU", "nML"], ["rwr"], bias=L["nML"][:, 0:1], scale=1.0)
                P.tt("dve", remt[:, a0:a1], rB[:, a0:a1], rM[:, a0:a1], ALU.add, ["rB", "rM"], ["remt"])
                P.act(remt[:, a0:a1], remt[:, a0:a1], AF.Exp, ["remt"], ["remt"], scale=-1.0)
                P.tt("dve", mrow[:], rB[:, a1 - 1:a1], rM[:, a1 - 1:a1], ALU.add, ["rB", "rM", "ra"], ["mrow"])
                cb_ = 3
                for qi, rr in enumerate((rU, ra, rwr, remt)):
                    P.tr(PS[cb_][0:Lc, 256 + qi * 4:256 + qi * 4 + 4], rr[:, a0:a1], identF[0:4, 0:4],
                         ["rU", "ra", "rwr", "remt", "identF"], [psk(cb_)])
                cols = L["cols"]
                P.cp("dve", cols[0:Lc, :], PS[cb_][0:Lc, 256:272], [psk(cb_)], ["cols"])
                P.ts("dve", L["dg"][:], identF[0:4, 0:4], ra[:, a1 - 1:a1], None, ALU.mult, None, ["identF", "ra"], ["dg"])
                P.mm(PS[cb_][:, 280:284], onesF[0:4, :], L["dg"][:], True, True, ["onesF", "dg"], [psk(cb_)])
                P.cp("dve", L["dbc"][:], PS[cb_][:, 280:284], [psk(cb_)], ["dbc"])
                cq0 = ck * Lc
                for h in range(H_B):
                    P.mm(PS[4][0:Lc, 0:Lc], kT[:, h, cq0:cq0 + Lc], qT[:, h, cq0:cq0 + Lc], True, True, ["mqk"], [psk(4)])
                    P.mm(PS[4][0:Lc, 64:64 + Lc], L["sel"][:, h * 64:h * 64 + Lc], rM[:, a0:a1], True, True, ["sel", "rM"], [psk(4)])
                    P.ts("dve", L["z"][0:Lc, 0:Lc], PS[4][0:Lc, 64:64 + Lc], cols[0:Lc, h:h + 1], 0.0, ALU.subtract, ALU.max,
                         [psk(4), "cols"], ["mz"])
                    P.act(L["W"][0:Lc, 0:Lc], L["z"][0:Lc, 0:Lc], AF.Exp, ["mz"], ["mW"], scale=-1.0)
                    P.tt("pool", L["W"][0:Lc, 0:Lc], L["W"][0:Lc, 0:Lc], L["tri"][0:Lc, 0:Lc], ALU.mult, ["mW", "tri"], ["mW"])
                    P.tt("dve", L["Dm"][0:Lc, 0:Lc], PS[4][0:Lc, 0:Lc], L["W"][0:Lc, 0:Lc], ALU.mult, [psk(4), "mW"], ["mDm"])
                    P.mm(PS[5][0:Lc, 0:257], L["Dm"][0:Lc, 0:Lc], Vaug[0:Lc, ck, h, :], True, True, ["mDm", "Vaug"], [psk(5)])
                    P.mm(PS[6][0:Lc, 0:257], qT[:, h, cq0:cq0 + Lc], L["Cb"][:, h, :], True, True, ["mqk", "Cb"], [psk(6)])
                    P.act(L["itmp"][0:Lc, :], PS[6][0:Lc, 0:257], AF.Copy, [psk(6), "cols"], ["itmp"], scale=cols[0:Lc, 4 + h:5 + h])
                    P.tt("dve", L["ND"][0:Lc, :], L["itmp"][0:Lc, :], PS[5][0:Lc, 0:257], ALU.add, ["itmp", psk(5)], ["ND"])
                    sc = L["sc"]
                    P.act(sc[0:Lc, 6:7], L["ND"][0:Lc, 256:257], AF.Abs, ["ND"], ["msc6"])
                    P.tt("dve", sc[0:Lc, 0:1], sc[0:Lc, 6:7], cols[0:Lc, 12 + h:13 + h], ALU.max, ["msc6", "cols"], ["msc"])
                    P.op("dve", lambda e: e.reciprocal(out=sc[0:Lc, 1:2], in_=sc[0:Lc, 0:1]), ["msc"], ["msc"])
                    P.act(L["itmp"][0:Lc, 0:256], L["ND"][0:Lc, 0:256], AF.Square, ["ND", "msc"], ["itmp", "msc2"],
                          scale=sc[0:Lc, 1:2], accum=sc[0:Lc, 2:3])
                    P.act(sc[0:Lc, 3:4], sc[0:Lc, 2:3], AF.Sqrt, ["msc2", "epsT"], ["msc3"], bias=epsT[0:Lc, 0:1], scale=1.0 / 256)
                    P.op("dve", lambda e: e.reciprocal(out=sc[0:Lc, 4:5], in_=sc[0:Lc, 3:4]), ["msc3"], ["msc4"])
                    P.tt("dve", sc[0:Lc, 5:6], sc[0:Lc, 4:5], sc[0:Lc, 1:2], ALU.mult, ["msc4", "msc"], ["msc5"])
                    P.stt(L["hn"][0:Lc, h * 256:(h + 1) * 256], L["ND"][0:Lc, 0:256], sc[0:Lc, 5:6],
                          gsig[0:Lc, ck, h * 256:(h + 1) * 256], ALU.mult, ALU.mult, ["ND", "msc5", "gsig"], [("hn", h)])
                    P.ts("pool", L["kw"][0:Lc, :], ktm[0:Lc, ck, h * 128:(h + 1) * 128], cols[0:Lc, 8 + h:9 + h], None,
                         ALU.mult, None, ["ktm", "cols"], ["mkw"])
                    P.mm(PS[7][:, 0:257], L["kw"][0:Lc, :], Vaug[0:Lc, ck, h, :], True, True, ["mkw", "Vaug"], [psk(7)])
                    P.stt(L["Cst"][:, h, :], L["Cst"][:, h, :], L["dbc"][:, h:h + 1], PS[7][:, 0:257], ALU.mult, ALU.add,
                          ["Cst", "dbc", psk(7)], ["Cst"])
                    P.cp("act", L["Cb"][:, h, :], L["Cst"][:, h, :], ["Cst"], ["Cb"])
                tb = 2
                for q in range(NCH):
                    P.tr(psb(tb)[:, q * 64:q * 64 + Lc], L["hn"][0:Lc, q * 128:(q + 1) * 128], identB[0:Lc, 0:Lc],
                         [("hn", q // 2), "identB"], [psk(tb)])
                P.tt("dve", L["hnT"][:, :, u0 + a0:u0 + a1], psb(tb)[:, 0:512].rearrange("p (q t) -> p q t", t=64)[:, :, 0:Lc],
                     L["goutT"][:, :].unsqueeze(2).broadcast_to([128, NCH, Lc]), ALU.mult, [psk(tb), "goutT"], ["hnT"])

        def mixer_mlstm(gr, i, j, L):
            n = gr.n
            norm_mod(gr, i, 1)
            if gr.kind == "p":
                if gr.g == 0:
                    mlstm_state_init(L, None)
                for c0 in range(0, n, 64):
                    mlstm_unit(gr, L, c0, 64, 64)
                if gr.g == TP // G - 1:
                    mlstm_state_out(L, bcp, bnp, bmp)
            else:
                for s_ in range(NS):
                    mlstm_state_init(L, s_)
                    mlstm_unit(gr, L, s_ * TS, TS, TS)
                    mlstm_state_out(L, bcs[s_], bns[s_], bms[s_])
            for ch in range(2):
                wt, wk = wload(wview(("bout", j), w_b_out[j], ch * 512, 512), NCH, 512)
                for cc in range(4):
                    c = ch * 4 + cc
                    b = next_ps()
                    for q in range(NCH):
                        P.mm(PS[b][:, 0:n], wt[:, q, cc * 128:(cc + 1) * 128], L["hnT"][:, q, 0:n], q == 0, q == NCH - 1,
                             [wk, "hnT"], [psk(b)])
                    resid_add(gr, i, 1, c, PS[b], psk(b))

        def setup_mla(i, j, ls):
            L = {}
            L["QTn"] = sb("QTn", [128, H_C, G], BF16, ls)
            L["QTr"] = sb("QTr", [64, H_C, G], BF16, ls)
            L["onT"] = sb("c_onT", [128, H_C, G], BF16, ls)
            L["cq"] = sb("cq", [128, 3, G], F32, ls)
            L["cqn"] = sb("cqn", [128, 3, G], BF16, ls)
            L["gqT"] = sb("gqT", [128, 3], F32, ls)
            L["gkv"] = sb("gkv", [128, 256], F32, ls)
            L["rc"] = [sb("rc%d" % q, [128, 64], F32, ls) for q in range(2)]
            L["qb"] = sb("c_qb", [128, 384], BF16, ls)
            L["kvb"] = sb("kvb", [128, 256], BF16, ls)
            L["kvT"] = sb("kvT", [128, 2, 2, 128], BF16, ls)
            L["ob"] = [sb("c_ob%d" % q, [128, 512], BF16, ls) for q in range(2)]
            L["kb"] = [sb("c_kb%d" % q, [128, 128], BF16, ls) for q in range(2)]
            L["krb"] = [sb("c_krb%d" % q, [128, 64], BF16, ls) for q in range(2)]
            L["vb"] = [sb("c_vb%d" % q, [128, 128], BF16, ls) for q in range(2)]
            L["kT"] = [sb("c_kT%d" % q, [128, 128], BF16, ls) for q in range(2)]
            L["krT"] = [sb("c_krT%d" % q, [64, 128], BF16, ls) for q in range(2)]
            L["e"] = [sb("c_e%d" % q, [128, G], BF16, ls) for q in range(2)]
            L["r"] = sb("c_r", [128, G], F32, ls)
            L["sc"] = sb("c_sc", [128, 4], F32, ls)
            L["junk"] = sb("c_junk", [128, 256], F32, ls)
            L["n"] = {"rc": 0, "ob": 0}
            load_rows_T(L["gqT"][:], "gqT", g_c_q.rearrange("i (c p) -> (i c) p", p=128), 3)
            P.dma("sp", L["gkv"][:], g_c_kv[0, :].partition_broadcast(128), (), ["gkv"])
            return L

        def mla_up(L, kvT_ap, nt, dst_rows, dkey):
            for cb in range(4):
                wt, wk = wload(wview(("cukv", 0), w_c_ukv[0], cb * 512, 512), 2, 512)
                b = next_ps()
                for k in range(2):
                    P.mm(PS[b][0:nt, :], kvT_ap[:, k, 0:nt], wt[:, k, :], k == 0, k == 1, ["kvT", wk], [psk(b)])
                oi = L["n"]["ob"] % 2
                L["n"]["ob"] += 1
                P.cp("act", L["ob"][oi][0:nt, :], PS[b][0:nt, :], [psk(b)], [("cob", oi)])
                P.dma("sp", dst_rows[:, cb * 512:(cb + 1) * 512], L["ob"][oi][0:nt, :], [("cob", oi)], [dkey], semkey=("cobo", oi))

        def latent_T(L, src_bf_ap, nt, slot):
            tb = 2
            for k in range(2):
                P.tr(psb(tb)[:, k * 128:k * 128 + nt], src_bf_ap[:, k * 128:(k + 1) * 128], identB[0:nt, 0:nt],
                     ["kvb", "identB"], [psk(tb)])
            P.cp("dve", L["kvT"][:, slot, :, 0:nt], psb(tb)[:, 0:256].rearrange("p (k t) -> p k t", t=128)[:, :, 0:nt],
                 [psk(tb)], ["kvT"])

        def mixer_mla(gr, i, j, L):
            n = gr.n
            norm_mod(gr, i, 1)
            QTn, QTr, onT = L["QTn"], L["QTr"], L["onT"]
            kvdst = ckvp if gr.kind == "p" else ckvs
            krdst = ckrp if gr.kind == "p" else ckrs
            ckey = ("ckv", gr.kind, gr.g)
            wt, wk = wload(wview(("cdq", 0), w_c_dq[0], 0, Q_LORA), NCH, Q_LORA)
            sb_ = 3
            for f in range(3):
                b = next_ps()
                for k in range(NCH):
                    P.mm(PS[b][:, 0:n], wt[:, k, f * 128:(f + 1) * 128], hT[:, k, 0:n], k == 0, k == NCH - 1, [wk, "hT"], [psk(b)])
                P.cp("dve", L["cq"][:, f, 0:n], PS[b][:, 0:n], [psk(b)], ["cq"])
                si = st["sq"] % 2
                st["sq"] += 1
                P.act(sqr[si][:, 0:n], PS[b][:, 0:n], AF.Square, [psk(b)], [("sq", si)])
                P.mm(PS[sb_][:, 0:n], onesB[:], sqr[si][:, 0:n], f == 0, f == 2, [("sq", si), "onesB"], [psk(sb_)])
            P.act(L["r"][:, 0:n], PS[sb_][:, 0:n], AF.Sqrt, [psk(sb_), "epsT"], ["c_r"], bias=epsT[:, 0:1], scale=1.0 / Q_LORA)
            P.op("dve", lambda e: e.reciprocal(out=L["r"][:, 0:n], in_=L["r"][:, 0:n]), ["c_r"], ["c_r"])
            for f in range(3):
                P.stt(L["cqn"][:, f, 0:n], L["cq"][:, f, 0:n], L["gqT"][:, f:f + 1], L["r"][:, 0:n], ALU.mult, ALU.mult,
                      ["cq", "gqT", "c_r"], ["cqn"])
            for cb in range(4):
                wt, wk = wload(wview(("cuq", 0), w_c_uq[0], cb * 384, 384), 3, 384)
                for (c0, nt, ti) in gr.tiles:
                    b = next_ps()
                    for k in range(3):
                        P.mm(PS[b][0:nt, 0:384], L["cqn"][:, k, c0:c0 + nt], wt[:, k, 0:384], k == 0, k == 2, ["cqn", wk], [psk(b)])
                    s = next_sl(); sk = ("sl", s)
                    P.cp("act", SL[s][0:nt, 0:384], PS[b][0:nt, 0:384], [psk(b)], [sk])
                    ri = L["n"]["rc"] % 2
                    L["n"]["rc"] += 1
                    row0 = (gr.c0 + c0) if gr.kind == "p" else TP
                    P.dma("sp", L["rc"][ri][0:nt, :], rope_c[row0:row0 + nt, :], (), [("rc", ri)])
                    P.res["ropetab"] = P.res[("rc", ri)]
                    rope_slab(SL[s][0:nt, 0:384].rearrange("p (s d) -> p s d", d=192), sk, nt, L["rc"][ri][0:nt, 0:32],
                              L["rc"][ri][0:nt, 32:64], 32, 128, 2)
                    P.res[("rc", ri)] = P.res["ropetab"]
                    P.cp("act", L["qb"][0:nt, :], SL[s][0:nt, 0:384], [sk], ["c_qb"])
                    tb = 2
                    for hh in range(2):
                        P.tr(psb(tb)[:, hh * 128:hh * 128 + nt], L["qb"][0:nt, hh * 192:hh * 192 + 128], identB[0:nt, 0:nt],
                             ["c_qb", "identB"], [psk(tb)])
                        P.tr(psb(tb)[0:64, 256 + hh * 128:256 + hh * 128 + nt], L["qb"][0:nt, hh * 192 + 128:hh * 192 + 192],
                             identB[0:nt, 0:nt], ["c_qb", "identB"], [psk(tb)])
                    P.cp("dve", QTn[:, cb * 2:cb * 2 + 2, c0:c0 + nt],
                         psb(tb)[:, 0:256].rearrange("p (q t) -> p q t", t=128)[:, :, 0:nt], [psk(tb)], ["QTn"])
                    P.cp("dve", QTr[:, cb * 2:cb * 2 + 2, c0:c0 + nt],
                         psb(tb)[0:64, 256:512].rearrange("p (q t) -> p q t", t=128)[:, :, 0:nt], [psk(tb)], ["QTr"])
            for tix, (c0, nt, ti) in enumerate(gr.tiles):
                wt, wk = wload(wview(("cdkv", 0), w_c_dkv[0], 0, 320), NCH, 320)
                b = next_ps()
                for k in range(NCH):
                    P.mm(PS[b][0:nt, 0:320], hT[:, k, c0:c0 + nt], wt[:, k, 0:320], k == 0, k == NCH - 1, ["hT", wk], [psk(b)])
                s = next_sl(); sk = ("sl", s)
                P.cp("act", SL[s][0:nt, 0:320], PS[b][0:nt, 0:320], [psk(b)], [sk])
                sc = L["sc"]
                P.act(L["junk"][0:nt, :], SL[s][0:nt, 0:256], AF.Square, [sk], ["c_junk", "c_sc"], accum=sc[0:nt, 0:1])
                P.act(sc[0:nt, 1:2], sc[0:nt, 0:1], AF.Sqrt, ["c_sc", "epsT"], ["c_sc1"], bias=epsT[0:nt, 0:1], scale=1.0 / KV_LORA)
                P.op("dve", lambda e: e.reciprocal(out=sc[0:nt, 2:3], in_=sc[0:nt, 1:2]), ["c_sc1"], ["c_sc2"])
                P.stt(SL[s][0:nt, 0:256], SL[s][0:nt, 0:256], sc[0:nt, 2:3], L["gkv"][0:nt, :], ALU.mult, ALU.mult,
                      [sk, "c_sc2", "gkv"], [sk])
                ri = L["n"]["rc"] % 2
                L["n"]["rc"] += 1
                row0 = (gr.c0 + c0) if gr.kind == "p" else TP
                P.dma("sp", L["rc"][ri][0:nt, :], rope_c[row0:row0 + nt, :], (), [("rc", ri)])
                P.res["ropetab"] = P.res[("rc", ri)]
                rope_slab(SL[s][0:nt, 256:320].rearrange("p (s d) -> p s d", d=64), sk, nt, L["rc"][ri][0:nt, 0:32],
                          L["rc"][ri][0:nt, 32:64], 32, 0, 1)
                P.res[("rc", ri)] = P.res["ropetab"]
                P.dma("sp", kvdst[gr.c0 + c0:gr.c0 + c0 + nt, :], SL[s][0:nt, 0:256], [sk], [ckey], semkey=("slo", s))
                P.dma("sp", krdst[gr.c0 + c0:gr.c0 + c0 + nt, :], SL[s][0:nt, 256:320], [sk], [ckey], semkey=("slo", s))
                P.cp("act", L["kvb"][0:nt, :], SL[s][0:nt, 0:256], [sk], ["kvb"])
                latent_T(L, L["kvb"][0:nt, :], nt, 0)
                if gr.kind == "p":
                    mla_up(L, L["kvT"][:, 0, :, :], nt, mla_kv_p[gr.c0 + c0:gr.c0 + c0 + nt, :], ckey)
                else:
                    for cb in range(4):
                        wt2, wk2 = wload(wview(("cukv", 0), w_c_ukv[0], cb * 512, 512), 2, 512)
                        b2 = next_ps()
                        for k in range(2):
                            P.mm(PS[b2][0:nt, :], L["kvT"][:, 0, k, 0:nt], wt2[:, k, :], k == 0, k == 1, ["kvT", wk2], [psk(b2)])
                        oi = L["n"]["ob"] % 2
                        L["n"]["ob"] += 1
                        P.cp("act", L["ob"][oi][0:nt, :], PS[b2][0:nt, :], [psk(b2)], [("cob", oi)])
                        for s_ in range(NS):
                            P.dma("sp", mla_kv_s[s_, PAST:PAST + TS, cb * 512:(cb + 1) * 512], L["ob"][oi][s_ * TS:(s_ + 1) * TS, :],
                                  [("cob", oi)], [ckey], semkey=("cobo", oi))
            if gr.kind == "s":
                for s_ in range(NS):
                    for pt in range(NPT):
                        P.dma("pool", L["kvb"][:, :], cc_kv[s_, pt * 128:(pt + 1) * 128, :], (), ["kvb"])
                        latent_T(L, L["kvb"][:, :], 128, 1)
                        mla_up(L, L["kvT"][:, 1, :, :], 128, mla_kv_s[s_, pt * 128:(pt + 1) * 128, :], ("cpast", s_))
            scale = 192 ** -0.5
            for si, (r, a, nq) in enumerate(gr.segs):
                if gr.kind == "p":
                    nkt = (gr.c0 + n) // 128
                    ktl = []
                    for kt in range(nkt):
                        gk = (kt * 128) // G
                        ktl.append((mla_kv_p[kt * 128:(kt + 1) * 128, :], ckrp[kt * 128:(kt + 1) * 128, :], 128,
                                    kt * 128 - gr.c0, [("ckv", "p", gk)]))
                else:
                    s_ = si
                    ktl = []
                    for kt in range(NPT):
                        ktl.append((mla_kv_s[s_, kt * 128:(kt + 1) * 128, :], cc_kr[s_, kt * 128:(kt + 1) * 128, :], 128, -1,
                                    [("cpast", s_)]))
                    ktl.append((mla_kv_s[s_, PAST:PAST + TS, :], ckrs[s_ * TS:(s_ + 1) * TS, :], TS, -1, [ckey]))
                nkt_ = len(ktl)
                for hd in range(H_C):
                    def stageA(kti):
                        (kvsrc, krsrc, nk, dcol, deps) = ktl[kti]
                        q2 = kti % 2
                        kb, krb, vb, kT, krT = L["kb"][q2], L["krb"][q2], L["vb"][q2], L["kT"][q2], L["krT"][q2]
                        P.dma("sp", kb[0:nk, :], kvsrc[:, hd * 256:hd * 256 + 128], deps, [("ckb", q2)])
                        P.dma("sp", vb[0:nk, :], kvsrc[:, hd * 256 + 128:hd * 256 + 256], deps, [("cvb", q2)])
                        P.dma("pool", krb[0:nk, :], krsrc, deps, [("ckrb", q2)])
                        tb = 2
                        P.tr(psb(tb)[:, 0:nk], kb[0:nk, :], identB[0:nk, 0:nk], [("ckb", q2), "identB"], [psk(tb)])
                        P.tr(psb(tb)[0:64, 128:128 + nk], krb[0:nk, :], identB[0:nk, 0:nk], [("ckrb", q2), "identB"], [psk(tb)])
                        P.cp("dve", kT[:, 0:nk], psb(tb)[:, 0:nk], [psk(tb)], [("ckT", q2)])
                        P.cp("dve", krT[:, 0:nk], psb(tb)[0:64, 128:128 + nk], [psk(tb)], [("ckrT", q2)])
                        cq0 = max(dcol, 0)
                        qa, qn = a + cq0, nq - cq0
                        P.mm(PS[q2][0:nk, 0:qn], kT[:, 0:nk], QTn[:, hd, qa:qa + qn], True, False, [("ckT", q2), "QTn"], [psk(q2)])
                        P.mm(PS[q2][0:nk, 0:qn], krT[0:64, 0:nk], QTr[0:64, hd, qa:qa + qn], False, True, [("ckrT", q2), "QTr"], [psk(q2)])

                    def stageBC(kti):
                        (kvsrc, krsrc, nk, dcol, deps) = ktl[kti]
                        q2 = kti % 2
                        e = L["e"][q2]; vb = L["vb"][q2]
                        first, last = kti == 0, kti == nkt_ - 1
                        cq0 = max(dcol, 0)
                        qn = nq - cq0
                        P.act(e[0:nk, 0:qn], PS[q2][0:nk, 0:qn], AF.Exp, [psk(q2)], [("ce", q2)], scale=scale)
                        if dcol >= 0:
                            P.memset("pool", e[64:128, 0:64], 0.0, [("ce", q2)])
                        P.mm(PS[4][:, cq0:nq], vb[0:nk, :], e[0:nk, 0:qn], first, last, [("cvb", q2), ("ce", q2)], [psk(4)])
                        P.mm(PS[5][:, cq0:nq], onesB[0:nk, :], e[0:nk, 0:qn], first, last, ["onesB", ("ce", q2)], [psk(5)])

                    stageA(0)
                    for kti in range(nkt_):
                        if kti + 1 < nkt_:
                            stageA(kti + 1)
                        stageBC(kti)
                    P.op("dve", lambda e_: e_.reciprocal(out=L["r"][:, 0:nq], in_=PS[5][:, 0:nq]), [psk(5)], ["c_r"])
                    P.tt("dve", onT[:, hd, a:a + nq], PS[4][:, 0:nq], L["r"][:, 0:nq], ALU.mult, [psk(4), "c_r"], ["c_onT"])
            for ch in range(2):
                wt, wk = wload(wview(("co", 0), w_c_o[0], ch * 512, 512), NCH, 512)
                for cc in range(4):
                    c = ch * 4 + cc
                    b = next_ps()
                    for hd in range(H_C):
                        P.mm(PS[b][:, 0:n], wt[:, hd, cc * 128:(cc + 1) * 128], onT[:, hd, 0:n], hd == 0, hd == H_C - 1,
                             [wk, "c_onT"], [psk(b)])
                    resid_add(gr, i, 1, c, PS[b], psk(b))

        def prep_layer(i):
            kind, j = i % 3, i // 3
            sp_ = []
            if kind == 0:
                sp_ += [wview(("aqkv", j), w_a_qkv[j], cb * 512, 512) for cb in range(6)]
                sp_ += [wview(("ao", j), w_a_o[j], ch * 512, 512) for ch in range(2)]
            elif kind == 1:
                sp_ += [wview(("bin", 0), w_b_in[0], blk * 512, 512) for blk in range(6)]
                sp_ += [wview(("bin", 0), w_b_in[0], 3072, 8)]
                sp_ += [wview(("bout", j), w_b_out[j], ch * 512, 512) for ch in range(2)]
            else:
                sp_ += [wview(("cdq", 0), w_c_dq[0], 0, Q_LORA), wview(("cdkv", 0), w_c_dkv[0], 0, 320)]
                sp_ += [wview(("cuq", 0), w_c_uq[0], cb * 384, 384) for cb in range(4)]
                sp_ += [wview(("cukv", 0), w_c_ukv[0], cb * 512, 512) for cb in range(4)]
                sp_ += [wview(("co", 0), w_c_o[0], ch * 512, 512) for ch in range(2)]
            for half in range(2):
                f0 = half * 11
                for (fb, nf) in [(f0, 4), (f0 + 4, 4), (f0 + 8, 3)]:
                    sp_.append(wview(("fin", i), w_ffn_in[i], fb * 128, nf * 128))
                    sp_.append(wview(("fin", i), w_ffn_in[i], D_FF + fb * 128, nf * 128))
                for ch in range(2):
                    for (k0, nk) in ((0, 8), (8, 3)):
                        sp_.append(wview(("fout", i), w_ffn_out[i], ch * 512, 512, k0=f0 + k0, nk=nk))
            for spc in sp_:
                wprep(spc, i)

        prep_layer(cfg.layers[0])
        for li, i in enumerate(cfg.layers):
            kind, j = i % 3, i // 3
            P.barrier()
            cur_layer[0] = i
            if li + 1 < len(cfg.layers):
                prep_layer(cfg.layers[li + 1])
            with ExitStack() as ls:
                if kind == 0:
                    L = setup_diff(i, j, ls)
                    for gr in groups:
                        mixer_diff(gr, i, j, L)
                        ffn(gr, i)
                elif kind == 1:
                    L = setup_mlstm(i, j, ls)
                    for gr in groups:
                        mixer_mlstm(gr, i, j, L)
                        ffn(gr, i)
                else:
                    L = setup_mla(i, j, ls)
                    for gr in groups:
                        mixer_mla(gr, i, j, L)
                        ffn(gr, i)
                P.barrier()

        P.barrier()
        hT_f = sb("hT_f", [128, NCH, G])
        for gr in groups:
            final_out(gr, y_p if gr.kind == "p" else y_s)
        P.finish()
        print("ops", P.nops, "waits", P.nwait, "dma sems", P.nsem, "cnt", P.cnt, flush=True)
    return nc


def rope_table(pos, rot):
    half = rot // 2
    inv = np.power(np.float32(ROPE_THETA), -np.arange(half, dtype=np.float32) * np.float32(2.0 / rot)).astype(np.float32)
    ang = pos.astype(np.float32)[:, None] * inv[None, :]
    return np.concatenate([np.cos(ang), np.sin(ang)], axis=1).astype(np.float32)


def host_consts(cfg):
    tri = (np.arange(64)[:, None] <= np.arange(64)[None, :]).astype(np.float32)
    sel = np.zeros((4, 4, 64), np.float32)
    for h in range(4):
        sel[h, h, :] = 1.0
    pos = np.concatenate([np.arange(cfg.TP), np.tile(cfg.PAST + np.arange(cfg.TS), cfg.NS)])
    return {
        "k_ident": np.eye(128, dtype=np.float32),
        "k_tri": tri,
        "k_sel": sel.reshape(4, 256),
        "rope_a": rope_table(pos, 16),
        "rope_c": rope_table(pos, 64),
    }


def make_in_maps(cfg, inp, n_cores=8):
    f = lambda a: np.ascontiguousarray(np.asarray(a, dtype=np.float32))
    NS, TS, TP, PAST = cfg.NS, cfg.TS, cfg.TP, cfg.PAST
    consts = host_consts(cfg)
    shared = {k: f(inp[k]) for k in ("w_ada", "b_ada", "g_norm1", "g_norm2", "w_a_qkv", "a_lambda", "g_a_sub", "w_a_o",
                                     "w_b_in", "b_b_gates", "g_b_out", "w_b_out", "w_c_dq", "g_c_q", "w_c_uq", "w_c_dkv",
                                     "g_c_kv", "w_c_ukv", "w_c_o", "w_ffn_in", "w_ffn_out")}
    shared["g_final"] = f(inp["g_final"]).reshape(1, D)
    shared.update(consts)
    nb = inp["x_prompt"].shape[0]
    maps = []
    for c in range(n_cores):
        b = c % nb
        ss = slice(NS * c, NS * c + NS)
        m = dict(shared)
        m["x_p"] = f(inp["x_prompt"][b])
        m["x_s"] = f(inp["x_sample"][ss]).reshape(NS * TS, D)
        m["c_in"] = f(np.concatenate([np.asarray(inp["c_prompt"])[b:b + 1], np.asarray(inp["c_sample"])[ss]], axis=0))
        m["ca_k"] = f(inp["cache_a_k"][:, ss]).reshape(2, NS, PAST, D)
        m["ca_v"] = f(inp["cache_a_v"][:, ss]).reshape(2, NS, PAST, D)
        m["sb_c"] = f(inp["state_b_c"][0, ss])
        m["sb_n"] = f(inp["state_b_n"][0, ss])
        m["sb_m"] = f(inp["state_b_m"][0, ss])
        m["cc_kv"] = f(inp["cache_c_kv"][0, ss])
        m["cc_kr"] = f(inp["cache_c_kr"][0, ss])
        maps.append(m)
    return maps


_NC_CACHE = {}


def kernel(**inputs):
    cfg = Cfg()
    if "nc" not in _NC_CACHE:
        _NC_CACHE["nc"] = build(cfg)
    nc = _NC_CACHE["nc"]
    maps = make_in_maps(cfg, inputs)
    res = run_bass_kernel_spmd(nc, maps, core_ids=list(range(8))).results
    NS, TS, TP = cfg.NS, cfg.TS, cfg.TP
    B = 4

    def pst(name, cores, shape=None):
        a = np.stack([np.asarray(res[c][name], dtype=np.float32) for c in cores], axis=0)
        return a

    pc = list(range(B))
    ac = list(range(8))
    y_prompt = pst("y_p", pc)
    y_sample = pst("y_s", ac).reshape(8 * NS, TS, D)
    a_k_p = np.moveaxis(pst("akp", pc), 0, 1).reshape(2, B, TP, H_A, 128)
    a_v_p = np.moveaxis(pst("avp", pc), 0, 1).reshape(2, B, TP, H_A, 128)
    b_c_p = pst("bcp", pc)[None]
    b_n_p = pst("bnp", pc)[None]
    b_m_p = pst("bmp", pc).reshape(1, B, H_B)
    c_kv_p = pst("ckvp", pc)[None]
    c_kr_p = pst("ckrp", pc)[None]
    a_k_s = np.moveaxis(pst("aks", ac).reshape(8, 2, NS, TS, D), 1, 0).reshape(2, 8 * NS, TS, H_A, 128)
    a_v_s = np.moveaxis(pst("avs", ac).reshape(8, 2, NS, TS, D), 1, 0).reshape(2, 8 * NS, TS, H_A, 128)
    b_c_s = pst("bcs", ac).reshape(1, 8 * NS, H_B, DQK_B, DV_B)
    b_n_s = pst("bns", ac).reshape(1, 8 * NS, H_B, DQK_B)
    b_m_s = pst("bms", ac).reshape(1, 8 * NS, H_B)
    c_kv_s = pst("ckvs", ac).reshape(1, 8 * NS, TS, KV_LORA)
    c_kr_s = pst("ckrs", ac).reshape(1, 8 * NS, TS, 64)
    outs = (y_prompt, y_sample, a_k_p, a_v_p, b_c_p, b_n_p, b_m_p, c_kv_p, c_kr_p,
            a_k_s, a_v_s, b_c_s, b_n_s, b_m_s, c_kv_s, c_kr_s)
    return tuple(np.ascontiguousarray(o, dtype=np.float32) for o in outs)
```

```python
import math
import os
from contextlib import ExitStack
import numpy as np
import concourse.bass as bass
import concourse.mybir as mybir
from concourse.bass_utils import run_bass_kernel_spmd

F32 = mybir.dt.float32
BF16 = mybir.dt.bfloat16
AF = mybir.ActivationFunctionType
ALU = mybir.AluOpType
AX = mybir.AxisListType

D = 1024
NCH = 8
EPS = 1e-6
ROPE_THETA = 500000.0
DEPTH = 4
H_A = 8
H_B = 4
DQK_B = 128
DV_B = 256
H_C = 8
Q_LORA = 384
KV_LORA = 256
D_FF = 2816
NFF = 22
N_B_IN = 3080


class Cfg:
    def __init__(self, TP=4096, G=256, PAST=2048, TS=32, NS=2, layers=(0, 1, 2, 3)):
        self.TP, self.G, self.PAST, self.TS, self.NS = TP, G, PAST, TS, NS
        self.layers = tuple(layers)


class Prog:
    ENGS = ("pe", "act", "dve", "pool", "sp")

    def __init__(self, nc, es):
        self.nc, self.es = nc, es
        self.eobj = {"pe": nc.tensor, "act": nc.scalar, "dve": nc.vector, "pool": nc.gpsimd, "sp": nc.sync}
        self.cnt = {e: 0 for e in self.ENGS}
        self.esem = {e: es.enter_context(nc.semaphore("s_" + e)) for e in self.ENGS}
        self.seen = {e: {} for e in self.ENGS}
        self.res = {}
        self.dsem = {}
        self.nsem = 0
        self.nwait = 0
        self.nops = 0
        self.trace = {e: [] for e in self.ENGS}
        self.marks = []

    def mark(self, label):
        self.marks.append((label, dict(self.cnt)))

    def _waits(self, eng, r, w):
        toks = []
        for k in r:
            st = self.res.get(k)
            if st is not None and st[0] is not None:
                toks.append(st[0])
        for k in w:
            st = self.res.get(k)
            if st is not None:
                if st[0] is not None:
                    toks.append(st[0])
                toks.extend(st[1])
        e = self.eobj[eng]
        seen = self.seen[eng]
        for (name, sem, val, src) in toks:
            if src == eng and eng == "pe":
                continue
            if seen.get(name, 0) >= val:
                continue
            seen[name] = val
            e.wait_ge(sem, val)
            self.trace[eng].append(("w", name, val))
            self.nwait += 1

    def _commit(self, tok, r, w):
        for k in r:
            st = self.res.get(k)
            if st is None:
                st = [None, []]
                self.res[k] = st
            st[1].append(tok)
        for k in w:
            self.res[k] = [tok, []]

    def op(self, eng, fn, r=(), w=()):
        if eng != "pe":
            psr = [k for k in r if isinstance(k, tuple) and k[0] == "ps"]
            if psr:
                r = [k for k in r if k not in psr]
                w = list(w) + psr
        self._waits(eng, r, w)
        inst = fn(self.eobj[eng])
        self.cnt[eng] += 1
        inst.then_inc(self.esem[eng], 1)
        self.trace[eng].append(("i", "s_" + eng, 1))
        tok = ("s_" + eng, self.esem[eng], self.cnt[eng], eng)
        self._commit(tok, r, w)
        self.nops += 1

    def dma(self, q, out, in_, r=(), w=(), semkey=None, acc=False):
        self._waits(q, r, () if acc else w)
        if semkey is None:
            semkey = w[0] if len(w) else r[0]
        if semkey not in self.dsem:
            self.dsem[semkey] = [self.es.enter_context(self.nc.semaphore("d%d" % self.nsem)), 0]
            self.nsem += 1
        ds = self.dsem[semkey]
        inst = self.eobj[q].dma_start(out=out, in_=in_)
        ds[1] += 16
        inst.then_inc(ds[0], 16)
        self.trace[q].append(("i", "d" + str(semkey), 16))
        tok = ("d" + str(semkey), ds[0], ds[1], None)
        self._commit(tok, r, w)
        self.nops += 1

    def barrier(self):
        for e in self.ENGS:
            eo = self.eobj[e]
            for e2 in self.ENGS:
                if e2 != e and self.cnt[e2] > self.seen[e].get("s_" + e2, 0):
                    eo.wait_ge(self.esem[e2], self.cnt[e2])
                    self.trace[e].append(("w", "s_" + e2, self.cnt[e2]))
                    self.seen[e]["s_" + e2] = self.cnt[e2]
            for k, ds in self.dsem.items():
                nm = "d" + str(k)
                if ds[1] > self.seen[e].get(nm, 0):
                    eo.wait_ge(ds[0], ds[1])
                    self.trace[e].append(("w", nm, ds[1]))
                    self.seen[e][nm] = ds[1]

    def finish(self):
        eo = self.eobj["sp"]
        for k, ds in self.dsem.items():
            eo.wait_ge(ds[0], ds[1])
            self.trace["sp"].append(("w", "d" + str(k), ds[1]))
        self.check_deadlock()

    def check_deadlock(self):
        val = {}
        ptr = {e: 0 for e in self.ENGS}
        prog = True
        while prog:
            prog = False
            for e in self.ENGS:
                tr = self.trace[e]
                while ptr[e] < len(tr):
                    k, nm, v = tr[ptr[e]]
                    if k == "w":
                        if val.get(nm, 0) >= v:
                            ptr[e] += 1
                            prog = True
                        else:
                            break
                    else:
                        val[nm] = val.get(nm, 0) + v
                        ptr[e] += 1
                        prog = True
        bad = {e: (ptr[e], len(self.trace[e]), self.trace[e][ptr[e]]) for e in self.ENGS if ptr[e] < len(self.trace[e])}
        if bad:
            raise RuntimeError("DEADLOCK in schedule: %r ; sem values: %r" % (bad, {k: val.get(k, 0) for k in [b[2][1] for b in bad.values()]}))

    def mm(self, out, lhsT, rhs, start, stop, r, w):
        self.op("pe", lambda e: e.matmul(out, lhsT=lhsT, rhs=rhs, start=start, stop=stop), r, w)

    def tr(self, out, in_, ident, r, w):
        self.op("pe", lambda e: e.transpose(out, in_, ident), r, w)

    def act(self, out, in_, func, r, w, bias=None, scale=None, accum=None):
        kw = {}
        if bias is not None:
            kw["bias"] = bias
        if scale is not None:
            kw["scale"] = scale
        if accum is not None:
            kw["accum_out"] = accum
        self.op("act", lambda e: e.activation(out=out, in_=in_, func=func, **kw), r, w)

    def tt(self, eng, out, a, b, op, r, w):
        self.op(eng, lambda e: e.tensor_tensor(out=out, in0=a, in1=b, op=op), r, w)

    def ts(self, eng, out, a, s1, s2, op0, op1, r, w):
        if op1 is None:
            self.op(eng, lambda e: e.tensor_scalar(out=out, in0=a, scalar1=s1, scalar2=None, op0=op0), r, w)
        else:
            self.op(eng, lambda e: e.tensor_scalar(out=out, in0=a, scalar1=s1, scalar2=s2, op0=op0, op1=op1), r, w)

    def stt(self, out, a, s, b, op0, op1, r, w):
        self.op("dve", lambda e: e.scalar_tensor_tensor(out=out, in0=a, scalar=s, in1=b, op0=op0, op1=op1), r, w)

    def cp(self, eng, out, in_, r, w):
        if eng == "act":
            self.op("act", lambda e: e.copy(out=out, in_=in_), r, w)
        else:
            self.op(eng, lambda e: e.tensor_copy(out=out, in_=in_), r, w)

    def memset(self, eng, ap, val, w):
        self.op(eng, lambda e: e.memset(ap, val), (), w)


def build(cfg):
    nc = bass.Bass("TRN2", target_bir_lowering=False)
    TP, G, PAST, TS, NS = cfg.TP, cfg.G, cfg.PAST, cfg.TS, cfg.NS
    NSK = NS * TS
    NTP = TP // 128
    NR = 1 + NS
    NPT = PAST // 128

    def din(name, shape, dt=F32):
        return nc.dram_tensor(name, list(shape), dt, kind="ExternalInput").ap()

    def dout(name, shape, dt=F32):
        return nc.dram_tensor(name, list(shape), dt, kind="ExternalOutput").ap()

    x_p = din("x_p", [TP, D]); x_s = din("x_s", [NSK, D]); c_in = din("c_in", [NR, D])
    ca_k = din("ca_k", [2, NS, PAST, D]); ca_v = din("ca_v", [2, NS, PAST, D])
    sb_c = din("sb_c", [NS, H_B, DQK_B, DV_B]); sb_n = din("sb_n", [NS, H_B, DQK_B]); sb_m = din("sb_m", [NS, H_B])
    cc_kv = din("cc_kv", [NS, PAST, KV_LORA]); cc_kr = din("cc_kr", [NS, PAST, 64])
    w_ada = din("w_ada", [DEPTH, D, 6 * D]); b_ada = din("b_ada", [DEPTH, 6 * D])
    g_norm1 = din("g_norm1", [DEPTH, D]); g_norm2 = din("g_norm2", [DEPTH, D])
    w_a_qkv = din("w_a_qkv", [2, D, 3 * D]); a_lambda = din("a_lambda", [2, 4, 64]); g_a_sub = din("g_a_sub", [2, 128])
    w_a_o = din("w_a_o", [2, D, D])
    w_b_in = din("w_b_in", [1, D, N_B_IN]); b_b_gates = din("b_b_gates", [1, 8]); g_b_out = din("g_b_out", [1, D])
    w_b_out = din("w_b_out", [1, D, D])
    w_c_dq = din("w_c_dq", [1, D, Q_LORA]); g_c_q = din("g_c_q", [1, Q_LORA]); w_c_uq = din("w_c_uq", [1, Q_LORA, 1536])
    w_c_dkv = din("w_c_dkv", [1, D, 320]); g_c_kv = din("g_c_kv", [1, KV_LORA]); w_c_ukv = din("w_c_ukv", [1, KV_LORA, 2048])
    w_c_o = din("w_c_o", [1, D, D])
    w_ffn_in = din("w_ffn_in", [DEPTH, D, 2 * D_FF]); w_ffn_out = din("w_ffn_out", [DEPTH, D_FF, D])
    g_final = din("g_final", [1, D])
    k_ident = din("k_ident", [128, 128])
    k_tri = din("k_tri", [64, 64])
    k_sel = din("k_sel", [4, 4 * 64])
    rope_a = din("rope_a", [TP + NSK, 16])
    rope_c = din("rope_c", [TP + NSK, 64])

    y_p = dout("y_p", [TP, D]); y_s = dout("y_s", [NSK, D])
    akp = dout("akp", [2, TP, D]); avp = dout("avp", [2, TP, D])
    bcp = dout("bcp", [H_B, DQK_B, DV_B]); bnp = dout("bnp", [H_B, DQK_B]); bmp = dout("bmp", [H_B, 1])
    ckvp = dout("ckvp", [TP, KV_LORA]); ckrp = dout("ckrp", [TP, 64])
    aks = dout("aks", [2, NSK, D]); avs = dout("avs", [2, NSK, D])
    bcs = dout("bcs", [NS, H_B, DQK_B, DV_B]); bns = dout("bns", [NS, H_B, DQK_B]); bms = dout("bms", [NS, H_B, 1])
    ckvs = dout("ckvs", [NSK, KV_LORA]); ckrs = dout("ckrs", [NSK, 64])
    NWT = 400
    wsc_l = [nc.dram_tensor("wsc%d" % q, [200, 128, 4096], BF16, kind="Internal").ap() for q in range(2)]
    mla_kv_p = nc.dram_tensor("mla_kv_p", [TP, 2048], BF16, kind="Internal").ap()
    mla_kv_s = nc.dram_tensor("mla_kv_s", [NS, PAST + TS, 2048], BF16, kind="Internal").ap()

    es = ExitStack()
    with es:
        P = Prog(nc, es)

        uniq = [0]

        def sb(name, shape, dt=F32, stack=None):
            if stack is not None:
                uniq[0] += 1
                name = "%s_u%d" % (name, uniq[0])
            return (stack or es).enter_context(nc.sbuf_tensor(name, list(shape), dt))

        PS = [es.enter_context(nc.psum_tensor("ps%d" % i, [128, 512], F32)) for i in range(8)]

        def psk(i):
            return ("ps", i)

        def psb(i):
            return PS[i][:].bitcast(BF16)

        xTp = sb("xTp", [128, NCH, TP])
        xTs = sb("xTs", [128, NCH, NSK])
        identF = sb("identF", [128, 128]); identB = sb("identB", [128, 128], BF16)
        onesB = sb("onesB", [128, 128], BF16); onesF = sb("onesF", [128, 128])
        hT = sb("hT", [128, NCH, G], BF16)
        sqr = [sb("sq%d" % i, [128, G], BF16) for i in range(2)]
        tmpN = [sb("tmpN%d" % i, [128, G]) for i in range(2)]
        rstd = sb("rstd", [128, G])
        NWB = 2
        WB = [sb("wb%d" % i, [128, NCH, 512], BF16) for i in range(NWB)]
        NSL = 4
        SL = [sb("sl%d" % i, [128, 512]) for i in range(NSL)]
        modT = sb("modT", [128, DEPTH, 48, NR])
        A1 = sb("A1", [128, DEPTH, NCH, NR]); A2 = sb("A2", [128, DEPTH, NCH, NR])
        gn1T = sb("gn1T", [128, DEPTH * NCH]); gn2T = sb("gn2T", [128, DEPTH * NCH]); gfT = sb("gfT", [128, NCH])
        ropeA = sb("ropeA", [128, NTP + 1, 16])
        epsT = sb("epsT", [128, 1])
        uT = sb("uT", [128, 11, G], BF16)

        st = {"wb": 0, "sl": 0, "sq": 0, "tn": 0, "pa": 0}

        P.dma("sp", identF[:], k_ident[:, :], (), ["identF"])
        P.cp("dve", identB[:], identF[:], ["identF"], ["identB"])
        P.memset("dve", onesB[:], 1.0, ["onesB"])
        P.memset("dve", onesF[:], 1.0, ["onesF"])
        P.memset("dve", epsT[:], EPS, ["epsT"])

        class Grp:
            pass
        groups = []
        for g in range(TP // G):
            gr = Grp()
            gr.kind = "p"; gr.g = g; gr.n = G; gr.xT = xTp; gr.c0 = g * G
            gr.segs = [(0, 0, G)]
            gr.tiles = [(t * 128, 128, g * (G // 128) + t) for t in range(G // 128)]
            groups.append(gr)
        gs = Grp()
        gs.kind = "s"; gs.g = 0; gs.n = NSK; gs.xT = xTs; gs.c0 = 0
        gs.segs = [(1 + s, s * TS, TS) for s in range(NS)]
        gs.tiles = [(0, NSK, NTP)]
        groups.append(gs)

        def xview(gr, c, a=0, n=None):
            n = gr.n - a if n is None else n
            return gr.xT[:, c, gr.c0 + a:gr.c0 + a + n]

        def xkey(gr):
            return ("x", gr.kind, gr.g)

        wtiles = {}
        cur_layer = [0]

        def wview(wname, w2d, c0, ncols, k0=0, nk=None):
            v = w2d.rearrange("(kc p) n -> p kc n", p=128)
            nk = v.shape[1] - k0 if nk is None else nk
            return (wname, c0, ncols, k0, nk, v[:, k0:k0 + nk, c0:c0 + ncols])

        def wprep(spec, layer):
            (wname, c0, ncols, k0, nk, view) = spec
            tk = (wname, c0, ncols, k0, nk)
            if tk in wtiles:
                return wtiles[tk]
            t = len(wtiles)
            assert t < NWT
            dst = wsc_l[t // 200][t % 200, :, 0:nk * ncols].rearrange("p (k n) -> p k n", n=ncols)
            ck = ("wcv", layer)
            P.dma("pool", dst, view, (), [ck], acc=True)
            wtiles[tk] = (dst, ck)
            return wtiles[tk]

        def wload(spec, nk, ncols, col0=0, same_slot=False):
            src, ck = wprep(spec, cur_layer[0])
            if same_slot:
                i = (st["wb"] - 1) % NWB
            else:
                i = st["wb"] % NWB
                st["wb"] += 1
            key = ("wb", i)
            P.dma("pool", WB[i][:, 0:nk, col0:col0 + ncols], src, [ck], [key], acc=same_slot)
            return WB[i], key

        PS_ROT = (0, 1, 6, 7) if os.environ.get('EXP_ROT', '4') == '4' else (0, 1, 0, 1)

        def next_ps():
            i = PS_ROT[st["pa"] % 4]
            st["pa"] += 1
            return i

        def next_sl():
            i = st["sl"] % NSL
            st["sl"] += 1
            return i

        def load_rows_T(dst, dkey, src_rows_ap, nrows):
            i = next_sl()
            P.dma("sp", SL[i][0:nrows, 0:128], src_rows_ap, (), [("sl", i)])
            b = next_ps()
            P.tr(PS[b][:, 0:nrows], SL[i][0:nrows, 0:128], identF[0:nrows, 0:nrows], [("sl", i), "identF"], [psk(b)])
            P.cp("dve", dst, PS[b][:, 0:nrows], [psk(b)], [dkey])

        cT = sb("cT", [128, NCH, NR]); scT = sb("scT", [128, NCH, NR], BF16)
        for k in range(NCH):
            load_rows_T(cT[:, k, :], "cT", c_in[:, k * 128:(k + 1) * 128], NR)
        P.act(scT[:], cT[:], AF.Silu, ["cT"], ["scT"])
        bT = sb("bT", [128, DEPTH, 48])
        for i in range(DEPTH):
            load_rows_T(bT[:, i, :], "bT", b_ada[i, :].rearrange("(j p) -> j p", p=128), 48)
        load_rows_T(gn1T[:], "gn1T", g_norm1.rearrange("i (c p) -> (i c) p", p=128), DEPTH * NCH)
        load_rows_T(gn2T[:], "gn2T", g_norm2.rearrange("i (c p) -> (i c) p", p=128), DEPTH * NCH)
        load_rows_T(gfT[:], "gfT", g_final.rearrange("i (c p) -> (i c) p", p=128), NCH)
        P.dma("sp", ropeA[:, 0:NTP, :], rope_a[0:TP, :].rearrange("(t p) f -> p t f", p=128), (), ["ropetab"])
        P.dma("sp", ropeA[0:NSK, NTP, :], rope_a[TP:TP + NSK, :], (), ["ropetab"], semkey="ropetab2")

        for i in cfg.layers:
            mb = 7
            cur_layer[0] = ("ada", i)
            for cb in range(12):
                wt, wk = wload(wview(("ada", i), w_ada[i], cb * 512, 512), NCH, 512)
                for fc in range(4):
                    j = cb * 4 + fc
                    for k in range(NCH):
                        P.mm(PS[mb][:, j * NR:(j + 1) * NR], wt[:, k, fc * 128:(fc + 1) * 128], scT[:, k, :],
                             k == 0, k == NCH - 1, [wk, "scT"], [psk(mb)])
            P.tt("dve", modT[:, i, :, :], PS[mb][:, 0:48 * NR].rearrange("p (j r) -> p j r", r=NR),
                 bT[:, i, :].unsqueeze(2).broadcast_to([128, 48, NR]), ALU.add, [psk(mb), "bT"], ["modT"])
            P.stt(A1[:, i, :, :], modT[:, i, 8:16, :], 1.0,
                  gn1T[:, i * NCH:(i + 1) * NCH].unsqueeze(2).broadcast_to([128, NCH, NR]), ALU.add, ALU.mult,
                  ["modT", "gn1T"], ["A1"])
            P.stt(A2[:, i, :, :], modT[:, i, 32:40, :], 1.0,
                  gn2T[:, i * NCH:(i + 1) * NCH].unsqueeze(2).broadcast_to([128, NCH, NR]), ALU.add, ALU.mult,
                  ["modT", "gn2T"], ["A2"])

        def sh(i, which, c, r):
            base = 0 if which == 1 else 24
            return modT[:, i, base + c, r:r + 1]

        def gt(i, which, c, r):
            base = 16 if which == 1 else 40
            return modT[:, i, base + c, r:r + 1]

        def Ac(i, which, c, r):
            return (A1 if which == 1 else A2)[:, i, c, r:r + 1]

        def load_xT(gr, src):
            for (c0, nt, _ti) in gr.tiles:
                for half in range(2):
                    i = next_sl()
                    P.dma("sp", SL[i][0:nt, :], src[gr.c0 + c0:gr.c0 + c0 + nt, half * 512:(half + 1) * 512], (), [("sl", i)])
                    b = next_ps()
                    for q in range(4):
                        P.tr(PS[b][:, q * 128:q * 128 + nt], SL[i][0:nt, q * 128:(q + 1) * 128], identF[0:nt, 0:nt],
                             [("sl", i), "identF"], [psk(b)])
                    P.cp("act" if half else "dve", gr.xT[:, half * 4:half * 4 + 4, gr.c0 + c0:gr.c0 + c0 + nt],
                         PS[b][:].rearrange("p (q t) -> p q t", t=128)[:, :, 0:nt], [psk(b)], [xkey(gr)])

        for gr in groups:
            load_xT(gr, x_p if gr.kind == "p" else x_s)

        def norm_stats(gr):
            n = gr.n
            b = next_ps()
            for c in range(NCH):
                i = st["sq"] % 2
                st["sq"] += 1
                P.act(sqr[i][:, 0:n], xview(gr, c), AF.Square, [xkey(gr)], [("sq", i)])
                P.mm(PS[b][:, 0:n], onesB[:], sqr[i][:, 0:n], c == 0, c == NCH - 1, [("sq", i), "onesB"], [psk(b)])
            P.act(rstd[:, 0:n], PS[b][:, 0:n], AF.Sqrt, [psk(b), "epsT"], ["rstd"], bias=epsT[:, 0:1], scale=1.0 / D)
            P.op("dve", lambda e: e.reciprocal(out=rstd[:, 0:n], in_=rstd[:, 0:n]), ["rstd"], ["rstd"])

        def norm_mod(gr, i, which):
            norm_stats(gr)
            for c in range(NCH):
                for (r, a, n) in gr.segs:
                    t = st["tn"] % 2
                    st["tn"] += 1
                    P.stt(tmpN[t][:, 0:n], xview(gr, c, a, n), Ac(i, which, c, r), rstd[:, a:a + n], ALU.mult, ALU.mult,
                          [xkey(gr), "rstd", "A1", "A2"], [("tn", t)])
                    P.act(hT[:, c, a:a + n], tmpN[t][:, 0:n], AF.Identity, [("tn", t), "modT"], ["hT"],
                          bias=sh(i, which, c, r), scale=1.0)

        def resid_add(gr, i, which, c, psap, pskey, extra_r=()):
            for (r, a, n) in gr.segs:
                P.stt(xview(gr, c, a, n), psap[:, a:a + n], gt(i, which, c, r), xview(gr, c, a, n), ALU.mult, ALU.add,
                      [pskey, "modT", xkey(gr)] + list(extra_r), [xkey(gr)])

        def ffn(gr, i):
            n = gr.n
            P.mark("L%d %s%d ffn" % (i, gr.kind, gr.g))
            norm_mod(gr, i, 2)
            for half in range(2):
                f0 = half * 11
                blocks = [(f0, 2), (f0 + 2, 2), (f0 + 4, 2), (f0 + 6, 2), (f0 + 8, 2), (f0 + 10, 1)]
                for (fb, nf) in blocks:
                    wa, wak = wload(wview(("fin", i), w_ffn_in[i], fb * 128, nf * 128), NCH, nf * 128)
                    wb_, wbk = wload(wview(("fin", i), w_ffn_in[i], D_FF + fb * 128, nf * 128), NCH, nf * 128, col0=256,
                                     same_slot=True)
                    for f in range(nf):
                        ba = next_ps()
                        for k in range(NCH):
                            P.mm(PS[ba][:, 0:n], wa[:, k, f * 128:(f + 1) * 128], hT[:, k, 0:n], k == 0, k == NCH - 1,
                                 [wak, "hT"], [psk(ba)])
                        bb = next_ps()
                        for k in range(NCH):
                            P.mm(PS[bb][:, 0:n], wb_[:, k, 256 + f * 128:256 + (f + 1) * 128], hT[:, k, 0:n], k == 0, k == NCH - 1,
                                 [wbk, "hT"], [psk(bb)])
                        t = st["tn"] % 2
                        st["tn"] += 1
                        P.act(tmpN[t][:, 0:n], PS[ba][:, 0:n], AF.Silu, [psk(ba)], [("tn", t)])
                        P.tt("dve", uT[:, fb - f0 + f, 0:n], tmpN[t][:, 0:n], PS[bb][:, 0:n], ALU.mult,
                             [("tn", t), psk(bb)], [("uT", fb - f0 + f)])
                for ch in range(2):
                    banks = [4, 5, 6, 7]
                    wts = []
                    for (k0, nk) in ((0, 8), (8, 3)):
                        wt, wk = wload(wview(("fout", i), w_ffn_out[i], ch * 512, 512, k0=f0 + k0, nk=nk), nk, 512)
                        wts.append((wt, wk, k0, nk))
                    for cc in range(4):
                        c = ch * 4 + cc
                        for (wt, wk, k0, nk) in wts:
                            for k in range(nk):
                                f = k0 + k
                                P.mm(PS[banks[cc]][:, 0:n], wt[:, k, cc * 128:(cc + 1) * 128], uT[:, f, 0:n],
                                     f == 0, f == 10, [wk, ("uT", f)], [psk(banks[cc])])
                        resid_add(gr, i, 2, c, PS[banks[cc]], psk(banks[cc]))

        def final_out(gr, dst):
            n = gr.n
            norm_stats(gr)
            for c in range(NCH):
                P.stt(hT_f[:, c, 0:n], xview(gr, c), gfT[:, c:c + 1], rstd[:, 0:n], ALU.mult, ALU.mult,
                      [xkey(gr), "rstd", "gfT"], ["hTf"])
            for (c0, nt, _ti) in gr.tiles:
                for half in range(2):
                    b = next_ps()
                    for q in range(4):
                        P.tr(PS[b][0:nt, q * 128:(q + 1) * 128], hT_f[:, half * 4 + q, c0:c0 + nt], identF[:, :],
                             ["hTf", "identF"], [psk(b)])
                    i = next_sl()
                    P.cp("act", SL[i][0:nt, :], PS[b][0:nt, :], [psk(b)], [("sl", i)])
                    P.dma("sp", dst[gr.c0 + c0:gr.c0 + c0 + nt, half * 512:(half + 1) * 512], SL[i][0:nt, :], [("sl", i)], ())

        def rope_slab(v, skey, nt, cos2, sin2, half, rot_cols, nsub, eng="dve"):
            x1 = v[:, :, rot_cols:rot_cols + half]
            x2 = v[:, :, rot_cols + half:rot_cols + 2 * half]
            cos = cos2.unsqueeze(1).broadcast_to([nt, nsub, half])
            sin = sin2.unsqueeze(1).broadcast_to([nt, nsub, half])
            t1 = rtmp[0][0:nt, 0:nsub * half].rearrange("p (s d) -> p s d", d=half)
            t2 = rtmp[1][0:nt, 0:nsub * half].rearrange("p (s d) -> p s d", d=half)
            t3 = rtmp[2][0:nt, 0:nsub * half].rearrange("p (s d) -> p s d", d=half)
            t4 = rtmp[3][0:nt, 0:nsub * half].rearrange("p (s d) -> p s d", d=half)
            tk = "ropetab"
            P.tt(eng, t1, x1, cos, ALU.mult, [skey, tk], ["rt1"])
            P.tt(eng, t2, x2, sin, ALU.mult, [skey, tk], ["rt2"])
            P.tt(eng, t3, x2, cos, ALU.mult, [skey, tk], ["rt3"])
            P.tt(eng, t4, x1, sin, ALU.mult, [skey, tk], ["rt4"])
            P.tt(eng, x1, t1, t2, ALU.subtract, ["rt1", "rt2", skey], [skey])
            P.tt(eng, x2, t3, t4, ALU.add, ["rt3", "rt4", skey], [skey])

        rtmp = [sb("rtmp%d" % i, [128, 64]) for i in range(4)]

        def attn_core(ls, gr, nq_cols, q_c0, heads_fn, key_tiles, lam_ap, kind, out_fn):
            pass

        def mixer_diff(gr, i, j, L):
            n = gr.n
            lam_init = 0.8 - 0.6 * math.exp(-0.3 * i)
            P.mark("L%d %s%d qkv" % (i, gr.kind, gr.g))
            norm_mod(gr, i, 1)
            QT = L["QT"]; onT = L["onT"]
            kdst = akp if gr.kind == "p" else aks
            vdst = avp if gr.kind == "p" else avs
            kvkey = ("kv", j, gr.kind, gr.g)
            for cb in range(6):
                wt, wk = wload(wview(("aqkv", j), w_a_qkv[j], cb * 512, 512), NCH, 512)
                for (c0, nt, ti) in gr.tiles:
                    b = next_ps()
                    for k in range(NCH):
                        P.mm(PS[b][0:nt, :], hT[:, k, c0:c0 + nt], wt[:, k, :], k == 0, k == NCH - 1, ["hT", wk], [psk(b)])
                    s = next_sl()
                    sk = ("sl", s)
                    P.cp("act", SL[s][0:nt, :], PS[b][0:nt, :], [psk(b)], [sk])
                    if cb < 4:
                        rope_slab(SL[s][0:nt, :].rearrange("p (s d) -> p s d", d=64), sk, nt, ropeA[0:nt, ti, 0:8],
                                  ropeA[0:nt, ti, 8:16], 8, 0, 8, eng="pool" if cb % 2 else "dve")
                    if cb < 2:
                        qb = L["qb"]
                        P.cp("act", qb[0:nt, :], SL[s][0:nt, :], [sk], ["qb"])
                        tb = 2
                        for q in range(4):
                            P.tr(psb(tb)[:, q * 128:q * 128 + nt], qb[0:nt, q * 128:(q + 1) * 128], identB[0:nt, 0:nt],
                                 ["qb", "identB"], [psk(tb)])
                        P.cp("dve", QT[:, cb * 4:cb * 4 + 4, c0:c0 + nt],
                             psb(tb)[:, 0:512].rearrange("p (q t) -> p q t", t=128)[:, :, 0:nt], [psk(tb)], ["QT"])
                    elif cb < 4:
                        P.dma("sp", kdst[j, gr.c0 + c0:gr.c0 + c0 + nt, (cb - 2) * 512:(cb - 1) * 512], SL[s][0:nt, :],
                              [sk], [kvkey], semkey=("slo", s))
                    else:
                        P.dma("sp", vdst[j, gr.c0 + c0:gr.c0 + c0 + nt, (cb - 4) * 512:(cb - 3) * 512], SL[s][0:nt, :],
                              [sk], [kvkey], semkey=("slo", s))
            P.mark("L%d %s%d attn" % (i, gr.kind, gr.g))
            scale = 64 ** -0.5
            for si, (r, a, nq) in enumerate(gr.segs):
                if gr.kind == "p":
                    nkt = (gr.c0 + n) // 128
                    ktl = []
                    for kt in range(nkt):
                        gk = (kt * 128) // G
                        ktl.append((akp[j, kt * 128:(kt + 1) * 128, :], avp[j, kt * 128:(kt + 1) * 128, :], 128,
                                    kt * 128 - gr.c0, ("kv", j, "p", gk)))
                else:
                    s_ = si
                    ktl = []
                    for kt in range(NPT):
                        ktl.append((ca_k[j, s_, kt * 128:(kt + 1) * 128, :], ca_v[j, s_, kt * 128:(kt + 1) * 128, :], 128,
                                    -1, None))
                    ktl.append((aks[j, s_ * TS:(s_ + 1) * TS, :], avs[j, s_ * TS:(s_ + 1) * TS, :], TS, -1, kvkey))
                for hd in range(H_A):
                    first = True
                    nk_total = len(ktl)
                    for kti, (ksrc, vsrc, nk, dcol, dep) in enumerate(ktl):
                        last = kti == nk_total - 1
                        cq0 = max(dcol, 0)
                        kb = L["kb"][kti % 2]; vb = L["vb"][kti % 2]
                        kbk = ("kb", kti % 2); vbk = ("vb", kti % 2)
                        deps = [dep] if dep is not None else []
                        P.dma("pool", kb[0:nk, :], ksrc[:, hd * 128:(hd + 1) * 128], deps, [kbk])
                        P.dma("pool", vb[0:nk, :], vsrc[:, hd * 128:(hd + 1) * 128], deps, [vbk])
                        tb = 2
                        P.tr(psb(tb)[:, 0:nk], kb[0:nk, :], identB[0:nk, 0:nk], [kbk, "identB"], [psk(tb)])
                        kT = L["kT"][kti % 2]; kTk = ("kT", kti % 2)
                        P.cp("dve", kT[:, 0:nk], psb(tb)[:, 0:nk], [psk(tb)], [kTk])
                        qa, qn = a + cq0, nq - cq0
                        P.mm(PS[0][0:nk, 0:qn], kT[0:64, 0:nk], QT[0:64, hd, qa:qa + qn], True, True, [kTk, "QT"], [psk(0)])
                        P.mm(PS[1][0:nk, 0:qn], kT[64:128, 0:nk], QT[64:128, hd, qa:qa + qn], True, True, [kTk, "QT"], [psk(1)])
                        e1 = L["e1"][kti % 2]; e2 = L["e2"][kti % 2]
                        e1k = ("e1", kti % 2); e2k = ("e2", kti % 2)
                        P.act(e1[0:nk, 0:qn], PS[0][0:nk, 0:qn], AF.Exp, [psk(0)], [e1k], scale=scale)
                        P.act(e2[0:nk, 0:qn], PS[1][0:nk, 0:qn], AF.Exp, [psk(1)], [e2k], scale=scale)
                        if dcol >= 0:
                            P.memset("pool", e1[64:128, 0:64], 0.0, [e1k])
                            P.memset("pool", e2[64:128, 0:64], 0.0, [e2k])
                        P.mm(PS[4][:, cq0:nq], vb[0:nk, :], e1[0:nk, 0:qn], first, last, [vbk, e1k], [psk(4)])
                        P.mm(PS[5][:, cq0:nq], vb[0:nk, :], e2[0:nk, 0:qn], first, last, [vbk, e2k], [psk(5)])
                        P.mm(PS[6][:, cq0:nq], onesB[0:nk, :], e1[0:nk, 0:qn], first, last, ["onesB", e1k], [psk(6)])
                        P.mm(PS[7][:, cq0:nq], onesB[0:nk, :], e2[0:nk, 0:qn], first, last, ["onesB", e2k], [psk(7)])
                        first = False
                    r1 = L["r1"]; r2 = L["r2"]; o1 = L["o1"]; o2 = L["o2"]
                    P.op("dve", lambda e: e.reciprocal(out=r1[:, 0:nq], in_=PS[6][:, 0:nq]), [psk(6)], ["r1"])
                    P.op("dve", lambda e: e.reciprocal(out=r2[:, 0:nq], in_=PS[7][:, 0:nq]), [psk(7)], ["r2"])
                    P.tt("dve", o1[:, 0:nq], PS[4][:, 0:nq], r1[:, 0:nq], ALU.mult, [psk(4), "r1"], ["o1"])
                    P.tt("dve", o2[:, 0:nq], PS[5][:, 0:nq], r2[:, 0:nq], ALU.mult, [psk(5), "r2"], ["o2"])
                    P.stt(o1[:, 0:nq], o2[:, 0:nq], L["nlam"][:, 0:1], o1[:, 0:nq], ALU.mult, ALU.add, ["o1", "o2", "nlam"], ["o1"])
                    sqi = st["sq"] % 2
                    st["sq"] += 1
                    P.act(sqr[sqi][:, 0:nq], o1[:, 0:nq], AF.Square, ["o1"], [("sq", sqi)])
                    P.mm(PS[3][:, 0:nq], onesB[:], sqr[sqi][:, 0:nq], True, True, [("sq", sqi), "onesB"], [psk(3)])
                    P.act(r1[:, 0:nq], PS[3][:, 0:nq], AF.Sqrt, [psk(3), "epsT"], ["r1"], bias=epsT[:, 0:1], scale=1.0 / 128)
                    P.op("dve", lambda e: e.reciprocal(out=r1[:, 0:nq], in_=r1[:, 0:nq]), ["r1"], ["r1"])
                    P.stt(onT[:, hd, a:a + nq], o1[:, 0:nq], L["gsub"][:, 0:1], r1[:, 0:nq], ALU.mult, ALU.mult,
                          ["o1", "gsub", "r1"], ["onT"])
            P.mark("L%d %s%d wo" % (i, gr.kind, gr.g))
            for ch in range(2):
                wt, wk = wload(wview(("ao", j), w_a_o[j], ch * 512, 512), NCH, 512)
                for cc in range(4):
                    c = ch * 4 + cc
                    b = next_ps()
                    for hd in range(H_A):
                        P.mm(PS[b][:, 0:n], wt[:, hd, cc * 128:(cc + 1) * 128], onT[:, hd, 0:n], hd == 0, hd == H_A - 1,
                             [wk, "onT"], [psk(b)])
                    resid_add(gr, i, 1, c, PS[b], psk(b))

        def setup_diff(i, j, ls):
            L = {}
            L["QT"] = sb("QT", [128, H_A, G], BF16, ls)
            L["onT"] = sb("onT", [128, H_A, G], BF16, ls)
            L["qb"] = sb("qb", [128, 512], BF16, ls)
            L["kb"] = [sb("kb%d" % q, [128, 128], BF16, ls) for q in range(2)]
            L["vb"] = [sb("vb%d" % q, [128, 128], BF16, ls) for q in range(2)]
            L["kT"] = [sb("kT%d" % q, [128, 128], BF16, ls) for q in range(2)]
            L["e1"] = [sb("e1%d" % q, [128, G], BF16, ls) for q in range(2)]
            L["e2"] = [sb("e2%d" % q, [128, G], BF16, ls) for q in range(2)]
            L["r1"] = sb("r1", [128, G], F32, ls); L["r2"] = sb("r2", [128, G], F32, ls)
            L["o1"] = sb("o1", [128, G], F32, ls); L["o2"] = sb("o2", [128, G], F32, ls)
            L["nlam"] = sb("nlam", [128, 1], F32, ls); L["gsub"] = sb("gsub", [128, 1], F32, ls)
            lam_init = 0.8 - 0.6 * math.exp(-0.3 * i)
            lv = sb("lv", [1, 4, 64], F32, ls); lp = sb("lp", [1, 2, 64], F32, ls); lsum = sb("lsum", [1, 2], F32, ls)
            lam1 = sb("lam1", [1, 1], F32, ls)
            P.dma("sp", lv[:], a_lambda[j:j + 1, :, :], (), ["lv"])
            P.tt("dve", lp[:], lv[:, 0:4:2, :], lv[:, 1:4:2, :], ALU.mult, ["lv"], ["lp"])
            P.op("dve", lambda e: e.tensor_reduce(out=lsum[:], in_=lp[:], axis=AX.X, op=ALU.add), ["lp"], ["lsum"])
            P.act(lsum[:], lsum[:], AF.Exp, ["lsum"], ["lsum"])
            P.tt("dve", lam1[:], lsum[:, 1:2], lsum[:, 0:1], ALU.subtract, ["lsum"], ["lam1"])
            P.ts("dve", lam1[:], lam1[:], -lam_init, None, ALU.add, None, ["lam1"], ["lam1"])
            P.mm(PS[3][:, 0:1], onesF[0:1, :], lam1[:], True, True, ["onesF", "lam1"], [psk(3)])
            P.cp("dve", L["nlam"][:], PS[3][:, 0:1], [psk(3)], ["nlam"])
            s = next_sl()
            P.dma("sp", SL[s][0:1, 0:128], g_a_sub[j:j + 1, :], (), [("sl", s)])
            P.tr(PS[3][:, 0:1], SL[s][0:1, 0:128], identF[0:1, 0:1], [("sl", s), "identF"], [psk(3)])
            P.ts("dve", L["gsub"][:], PS[3][:, 0:1], 1.0 - lam_init, None, ALU.mult, None, [psk(3)], ["gsub"])
            return L


        def setup_mlstm(i, j, ls):
            L = {}
            L["Cst"] = sb("Cst", [128, H_B, 257], F32, ls)
            L["Cb"] = sb("Cb", [128, H_B, 257], BF16, ls)
            L["mrow"] = sb("mrow", [4, 1], F32, ls)
            L["qT"] = sb("qT", [128, H_B, 64], BF16, ls)
            L["kT"] = sb("mkT", [128, H_B, 64], BF16, ls)
            L["ktm"] = sb("ktm", [64, 1, 512], F32, ls)
            L["Vaug"] = sb("Vaug", [64, 1, H_B, 257], BF16, ls)
            L["gsig"] = sb("gsig", [64, 1, 1024], F32, ls)
            L["hnT"] = sb("hnT", [128, NCH, G], BF16, ls)
            L["goutT"] = sb("goutT", [128, NCH], F32, ls)
            L["bg"] = sb("bg", [4, 2], F32, ls)
            L["tri"] = sb("tri", [64, 64], F32, ls)
            L["sel"] = sb("sel", [4, 256], F32, ls)
            L["one1"] = sb("one1", [128, 1], F32, ls)
            for nm in ("rI", "rF", "rA", "rE", "rL", "rB", "rU", "rCM", "rM", "ra", "rwr", "remt"):
                L[nm] = sb("ml_" + nm, [4, 64], F32, ls)
            L["nML"] = sb("nML", [4, 1], F32, ls)
            L["dg"] = sb("dg", [4, 4], F32, ls)
            L["cols"] = sb("cols", [64, 16], F32, ls)
            L["dbc"] = sb("dbc", [128, 4], F32, ls)
            L["z"] = sb("mz", [64, 64], F32, ls)
            L["W"] = sb("mW", [64, 64], F32, ls)
            L["Dm"] = sb("mDm", [64, 64], BF16, ls)
            L["itmp"] = sb("itmp", [64, 257], F32, ls)
            L["ND"] = sb("ND", [64, 257], F32, ls)
            L["sc"] = sb("msc", [64, 8], F32, ls)
            L["hn"] = sb("mhn", [64, 1024], BF16, ls)
            L["kw"] = sb("mkw", [64, 128], BF16, ls)
            L["nrow"] = sb("nrow", [4, 128], F32, ls)
            load_rows_T(L["goutT"][:], "goutT", g_b_out.rearrange("i (c p) -> (i c) p", p=128), NCH)
            P.dma("sp", L["bg"][:, 0:1], b_b_gates[0:1, 0:4].rearrange("o f -> f o"), (), ["bg"])
            P.dma("sp", L["bg"][:, 1:2], b_b_gates[0:1, 4:8].rearrange("o f -> f o"), (), ["bg"])
            P.dma("sp", L["tri"][:], k_tri[:, :], (), ["tri"])
            P.dma("sp", L["sel"][:], k_sel[:, :], (), ["sel"])
            P.memset("dve", L["one1"][:], 1.0, ["one1"])
            return L

        def mlstm_state_init(L, seq):
            Cst = L["Cst"]
            if seq is None:
                P.memset("dve", Cst[:], 0.0, ["Cst"])
                P.memset("dve", L["mrow"][:], 0.0, ["mrow"])
            else:
                P.dma("sp", Cst[:, :, 0:256], sb_c[seq].rearrange("h d v -> d h v"), (), ["Cst"])
                load_rows_T(Cst[:, :, 256], "Cst", sb_n[seq], 4)
                P.dma("sp", L["mrow"][:], sb_m[seq:seq + 1, :].rearrange("o f -> f o"), (), ["mrow"])
            P.cp("act", L["Cb"][:], Cst[:], ["Cst"], ["Cb"])

        def mlstm_state_out(L, dc, dn, dm):
            Cst = L["Cst"]
            P.dma("sp", dc.rearrange("h d v -> d h v"), Cst[:, :, 0:256], ["Cst"], (), semkey="Cst_o")
            P.tr(PS[3][0:4, 0:128], Cst[:, :, 256], identF[:, :], ["Cst", "identF"], [psk(3)])
            P.cp("dve", L["nrow"][:], PS[3][0:4, 0:128], [psk(3)], ["nrow"])
            P.dma("sp", dn, L["nrow"][:], ["nrow"], (), semkey="nrow_o")
            P.dma("sp", dm, L["mrow"][:], ["mrow"], (), semkey="mrow_o")

        def mlstm_unit(gr, L, u0, nu, Lc):
            nchk = nu // Lc
            wj = w_b_in[0]
            qT, kT, ktm, Vaug, gsig = L["qT"], L["kT"], L["ktm"], L["Vaug"], L["gsig"]
            kscale = DQK_B ** -0.5
            for blk, dst, scl in ((0, qT, 1.0), (1, kT, kscale)):
                wt, wk = wload(wview(("bin", 0), wj, blk * 512, 512), NCH, 512)
                for h in range(H_B):
                    b = next_ps()
                    for k in range(NCH):
                        P.mm(PS[b][:, 0:nu], wt[:, k, h * 128:(h + 1) * 128], hT[:, k, u0:u0 + nu], k == 0, k == NCH - 1,
                             [wk, "hT"], [psk(b)])
                    P.act(dst[:, h, 0:nu], PS[b][:, 0:nu], AF.Copy, [psk(b)], ["mqk"], scale=scl)
            for blk in range(1, 6):
                wt, wk = wload(wview(("bin", 0), wj, blk * 512, 512), NCH, 512)
                for ck in range(nchk):
                    b = next_ps()
                    cc0 = u0 + ck * Lc
                    for k in range(NCH):
                        P.mm(PS[b][0:Lc, :], hT[:, k, cc0:cc0 + Lc], wt[:, k, :], k == 0, k == NCH - 1, ["hT", wk], [psk(b)])
                    if blk == 1:
                        P.act(ktm[0:Lc, ck, :], PS[b][0:Lc, :], AF.Copy, [psk(b)], ["ktm"], scale=kscale)
                    elif blk < 4:
                        hh = (blk - 2) * 2
                        P.cp("dve", Vaug[0:Lc, ck, hh:hh + 2, 0:256], PS[b][0:Lc, :].rearrange("p (h v) -> p h v", v=256),
                             [psk(b)], ["Vaug"])
                    else:
                        o0 = (blk - 4) * 512
                        P.act(gsig[0:Lc, ck, o0:o0 + 512], PS[b][0:Lc, :], AF.Sigmoid, [psk(b)], ["gsig"])
            for ck in range(nchk):
                P.memset("pool", Vaug[0:Lc, ck, :, 256:257], 1.0, ["Vaug"])
            wt, wk = wload(wview(("bin", 0), wj, 3072, 8), NCH, 8)
            gb = 3
            for k in range(NCH):
                P.mm(PS[gb][0:4, 0:nu], wt[:, k, 0:4], hT[:, k, u0:u0 + nu], k == 0, k == NCH - 1, [wk, "hT"], [psk(gb)])
            for k in range(NCH):
                P.mm(PS[gb][0:4, 128:128 + nu], wt[:, k, 4:8], hT[:, k, u0:u0 + nu], k == 0, k == NCH - 1, [wk, "hT"], [psk(gb)])
            rI, rF, rA, rE, rL, rB, rU, rCM, rM, ra, rwr, remt = (L[x] for x in
                ("rI", "rF", "rA", "rE", "rL", "rB", "rU", "rCM", "rM", "ra", "rwr", "remt"))
            bg = L["bg"]
            P.act(rI[:, 0:nu], PS[gb][0:4, 0:nu], AF.Identity, [psk(gb), "bg"], ["rI"], bias=bg[:, 0:1], scale=1.0)
            P.act(rF[:, 0:nu], PS[gb][0:4, 128:128 + nu], AF.Identity, [psk(gb), "bg"], ["rF"], bias=bg[:, 1:2], scale=1.0)
            P.act(rA[:, 0:nu], rF[:, 0:nu], AF.Abs, ["rF"], ["rA"])
            P.act(rE[:, 0:nu], rA[:, 0:nu], AF.Exp, ["rA"], ["rE"], scale=-1.0)
            P.act(rL[:, 0:nu], rE[:, 0:nu], AF.Ln, ["rE", "one1"], ["rL"], bias=L["one1"][0:4, 0:1], scale=1.0)
            P.ts("dve", rA[:, 0:nu], rF[:, 0:nu], 0.0, None, ALU.min, None, ["rF", "rA"], ["rA"])
            P.tt("dve", rF[:, 0:nu], rA[:, 0:nu], rL[:, 0:nu], ALU.subtract, ["rA", "rL"], ["rF"])
            P.memset("dve", rE[:, 0:nu], 0.0, ["rE"])
            mrow = L["mrow"]
            for ck in range(nchk):
                a0, a1 = ck * Lc, (ck + 1) * Lc
                P.op("dve", lambda e: e.tensor_tensor_scan(out=rB[:, a0:a1], data0=rF[:, a0:a1], data1=rE[:, a0:a1],
                                                           initial=0.0, op0=ALU.add, op1=ALU.add), ["rF", "rE"], ["rB"])
                P.tt("dve", rU[:, a0:a1], rI[:, a0:a1], rB[:, a0:a1], ALU.subtract, ["rI", "rB"], ["rU"])
                P.op("dve", lambda e: e.tensor_tensor_scan(out=rCM[:, a0:a1], data0=rU[:, a0:a1], data1=rU[:, a0:a1],
                                                           initial=-1e30, op0=ALU.max, op1=ALU.max), ["rU"], ["rCM"])
                P.ts("dve", rM[:, a0:a1], rCM[:, a0:a1], mrow[:, 0:1], None, ALU.max, None, ["rCM", "mrow"], ["rM"])
                P.act(ra[:, a0:a1], rM[:, a0:a1], AF.Exp, ["rM", "mrow"], ["ra"], bias=mrow[:, 0:1], scale=-1.0)
                P.ts("dve", L["nML"][:], rM[:, a1 - 1:a1], -1.0, None, ALU.mult, None, ["rM"], ["nML"])
                P.act(rwr[:, a0:a1], rU[:, a0:a1], AF.Exp, ["rU", "nML"], ["rwr"], bias=L["nML"][:, 0:1], scale=1.0)
                P.tt("dve", remt[:, a0:a1], rB[:, a0:a1], rM[:, a0:a1], ALU.add, ["rB", "rM"], ["remt"])
                P.act(remt[:, a0:a1], remt[:, a0:a1], AF.Exp, ["remt"], ["remt"], scale=-1.0)
                P.tt("dve", mrow[:], rB[:, a1 - 1:a1], rM[:, a1 - 1:a1], ALU.add, ["rB", "rM", "ra"], ["mrow"])
                cb_ = 3
                for qi, rr in enumerate((rU, ra, rwr, remt)):
                    P.tr(PS[cb_][0:Lc, 256 + qi * 4:256 + qi * 4 + 4], rr[:, a0:a1], identF[0:4, 0:4],
                         ["rU", "ra", "rwr", "remt", "identF"], [psk(cb_)])
                cols = L["cols"]
                P.cp("dve", cols[0:Lc, :], PS[cb_][0:Lc, 256:272], [psk(cb_)], ["cols"])
                P.ts("dve", L["dg"][:], identF[0:4, 0:4], ra[:, a1 - 1:a1], None, ALU.mult, None, ["identF", "ra"], ["dg"])
                P.mm(PS[cb_][:, 280:284], onesF[0:4, :], L["dg"][:], True, True, ["onesF", "dg"], [psk(cb_)])
                P.cp("dve", L["dbc"][:], PS[cb_][:, 280:284], [psk(cb_)], ["dbc"])
                cq0 = ck * Lc
                for h in range(H_B):
                    P.mm(PS[4][0:Lc, 0:Lc], kT[:, h, cq0:cq0 + Lc], qT[:, h, cq0:cq0 + Lc], True, True, ["mqk"], [psk(4)])
                    P.mm(PS[4][0:Lc, 64:64 + Lc], L["sel"][:, h * 64:h * 64 + Lc], rM[:, a0:a1], True, True, ["sel", "rM"], [psk(4)])
                    P.ts("dve", L["z"][0:Lc, 0:Lc], PS[4][0:Lc, 64:64 + Lc], cols[0:Lc, h:h + 1], 0.0, ALU.subtract, ALU.max,
                         [psk(4), "cols"], ["mz"])
                    P.act(L["W"][0:Lc, 0:Lc], L["z"][0:Lc, 0:Lc], AF.Exp, ["mz"], ["mW"], scale=-1.0)
                    P.tt("pool", L["W"][0:Lc, 0:Lc], L["W"][0:Lc, 0:Lc], L["tri"][0:Lc, 0:Lc], ALU.mult, ["mW", "tri"], ["mW"])
                    P.tt("dve", L["Dm"][0:Lc, 0:Lc], PS[4][0:Lc, 0:Lc], L["W"][0:Lc, 0:Lc], ALU.mult, [psk(4), "mW"], ["mDm"])
                    P.mm(PS[5][0:Lc, 0:257], L["Dm"][0:Lc, 0:Lc], Vaug[0:Lc, ck, h, :], True, True, ["mDm", "Vaug"], [psk(5)])
                    P.mm(PS[6][0:Lc, 0:257], qT[:, h, cq0:cq0 + Lc], L["Cb"][:, h, :], True, True, ["mqk", "Cb"], [psk(6)])
                    P.act(L["itmp"][0:Lc, :], PS[6][0:Lc, 0:257], AF.Copy, [psk(6), "cols"], ["itmp"], scale=cols[0:Lc, 4 + h:5 + h])
                    P.tt("dve", L["ND"][0:Lc, :], L["itmp"][0:Lc, :], PS[5][0:Lc, 0:257], ALU.add, ["itmp", psk(5)], ["ND"])
                    sc = L["sc"]
                    P.act(sc[0:Lc, 6:7], L["ND"][0:Lc, 256:257], AF.Abs, ["ND"], ["msc6"])
                    P.tt("dve", sc[0:Lc, 0:1], sc[0:Lc, 6:7], cols[0:Lc, 12 + h:13 + h], ALU.max, ["msc6", "cols"], ["msc"])
                    P.op("dve", lambda e: e.reciprocal(out=sc[0:Lc, 1:2], in_=sc[0:Lc, 0:1]), ["msc"], ["msc"])
                    P.act(L["itmp"][0:Lc, 0:256], L["ND"][0:Lc, 0:256], AF.Square, ["ND", "msc"], ["itmp", "msc2"],
                          scale=sc[0:Lc, 1:2], accum=sc[0:Lc, 2:3])
                    P.act(sc[0:Lc, 3:4], sc[0:Lc, 2:3], AF.Sqrt, ["msc2", "epsT"], ["msc3"], bias=epsT[0:Lc, 0:1], scale=1.0 / 256)
                    P.op("dve", lambda e: e.reciprocal(out=sc[0:Lc, 4:5], in_=sc[0:Lc, 3:4]), ["msc3"], ["msc4"])
                    P.tt("dve", sc[0:Lc, 5:6], sc[0:Lc, 4:5], sc[0:Lc, 1:2], ALU.mult, ["msc4", "msc"], ["msc5"])
                    P.stt(L["hn"][0:Lc, h * 256:(h + 1) * 256], L["ND"][0:Lc, 0:256], sc[0:Lc, 5:6],
                          gsig[0:Lc, ck, h * 256:(h + 1) * 256], ALU.mult, ALU.mult, ["ND", "msc5", "gsig"], [("hn", h)])
                    P.ts("pool", L["kw"][0:Lc, :], ktm[0:Lc, ck, h * 128:(h + 1) * 128], cols[0:Lc, 8 + h:9 + h], None,
                         ALU.mult, None, ["ktm", "cols"], ["mkw"])
                    P.mm(PS[7][:, 0:257], L["kw"][0:Lc, :], Vaug[0:Lc, ck, h, :], True, True, ["mkw", "Vaug"], [psk(7)])
                    P.stt(L["Cst"][:, h, :], L["Cst"][:, h, :], L["dbc"][:, h:h + 1], PS[7][:, 0:257], ALU.mult, ALU.add,
                          ["Cst", "dbc", psk(7)], ["Cst"])
                    P.cp("act", L["Cb"][:, h, :], L["Cst"][:, h, :], ["Cst"], ["Cb"])
                tb = 2
                for q in range(NCH):
                    P.tr(psb(tb)[:, q * 64:q * 64 + Lc], L["hn"][0:Lc, q * 128:(q + 1) * 128], identB[0:Lc, 0:Lc],
                         [("hn", q // 2), "identB"], [psk(tb)])
                P.tt("dve", L["hnT"][:, :, u0 + a0:u0 + a1], psb(tb)[:, 0:512].rearrange("p (q t) -> p q t", t=64)[:, :, 0:Lc],
                     L["goutT"][:, :].unsqueeze(2).broadcast_to([128, NCH, Lc]), ALU.mult, [psk(tb), "goutT"], ["hnT"])

        def mixer_mlstm(gr, i, j, L):
            n = gr.n
            P.mark("L%d %s%d mlstm" % (i, gr.kind, gr.g))
            norm_mod(gr, i, 1)
            if gr.kind == "p":
                if gr.g == 0:
                    mlstm_state_init(L, None)
                for c0 in range(0, n, 64):
                    mlstm_unit(gr, L, c0, 64, 64)
                if gr.g == TP // G - 1:
                    mlstm_state_out(L, bcp, bnp, bmp)
            else:
                for s_ in range(NS):
                    mlstm_state_init(L, s_)
                    mlstm_unit(gr, L, s_ * TS, TS, TS)
                    mlstm_state_out(L, bcs[s_], bns[s_], bms[s_])
            for ch in range(2):
                wt, wk = wload(wview(("bout", j), w_b_out[j], ch * 512, 512), NCH, 512)
                for cc in range(4):
                    c = ch * 4 + cc
                    b = next_ps()
                    for q in range(NCH):
                        P.mm(PS[b][:, 0:n], wt[:, q, cc * 128:(cc + 1) * 128], L["hnT"][:, q, 0:n], q == 0, q == NCH - 1,
                             [wk, "hnT"], [psk(b)])
                    resid_add(gr, i, 1, c, PS[b], psk(b))

        def setup_mla(i, j, ls):
            L = {}
            L["QTn"] = sb("QTn", [128, H_C, G], BF16, ls)
            L["QTr"] = sb("QTr", [64, H_C, G], BF16, ls)
            L["onT"] = sb("c_onT", [128, H_C, G], BF16, ls)
            L["cq"] = sb("cq", [128, 3, G], F32, ls)
            L["cqn"] = sb("cqn", [128, 3, G], BF16, ls)
            L["gqT"] = sb("gqT", [128, 3], F32, ls)
            L["gkv"] = sb("gkv", [128, 256], F32, ls)
            L["rc"] = [sb("rc%d" % q, [128, 64], F32, ls) for q in range(2)]
            L["qb"] = sb("c_qb", [128, 384], BF16, ls)
            L["kvb"] = sb("kvb", [128, 256], BF16, ls)
            L["kvT"] = sb("kvT", [128, 2, 2, 128], BF16, ls)
            L["ob"] = [sb("c_ob%d" % q, [128, 512], BF16, ls) for q in range(2)]
            L["kb"] = [sb("c_kb%d" % q, [128, 128], BF16, ls) for q in range(2)]
            L["krb"] = [sb("c_krb%d" % q, [128, 64], BF16, ls) for q in range(2)]
            L["vb"] = [sb("c_vb%d" % q, [128, 128], BF16, ls) for q in range(2)]
            L["kT"] = [sb("c_kT%d" % q, [128, 128], BF16, ls) for q in range(2)]
            L["krT"] = [sb("c_krT%d" % q, [64, 128], BF16, ls) for q in range(2)]
            L["e"] = [sb("c_e%d" % q, [128, G], BF16, ls) for q in range(2)]
            L["r"] = sb("c_r", [128, G], F32, ls)
            L["sc"] = sb("c_sc", [128, 4], F32, ls)
            L["junk"] = sb("c_junk", [128, 256], F32, ls)
            L["n"] = {"rc": 0, "ob": 0}
            load_rows_T(L["gqT"][:], "gqT", g_c_q.rearrange("i (c p) -> (i c) p", p=128), 3)
            P.dma("sp", L["gkv"][:], g_c_kv[0, :].partition_broadcast(128), (), ["gkv"])
            return L

        def mla_up(L, kvT_ap, nt, dst_rows, dkey):
            for cb in range(4):
                wt, wk = wload(wview(("cukv", 0), w_c_ukv[0], cb * 512, 512), 2, 512)
                b = next_ps()
                for k in range(2):
                    P.mm(PS[b][0:nt, :], kvT_ap[:, k, 0:nt], wt[:, k, :], k == 0, k == 1, ["kvT", wk], [psk(b)])
                oi = L["n"]["ob"] % 2
                L["n"]["ob"] += 1
                P.cp("act", L["ob"][oi][0:nt, :], PS[b][0:nt, :], [psk(b)], [("cob", oi)])
                P.dma("sp", dst_rows[:, cb * 512:(cb + 1) * 512], L["ob"][oi][0:nt, :], [("cob", oi)], [dkey], semkey=("cobo", oi))

        def latent_T(L, src_bf_ap, nt, slot):
            tb = 2
            for k in range(2):
                P.tr(psb(tb)[:, k * 128:k * 128 + nt], src_bf_ap[:, k * 128:(k + 1) * 128], identB[0:nt, 0:nt],
                     ["kvb", "identB"], [psk(tb)])
            P.cp("dve", L["kvT"][:, slot, :, 0:nt], psb(tb)[:, 0:256].rearrange("p (k t) -> p k t", t=128)[:, :, 0:nt],
                 [psk(tb)], ["kvT"])

        def mixer_mla(gr, i, j, L):
            n = gr.n
            P.mark("L%d %s%d mlaproj" % (i, gr.kind, gr.g))
            norm_mod(gr, i, 1)
            QTn, QTr, onT = L["QTn"], L["QTr"], L["onT"]
            kvdst = ckvp if gr.kind == "p" else ckvs
            krdst = ckrp if gr.kind == "p" else ckrs
            ckey = ("ckv", gr.kind, gr.g)
            wt, wk = wload(wview(("cdq", 0), w_c_dq[0], 0, Q_LORA), NCH, Q_LORA)
            sb_ = 3
            for f in range(3):
                b = next_ps()
                for k in range(NCH):
                    P.mm(PS[b][:, 0:n], wt[:, k, f * 128:(f + 1) * 128], hT[:, k, 0:n], k == 0, k == NCH - 1, [wk, "hT"], [psk(b)])
                P.cp("dve", L["cq"][:, f, 0:n], PS[b][:, 0:n], [psk(b)], ["cq"])
                si = st["sq"] % 2
                st["sq"] += 1
                P.act(sqr[si][:, 0:n], PS[b][:, 0:n], AF.Square, [psk(b)], [("sq", si)])
                P.mm(PS[sb_][:, 0:n], onesB[:], sqr[si][:, 0:n], f == 0, f == 2, [("sq", si), "onesB"], [psk(sb_)])
            P.act(L["r"][:, 0:n], PS[sb_][:, 0:n], AF.Sqrt, [psk(sb_), "epsT"], ["c_r"], bias=epsT[:, 0:1], scale=1.0 / Q_LORA)
            P.op("dve", lambda e: e.reciprocal(out=L["r"][:, 0:n], in_=L["r"][:, 0:n]), ["c_r"], ["c_r"])
            for f in range(3):
                P.stt(L["cqn"][:, f, 0:n], L["cq"][:, f, 0:n], L["gqT"][:, f:f + 1], L["r"][:, 0:n], ALU.mult, ALU.mult,
                      ["cq", "gqT", "c_r"], ["cqn"])
            for cb in range(4):
                wt, wk = wload(wview(("cuq", 0), w_c_uq[0], cb * 384, 384), 3, 384)
                for (c0, nt, ti) in gr.tiles:
                    b = next_ps()
                    for k in range(3):
                        P.mm(PS[b][0:nt, 0:384], L["cqn"][:, k, c0:c0 + nt], wt[:, k, 0:384], k == 0, k == 2, ["cqn", wk], [psk(b)])
                    s = next_sl(); sk = ("sl", s)
                    P.cp("act", SL[s][0:nt, 0:384], PS[b][0:nt, 0:384], [psk(b)], [sk])
                    ri = L["n"]["rc"] % 2
                    L["n"]["rc"] += 1
                    row0 = (gr.c0 + c0) if gr.kind == "p" else TP
                    P.dma("sp", L["rc"][ri][0:nt, :], rope_c[row0:row0 + nt, :], (), [("rc", ri)])
                    P.res["ropetab"] = P.res[("rc", ri)]
                    rope_slab(SL[s][0:nt, 0:384].rearrange("p (s d) -> p s d", d=192), sk, nt, L["rc"][ri][0:nt, 0:32],
                              L["rc"][ri][0:nt, 32:64], 32, 128, 2)
                    P.res[("rc", ri)] = P.res["ropetab"]
                    P.cp("act", L["qb"][0:nt, :], SL[s][0:nt, 0:384], [sk], ["c_qb"])
                    tb = 2
                    for hh in range(2):
                        P.tr(psb(tb)[:, hh * 128:hh * 128 + nt], L["qb"][0:nt, hh * 192:hh * 192 + 128], identB[0:nt, 0:nt],
                             ["c_qb", "identB"], [psk(tb)])
                        P.tr(psb(tb)[0:64, 256 + hh * 128:256 + hh * 128 + nt], L["qb"][0:nt, hh * 192 + 128:hh * 192 + 192],
                             identB[0:nt, 0:nt], ["c_qb", "identB"], [psk(tb)])
                    P.cp("dve", QTn[:, cb * 2:cb * 2 + 2, c0:c0 + nt],
                         psb(tb)[:, 0:256].rearrange("p (q t) -> p q t", t=128)[:, :, 0:nt], [psk(tb)], ["QTn"])
                    P.cp("dve", QTr[:, cb * 2:cb * 2 + 2, c0:c0 + nt],
                         psb(tb)[0:64, 256:512].rearrange("p (q t) -> p q t", t=128)[:, :, 0:nt], [psk(tb)], ["QTr"])
            for tix, (c0, nt, ti) in enumerate(gr.tiles):
                wt, wk = wload(wview(("cdkv", 0), w_c_dkv[0], 0, 320), NCH, 320)
                b = next_ps()
                for k in range(NCH):
                    P.mm(PS[b][0:nt, 0:320], hT[:, k, c0:c0 + nt], wt[:, k, 0:320], k == 0, k == NCH - 1, ["hT", wk], [psk(b)])
                s = next_sl(); sk = ("sl", s)
                P.cp("act", SL[s][0:nt, 0:320], PS[b][0:nt, 0:320], [psk(b)], [sk])
                sc = L["sc"]
                P.act(L["junk"][0:nt, :], SL[s][0:nt, 0:256], AF.Square, [sk], ["c_junk", "c_sc"], accum=sc[0:nt, 0:1])
                P.act(sc[0:nt, 1:2], sc[0:nt, 0:1], AF.Sqrt, ["c_sc", "epsT"], ["c_sc1"], bias=epsT[0:nt, 0:1], scale=1.0 / KV_LORA)
                P.op("dve", lambda e: e.reciprocal(out=sc[0:nt, 2:3], in_=sc[0:nt, 1:2]), ["c_sc1"], ["c_sc2"])
                P.stt(SL[s][0:nt, 0:256], SL[s][0:nt, 0:256], sc[0:nt, 2:3], L["gkv"][0:nt, :], ALU.mult, ALU.mult,
                      [sk, "c_sc2", "gkv"], [sk])
                ri = L["n"]["rc"] % 2
                L["n"]["rc"] += 1
                row0 = (gr.c0 + c0) if gr.kind == "p" else TP
                P.dma("sp", L["rc"][ri][0:nt, :], rope_c[row0:row0 + nt, :], (), [("rc", ri)])
                P.res["ropetab"] = P.res[("rc", ri)]
                rope_slab(SL[s][0:nt, 256:320].rearrange("p (s d) -> p s d", d=64), sk, nt, L["rc"][ri][0:nt, 0:32],
                          L["rc"][ri][0:nt, 32:64], 32, 0, 1)
                P.res[("rc", ri)] = P.res["ropetab"]
                P.dma("sp", kvdst[gr.c0 + c0:gr.c0 + c0 + nt, :], SL[s][0:nt, 0:256], [sk], [ckey], semkey=("slo", s))
                P.dma("sp", krdst[gr.c0 + c0:gr.c0 + c0 + nt, :], SL[s][0:nt, 256:320], [sk], [ckey], semkey=("slo", s))
                P.cp("act", L["kvb"][0:nt, :], SL[s][0:nt, 0:256], [sk], ["kvb"])
                latent_T(L, L["kvb"][0:nt, :], nt, 0)
                if gr.kind == "p":
                    mla_up(L, L["kvT"][:, 0, :, :], nt, mla_kv_p[gr.c0 + c0:gr.c0 + c0 + nt, :], ckey)
                else:
                    for cb in range(4):
                        wt2, wk2 = wload(wview(("cukv", 0), w_c_ukv[0], cb * 512, 512), 2, 512)
                        b2 = next_ps()
                        for k in range(2):
                            P.mm(PS[b2][0:nt, :], L["kvT"][:, 0, k, 0:nt], wt2[:, k, :], k == 0, k == 1, ["kvT", wk2], [psk(b2)])
                        oi = L["n"]["ob"] % 2
                        L["n"]["ob"] += 1
                        P.cp("act", L["ob"][oi][0:nt, :], PS[b2][0:nt, :], [psk(b2)], [("cob", oi)])
                        for s_ in range(NS):
                            P.dma("sp", mla_kv_s[s_, PAST:PAST + TS, cb * 512:(cb + 1) * 512], L["ob"][oi][s_ * TS:(s_ + 1) * TS, :],
                                  [("cob", oi)], [ckey], semkey=("cobo", oi))
            if gr.kind == "s":
                for s_ in range(NS):
                    for pt in range(NPT):
                        P.dma("pool", L["kvb"][:, :], cc_kv[s_, pt * 128:(pt + 1) * 128, :], (), ["kvb"])
                        latent_T(L, L["kvb"][:, :], 128, 1)
                        mla_up(L, L["kvT"][:, 1, :, :], 128, mla_kv_s[s_, pt * 128:(pt + 1) * 128, :], ("cpast", s_))
            P.mark("L%d %s%d attn" % (i, gr.kind, gr.g))
            scale = 192 ** -0.5
            for si, (r, a, nq) in enumerate(gr.segs):
                if gr.kind == "p":
                    nkt = (gr.c0 + n) // 128
                    ktl = []
                    for kt in range(nkt):
                        gk = (kt * 128) // G
                        ktl.append((mla_kv_p[kt * 128:(kt + 1) * 128, :], ckrp[kt * 128:(kt + 1) * 128, :], 128,
                                    kt * 128 - gr.c0, [("ckv", "p", gk)]))
                else:
                    s_ = si
                    ktl = []
                    for kt in range(NPT):
                        ktl.append((mla_kv_s[s_, kt * 128:(kt + 1) * 128, :], cc_kr[s_, kt * 128:(kt + 1) * 128, :], 128, -1,
                                    [("cpast", s_)]))
                    ktl.append((mla_kv_s[s_, PAST:PAST + TS, :], ckrs[s_ * TS:(s_ + 1) * TS, :], TS, -1, [ckey]))
                nkt_ = len(ktl)
                for hd in range(H_C):
                    def stageA(kti):
                        (kvsrc, krsrc, nk, dcol, deps) = ktl[kti]
                        q2 = kti % 2
                        kb, krb, vb, kT, krT = L["kb"][q2], L["krb"][q2], L["vb"][q2], L["kT"][q2], L["krT"][q2]
                        P.dma("sp", kb[0:nk, :], kvsrc[:, hd * 256:hd * 256 + 128], deps, [("ckb", q2)])
                        P.dma("sp", vb[0:nk, :], kvsrc[:, hd * 256 + 128:hd * 256 + 256], deps, [("cvb", q2)])
                        P.dma("pool", krb[0:nk, :], krsrc, deps, [("ckrb", q2)])
                        tb = 2
                        P.tr(psb(tb)[:, 0:nk], kb[0:nk, :], identB[0:nk, 0:nk], [("ckb", q2), "identB"], [psk(tb)])
                        P.tr(psb(tb)[0:64, 128:128 + nk], krb[0:nk, :], identB[0:nk, 0:nk], [("ckrb", q2), "identB"], [psk(tb)])
                        P.cp("dve", kT[:, 0:nk], psb(tb)[:, 0:nk], [psk(tb)], [("ckT", q2)])
                        P.cp("dve", krT[:, 0:nk], psb(tb)[0:64, 128:128 + nk], [psk(tb)], [("ckrT", q2)])
                        cq0 = max(dcol, 0)
                        qa, qn = a + cq0, nq - cq0
                        P.mm(PS[q2][0:nk, 0:qn], kT[:, 0:nk], QTn[:, hd, qa:qa + qn], True, False, [("ckT", q2), "QTn"], [psk(q2)])
                        P.mm(PS[q2][0:nk, 0:qn], krT[0:64, 0:nk], QTr[0:64, hd, qa:qa + qn], False, True, [("ckrT", q2), "QTr"], [psk(q2)])

                    def stageBC(kti):
                        (kvsrc, krsrc, nk, dcol, deps) = ktl[kti]
                        q2 = kti % 2
                        e = L["e"][q2]; vb = L["vb"][q2]
                        first, last = kti == 0, kti == nkt_ - 1
                        cq0 = max(dcol, 0)
                        qn = nq - cq0
                        P.act(e[0:nk, 0:qn], PS[q2][0:nk, 0:qn], AF.Exp, [psk(q2)], [("ce", q2)], scale=scale)
                        if dcol >= 0:
                            P.memset("pool", e[64:128, 0:64], 0.0, [("ce", q2)])
                        P.mm(PS[4][:, cq0:nq], vb[0:nk, :], e[0:nk, 0:qn], first, last, [("cvb", q2), ("ce", q2)], [psk(4)])
                        P.mm(PS[5][:, cq0:nq], onesB[0:nk, :], e[0:nk, 0:qn], first, last, ["onesB", ("ce", q2)], [psk(5)])

                    stageA(0)
                    for kti in range(nkt_):
                        if kti + 1 < nkt_:
                            stageA(kti + 1)
                        stageBC(kti)
                    P.op("dve", lambda e_: e_.reciprocal(out=L["r"][:, 0:nq], in_=PS[5][:, 0:nq]), [psk(5)], ["c_r"])
                    P.tt("dve", onT[:, hd, a:a + nq], PS[4][:, 0:nq], L["r"][:, 0:nq], ALU.mult, [psk(4), "c_r"], ["c_onT"])
            for ch in range(2):
                wt, wk = wload(wview(("co", 0), w_c_o[0], ch * 512, 512), NCH, 512)
                for cc in range(4):
                    c = ch * 4 + cc
                    b = next_ps()
                    for hd in range(H_C):
                        P.mm(PS[b][:, 0:n], wt[:, hd, cc * 128:(cc + 1) * 128], onT[:, hd, 0:n], hd == 0, hd == H_C - 1,
                             [wk, "c_onT"], [psk(b)])
                    resid_add(gr, i, 1, c, PS[b], psk(b))

        def prep_layer(i):
            kind, j = i % 3, i // 3
            sp_ = []
            if kind == 0:
                sp_ += [wview(("aqkv", j), w_a_qkv[j], cb * 512, 512) for cb in range(6)]
                sp_ += [wview(("ao", j), w_a_o[j], ch * 512, 512) for ch in range(2)]
            elif kind == 1:
                sp_ += [wview(("bin", 0), w_b_in[0], blk * 512, 512) for blk in range(6)]
                sp_ += [wview(("bin", 0), w_b_in[0], 3072, 8)]
                sp_ += [wview(("bout", j), w_b_out[j], ch * 512, 512) for ch in range(2)]
            else:
                sp_ += [wview(("cdq", 0), w_c_dq[0], 0, Q_LORA), wview(("cdkv", 0), w_c_dkv[0], 0, 320)]
                sp_ += [wview(("cuq", 0), w_c_uq[0], cb * 384, 384) for cb in range(4)]
                sp_ += [wview(("cukv", 0), w_c_ukv[0], cb * 512, 512) for cb in range(4)]
                sp_ += [wview(("co", 0), w_c_o[0], ch * 512, 512) for ch in range(2)]
            for half in range(2):
                f0 = half * 11
                for (fb, nf) in [(f0, 2), (f0 + 2, 2), (f0 + 4, 2), (f0 + 6, 2), (f0 + 8, 2), (f0 + 10, 1)]:
                    sp_.append(wview(("fin", i), w_ffn_in[i], fb * 128, nf * 128))
                    sp_.append(wview(("fin", i), w_ffn_in[i], D_FF + fb * 128, nf * 128))
                for ch in range(2):
                    for (k0, nk) in ((0, 8), (8, 3)):
                        sp_.append(wview(("fout", i), w_ffn_out[i], ch * 512, 512, k0=f0 + k0, nk=nk))
            for spc in sp_:
                wprep(spc, i)

        prep_layer(cfg.layers[0])
        for li, i in enumerate(cfg.layers):
            kind, j = i % 3, i // 3
            P.barrier()
            cur_layer[0] = i
            if li + 1 < len(cfg.layers):
                prep_layer(cfg.layers[li + 1])
            with ExitStack() as ls:
                if kind == 0:
                    L = setup_diff(i, j, ls)
                    for gr in groups:
                        mixer_diff(gr, i, j, L)
                        ffn(gr, i)
                elif kind == 1:
                    L = setup_mlstm(i, j, ls)
                    for gr in groups:
                        mixer_mlstm(gr, i, j, L)
                        ffn(gr, i)
                else:
                    L = setup_mla(i, j, ls)
                    for gr in groups:
                        mixer_mla(gr, i, j, L)
                        ffn(gr, i)
                P.barrier()

        P.barrier()
        P.mark("final")
        hT_f = sb("hT_f", [128, NCH, G])
        for gr in groups:
            final_out(gr, y_p if gr.kind == "p" else y_s)
        P.finish()
        P.mark("end")
        if os.environ.get("KMARKS"):
            import json as _json
            _json.dump(P.marks, open(os.environ["KMARKS"], "w"))
        print("ops", P.nops, "waits", P.nwait, "dma sems", P.nsem, "cnt", P.cnt, flush=True)
    return nc


def rope_table(pos, rot):
    half = rot // 2
    inv = np.power(np.float32(ROPE_THETA), -np.arange(half, dtype=np.float32) * np.float32(2.0 / rot)).astype(np.float32)
    ang = pos.astype(np.float32)[:, None] * inv[None, :]
    return np.concatenate([np.cos(ang), np.sin(ang)], axis=1).astype(np.float32)


def host_consts(cfg):
    tri = (np.arange(64)[:, None] <= np.arange(64)[None, :]).astype(np.float32)
    sel = np.zeros((4, 4, 64), np.float32)
    for h in range(4):
        sel[h, h, :] = 1.0
    pos = np.concatenate([np.arange(cfg.TP), np.tile(cfg.PAST + np.arange(cfg.TS), cfg.NS)])
    return {
        "k_ident": np.eye(128, dtype=np.float32),
        "k_tri": tri,
        "k_sel": sel.reshape(4, 256),
        "rope_a": rope_table(pos, 16),
        "rope_c": rope_table(pos, 64),
    }


def make_in_maps(cfg, inp, n_cores=8):
    f = lambda a: np.ascontiguousarray(np.asarray(a, dtype=np.float32))
    NS, TS, TP, PAST = cfg.NS, cfg.TS, cfg.TP, cfg.PAST
    consts = host_consts(cfg)
    shared = {k: f(inp[k]) for k in ("w_ada", "b_ada", "g_norm1", "g_norm2", "w_a_qkv", "a_lambda", "g_a_sub", "w_a_o",
                                     "w_b_in", "b_b_gates", "g_b_out", "w_b_out", "w_c_dq", "g_c_q", "w_c_uq", "w_c_dkv",
                                     "g_c_kv", "w_c_ukv", "w_c_o", "w_ffn_in", "w_ffn_out")}
    shared["g_final"] = f(inp["g_final"]).reshape(1, D)
    shared.update(consts)
    nb = inp["x_prompt"].shape[0]
    maps = []
    for c in range(n_cores):
        b = c % nb
        ss = slice(NS * c, NS * c + NS)
        m = dict(shared)
        m["x_p"] = f(inp["x_prompt"][b])
        m["x_s"] = f(inp["x_sample"][ss]).reshape(NS * TS, D)
        m["c_in"] = f(np.concatenate([np.asarray(inp["c_prompt"])[b:b + 1], np.asarray(inp["c_sample"])[ss]], axis=0))
        m["ca_k"] = f(inp["cache_a_k"][:, ss]).reshape(2, NS, PAST, D)
        m["ca_v"] = f(inp["cache_a_v"][:, ss]).reshape(2, NS, PAST, D)
        m["sb_c"] = f(inp["state_b_c"][0, ss])
        m["sb_n"] = f(inp["state_b_n"][0, ss])
        m["sb_m"] = f(inp["state_b_m"][0, ss])
        m["cc_kv"] = f(inp["cache_c_kv"][0, ss])
        m["cc_kr"] = f(inp["cache_c_kr"][0, ss])
        maps.append(m)
    return maps


_NC_CACHE = {}


def kernel(**inputs):
    cfg = Cfg()
    if "nc" not in _NC_CACHE:
        _NC_CACHE["nc"] = build(cfg)
    nc = _NC_CACHE["nc"]
    maps = make_in_maps(cfg, inputs)
    res = run_bass_kernel_spmd(nc, maps, core_ids=list(range(8))).results
    NS, TS, TP = cfg.NS, cfg.TS, cfg.TP
    B = 4

    def pst(name, cores, shape=None):
        a = np.stack([np.asarray(res[c][name], dtype=np.float32) for c in cores], axis=0)
        return a

    pc = list(range(B))
    ac = list(range(8))
    y_prompt = pst("y_p", pc)
    y_sample = pst("y_s", ac).reshape(8 * NS, TS, D)
    a_k_p = np.moveaxis(pst("akp", pc), 0, 1).reshape(2, B, TP, H_A, 128)
    a_v_p = np.moveaxis(pst("avp", pc), 0, 1).reshape(2, B, TP, H_A, 128)
    b_c_p = pst("bcp", pc)[None]
    b_n_p = pst("bnp", pc)[None]
    b_m_p = pst("bmp", pc).reshape(1, B, H_B)
    c_kv_p = pst("ckvp", pc)[None]
    c_kr_p = pst("ckrp", pc)[None]
    a_k_s = np.moveaxis(pst("aks", ac).reshape(8, 2, NS, TS, D), 1, 0).reshape(2, 8 * NS, TS, H_A, 128)
    a_v_s = np.moveaxis(pst("avs", ac).reshape(8, 2, NS, TS, D), 1, 0).reshape(2, 8 * NS, TS, H_A, 128)
    b_c_s = pst("bcs", ac).reshape(1, 8 * NS, H_B, DQK_B, DV_B)
    b_n_s = pst("bns", ac).reshape(1, 8 * NS, H_B, DQK_B)
    b_m_s = pst("bms", ac).reshape(1, 8 * NS, H_B)
    c_kv_s = pst("ckvs", ac).reshape(1, 8 * NS, TS, KV_LORA)
    c_kr_s = pst("ckrs", ac).reshape(1, 8 * NS, TS, 64)
    outs = (y_prompt, y_sample, a_k_p, a_v_p, b_c_p, b_n_p, b_m_p, c_kv_p, c_kr_p,
            a_k_s, a_v_s, b_c_s, b_n_s, b_m_s, c_kv_s, c_kr_s)
    return tuple(np.ascontiguousarray(o, dtype=np.float32) for o in outs)
```
